# Optimizing a Trainium2 kernel written in Bass

```python
import jax, jax.numpy as jnp
from jax import lax
import numpy as np

D_MODEL = 1024
BATCH = 16
SEQ = 4096
DEPTH = 2
DEC_BATCH = 16
DEC_SEQ = 32
PAST_LEN = 2048

CHUNK = 64
Q_BLOCK = 128
HEAD_DIM = 64
N_SB_HEADS = 4
N_RET_HEADS = 4
N_FOX_HEADS = 4
N_DSA_HEADS = 4
N_IDX_HEADS = 4
IDX_DIM = 64
DSA_TOP_K = 256
N_BRANCHES = 4
BRANCH_WIDTH = 4 * HEAD_DIM
FFN_HIDDEN = -(-8 * D_MODEL // (3 * 256)) * 256
ROPE_BASE = 10000.0
LN_EPS = 1e-5
FORGET_BIAS_INIT = 2.0
ALPHA = (2 * DEPTH) ** 0.25
BETA = (8 * DEPTH) ** -0.25

IN_LAYOUT = (
    ('sb_q', N_SB_HEADS * HEAD_DIM), ('sb_k', N_SB_HEADS * HEAD_DIM), ('sb_v', N_SB_HEADS * HEAD_DIM),
    ('ret_q', N_RET_HEADS * HEAD_DIM), ('ret_k', N_RET_HEADS * HEAD_DIM), ('ret_v', N_RET_HEADS * HEAD_DIM),
    ('ret_g', N_RET_HEADS * HEAD_DIM),
    ('fox_q', N_FOX_HEADS * HEAD_DIM), ('fox_k', N_FOX_HEADS * HEAD_DIM), ('fox_v', N_FOX_HEADS * HEAD_DIM),
    ('fox_f', N_FOX_HEADS),
    ('dsa_q', N_DSA_HEADS * HEAD_DIM), ('dsa_k', HEAD_DIM), ('dsa_v', HEAD_DIM),
    ('idx_q', N_IDX_HEADS * IDX_DIM), ('idx_k', IDX_DIM), ('idx_w', N_IDX_HEADS),
    ('merge_gate', N_BRANCHES * D_MODEL),
)
IN_WIDTH = sum(w for _, w in IN_LAYOUT)
DEEPNORM_V_COLS = ('sb_v', 'ret_v', 'fox_v', 'dsa_v')

kernel_name = 'hybrid_stickbreak_retention_fox_dsa_streaming_step'


def layer_norm(x, g, b):
    xf = x.astype(jnp.float32)
    xc = xf - jnp.mean(xf, -1, keepdims=True)
    var = jnp.mean(xc * xc, -1, keepdims=True)
    return (xc * lax.rsqrt(var + LN_EPS) * g + b).astype(x.dtype)


def head_norm(o):
    oc = o - jnp.mean(o, -1, keepdims=True)
    return oc * lax.rsqrt(jnp.mean(oc * oc, -1, keepdims=True) + LN_EPS)


def split_projection(p):
    offs = np.cumsum([w for _, w in IN_LAYOUT])[:-1].tolist()
    return dict(zip([n for n, _ in IN_LAYOUT], jnp.split(p, offs, axis=-1)))


def rotary(x, pos):
    half = x.shape[-1] // 2
    inv_freq = ROPE_BASE ** (-jnp.arange(half, dtype=jnp.float32) / half)
    ang = pos.astype(jnp.float32)[:, None] * inv_freq[None, :]
    cos = jnp.cos(ang)[None, :, None, :]
    sin = jnp.sin(ang)[None, :, None, :]
    xf = x.astype(jnp.float32)
    x1, x2 = xf[..., :half], xf[..., half:]
    return jnp.concatenate([x1 * cos - x2 * sin, x2 * cos + x1 * sin], -1)


def sweep_query_blocks(fn, qs, qpos):
    t = qpos.shape[0]
    blk = min(Q_BLOCK, t)
    nb = t // blk

    def to_blocks(a):
        return jnp.moveaxis(a.reshape(a.shape[0], nb, blk, *a.shape[2:]), 1, 0)

    out = lax.map(lambda xs: fn(*xs[0], xs[1]), (tuple(to_blocks(a) for a in qs), qpos.reshape(nb, blk)))
    out = jnp.moveaxis(out, 0, 1)
    return out.reshape(out.shape[0], t, *out.shape[3:])


def stick_breaking_block(q, qpos, k, v, kpos):
    z = jnp.einsum('bqhd,blhd->bhql', q, k).astype(jnp.float32) * HEAD_DIM ** -0.5
    earlier = kpos[None, :] < qpos[:, None]
    log_1mb = jnp.where(earlier, jax.nn.log_sigmoid(-z), 0.0)
    after = lax.cumsum(log_1mb, axis=3, reverse=True) - log_1mb
    w = jnp.where(earlier, jnp.exp(jax.nn.log_sigmoid(z) + after), 0.0)
    return jnp.einsum('bhql,blhd->bqhd', w, v.astype(jnp.float32)).astype(q.dtype)


def forgetting_block(q, cq, qpos, k, v, ck, kpos):
    logits = jnp.einsum('bqhd,blhd->bhql', q, k).astype(jnp.float32) * HEAD_DIM ** -0.5
    logits = logits + jnp.moveaxis(cq, -1, 1)[..., None] - jnp.moveaxis(ck, -1, 1)[:, :, None, :]
    logits = jnp.where((kpos[None, :] <= qpos[:, None])[None, None], logits, -jnp.inf)
    p = jax.nn.softmax(logits, axis=-1)
    return jnp.einsum('bhql,blhd->bqhd', p, v.astype(jnp.float32)).astype(q.dtype)


def dsa_block(q, qi, wi, qpos, k, v, ki, kpos, top_k):
    idx_logit = jnp.einsum('bqhe,ble->bqhl', qi, ki).astype(jnp.float32) * IDX_DIM ** -0.5
    score = jnp.einsum('bqh,bqhl->bql', wi.astype(jnp.float32) * N_IDX_HEADS ** -0.5, jax.nn.relu(idx_logit))
    admissible = (kpos[None, :] // CHUNK) <= (qpos[:, None] // CHUNK)
    score = jnp.where(admissible[None], score, -jnp.inf)
    _, sel = lax.top_k(score, top_k)
    valid = (kpos[sel] // CHUNK) <= (qpos // CHUNK)[None, :, None]
    gather = jax.vmap(lambda rows, idx: rows[idx])
    k_sel = gather(k, sel)
    v_sel = gather(v, sel)
    logits = jnp.einsum('bqhd,bqkd->bhqk', q, k_sel).astype(jnp.float32) * HEAD_DIM ** -0.5
    logits = jnp.where(valid[:, None], logits, -jnp.inf)
    p = jax.nn.softmax(logits, axis=-1)
    return jnp.einsum('bhqk,bqkd->bqhd', p, v_sel.astype(jnp.float32)).astype(q.dtype)


def retention_chunk(state, q, k, v):
    c = q.shape[1]
    log_gamma = jnp.log(1.0 - 2.0 ** (-5.0 - jnp.arange(N_RET_HEADS, dtype=jnp.float32)))
    n = jnp.arange(c, dtype=jnp.float32)
    rel = n[:, None] - n[None, :]
    decay = jnp.where(rel >= 0, jnp.exp(jnp.maximum(rel, 0.0)[None] * log_gamma[:, None, None]), 0.0)
    scores = jnp.einsum('bnhd,bmhd->bhnm', q, k) * decay[None]
    o = jnp.einsum('bhnm,bmhe->bnhe', scores, v)
    o = o + jnp.einsum('bnhd,bhde->bnhe', q, state) * jnp.exp((n[:, None] + 1.0) * log_gamma[None, :])[None, :, :, None]
    k_dec = k * jnp.exp((c - 1.0 - n)[:, None] * log_gamma[None, :])[None, :, :, None]
    state = jnp.exp(c * log_gamma)[None, :, None, None] * state + jnp.einsum('bmhd,bmhe->bhde', k_dec, v)
    return state, o


def retention(q, k, v, state):
    b, t = q.shape[:2]
    c = min(CHUNK, t)
    nc = t // c

    def to_chunks(a):
        return jnp.moveaxis(a.reshape(b, nc, c, *a.shape[2:]), 1, 0)

    state, o = lax.scan(lambda s, xs: retention_chunk(s, *xs), state, (to_chunks(q), to_chunks(k), to_chunks(v)))
    return jnp.moveaxis(o, 0, 1).reshape(b, t, *o.shape[3:]), state


def trunk_layer(h, past, ret_state, w_in, b_forget, w_branch, w_out, ln1_g, ln1_b,
                w_ffn_in, w_ffn_out, ln2_g, ln2_b):
    sb_k0, sb_v0, fox_k0, fox_v0, fox_lf0, dsa_k0, dsa_v0, dsa_ki0 = past
    b, t, _ = h.shape
    p_len = sb_k0.shape[1]
    qpos = p_len + jnp.arange(t, dtype=jnp.int32)
    kpos = jnp.arange(p_len + t, dtype=jnp.int32)
    pr = split_projection(h @ w_in)

    def heads(name, n):
        return pr[name].reshape(b, t, n, -1)

    sb_q, sb_k, sb_v = heads('sb_q', N_SB_HEADS), heads('sb_k', N_SB_HEADS), heads('sb_v', N_SB_HEADS)
    sb_k_all = jnp.concatenate([sb_k0, sb_k], 1)
    sb_v_all = jnp.concatenate([sb_v0, sb_v], 1)
    y_sb = sweep_query_blocks(lambda qb, pb: stick_breaking_block(qb, pb, sb_k_all, sb_v_all, kpos), (sb_q,), qpos)

    ret_q = rotary(heads('ret_q', N_RET_HEADS), qpos)
    ret_k = rotary(heads('ret_k', N_RET_HEADS), qpos) * HEAD_DIM ** -0.5
    ret_v = heads('ret_v', N_RET_HEADS).astype(jnp.float32)
    y_ret, ret_state_new = retention(ret_q, ret_k, ret_v, ret_state.astype(jnp.float32))
    y_ret = (head_norm(y_ret).reshape(b, t, -1) * jax.nn.silu(pr['ret_g'].astype(jnp.float32))).astype(h.dtype)

    fox_q, fox_k, fox_v = heads('fox_q', N_FOX_HEADS), heads('fox_k', N_FOX_HEADS), heads('fox_v', N_FOX_HEADS)
    fox_lf = jax.nn.log_sigmoid((pr['fox_f'] + b_forget).astype(jnp.float32))
    fox_k_all = jnp.concatenate([fox_k0, fox_k], 1)
    fox_v_all = jnp.concatenate([fox_v0, fox_v], 1)
    cum = jnp.cumsum(jnp.concatenate([fox_lf0.astype(jnp.float32), fox_lf], 1), axis=1)
    y_fox = sweep_query_blocks(
        lambda qb, cqb, pb: forgetting_block(qb, cqb, pb, fox_k_all, fox_v_all, cum, kpos),
        (fox_q, cum[:, p_len:]), qpos)

    dsa_q, dsa_k, dsa_v = heads('dsa_q', N_DSA_HEADS), pr['dsa_k'], pr['dsa_v']
    idx_q, idx_k, idx_w = heads('idx_q', N_IDX_HEADS), pr['idx_k'], pr['idx_w']
    dsa_k_all = jnp.concatenate([dsa_k0, dsa_k], 1)
    dsa_v_all = jnp.concatenate([dsa_v0, dsa_v], 1)
    dsa_ki_all = jnp.concatenate([dsa_ki0, idx_k], 1)
    top_k = min(DSA_TOP_K, (p_len + t) // 4)
    y_dsa = sweep_query_blocks(
        lambda qb, qib, wib, pb: dsa_block(qb, qib, wib, pb, dsa_k_all, dsa_v_all, dsa_ki_all, kpos, top_k),
        (dsa_q, idx_q, idx_w), qpos)

    gates = jax.nn.sigmoid(pr['merge_gate'].reshape(b, t, N_BRANCHES, D_MODEL))
    branches = (y_sb.reshape(b, t, -1), y_ret, y_fox.reshape(b, t, -1), y_dsa.reshape(b, t, -1))
    merged = gates[:, :, 0] * (branches[0] @ w_branch[0])
    for i in range(1, N_BRANCHES):
        merged = merged + gates[:, :, i] * (branches[i] @ w_branch[i])
    h = layer_norm(ALPHA * h + merged @ w_out, ln1_g, ln1_b)

    a, u = jnp.split(h @ w_ffn_in, 2, axis=-1)
    h = layer_norm(ALPHA * h + (jax.nn.silu(a) * u) @ w_ffn_out, ln2_g, ln2_b)
    new_state = (sb_k, sb_v, ret_state_new, fox_k, fox_v, fox_lf, dsa_k, dsa_v, idx_k)
    return h, new_state


def setup_inputs(seed: int = 0) -> dict:
    key = jax.random.key(seed)
    ks = jax.random.split(key, 24)

    def nrm(k, shape, scale=1.0):
        return scale * jax.random.normal(k, shape, jnp.float32)

    col_scale = jnp.asarray(np.concatenate(
        [np.full((w,), BETA if n in DEEPNORM_V_COLS else 1.0, np.float32) for n, w in IN_LAYOUT]))
    kv_shape = (DEPTH, DEC_BATCH, PAST_LEN, N_SB_HEADS, HEAD_DIM)
    fox_shape = (DEPTH, DEC_BATCH, PAST_LEN, N_FOX_HEADS, HEAD_DIM)
    return {
        'x_prompt': nrm(ks[0], (BATCH, SEQ, D_MODEL)),
        'x_sample': nrm(ks[1], (DEC_BATCH, DEC_SEQ, D_MODEL)),
        'cache_sb_k': nrm(ks[2], kv_shape),
        'cache_sb_v': nrm(ks[3], kv_shape, BETA),
        'state_ret': nrm(ks[4], (DEPTH, DEC_BATCH, N_RET_HEADS, HEAD_DIM, HEAD_DIM), 0.25),
        'cache_fox_k': nrm(ks[5], fox_shape),
        'cache_fox_v': nrm(ks[6], fox_shape, BETA),
        'cache_fox_logf': jax.nn.log_sigmoid(FORGET_BIAS_INIT + nrm(ks[7], (DEPTH, DEC_BATCH, PAST_LEN, N_FOX_HEADS))),
        'cache_dsa_k': nrm(ks[8], (DEPTH, DEC_BATCH, PAST_LEN, HEAD_DIM)),
        'cache_dsa_v': nrm(ks[9], (DEPTH, DEC_BATCH, PAST_LEN, HEAD_DIM), BETA),
        'cache_dsa_kidx': nrm(ks[10], (DEPTH, DEC_BATCH, PAST_LEN, IDX_DIM)),
        'w_in': nrm(ks[11], (DEPTH, D_MODEL, IN_WIDTH), D_MODEL ** -0.5) * col_scale,
        'b_forget': FORGET_BIAS_INIT + nrm(ks[12], (DEPTH, N_FOX_HEADS), 0.1),
        'w_branch': nrm(ks[13], (DEPTH, N_BRANCHES, BRANCH_WIDTH, D_MODEL), BETA * BRANCH_WIDTH ** -0.5),
        'w_out': nrm(ks[14], (DEPTH, D_MODEL, D_MODEL), BETA * D_MODEL ** -0.5),
        'ln1_g': 1.0 + nrm(ks[15], (DEPTH, D_MODEL), 0.02),
        'ln1_b': nrm(ks[16], (DEPTH, D_MODEL), 0.02),
        'w_ffn_in': nrm(ks[17], (DEPTH, D_MODEL, 2 * FFN_HIDDEN), D_MODEL ** -0.5),
        'w_ffn_out': nrm(ks[18], (DEPTH, FFN_HIDDEN, D_MODEL), BETA * FFN_HIDDEN ** -0.5),
        'ln2_g': 1.0 + nrm(ks[19], (DEPTH, D_MODEL), 0.02),
        'ln2_b': nrm(ks[20], (DEPTH, D_MODEL), 0.02),
    }


def reference(x_prompt, x_sample, cache_sb_k, cache_sb_v, state_ret, cache_fox_k, cache_fox_v,
              cache_fox_logf, cache_dsa_k, cache_dsa_v, cache_dsa_kidx, w_in, b_forget, w_branch,
              w_out, ln1_g, ln1_b, w_ffn_in, w_ffn_out, ln2_g, ln2_b):
    b = x_prompt.shape[0]
    dt = x_prompt.dtype
    empty_past = (
        jnp.zeros((b, 0, N_SB_HEADS, HEAD_DIM), dt), jnp.zeros((b, 0, N_SB_HEADS, HEAD_DIM), dt),
        jnp.zeros((b, 0, N_FOX_HEADS, HEAD_DIM), dt), jnp.zeros((b, 0, N_FOX_HEADS, HEAD_DIM), dt),
        jnp.zeros((b, 0, N_FOX_HEADS), jnp.float32),
        jnp.zeros((b, 0, HEAD_DIM), dt), jnp.zeros((b, 0, HEAD_DIM), dt), jnp.zeros((b, 0, IDX_DIM), dt),
    )
    ret_zero = jnp.zeros((b, N_RET_HEADS, HEAD_DIM, HEAD_DIM), jnp.float32)
    hp, hs = x_prompt, x_sample
    st_p, st_s = [], []
    for l in range(DEPTH):
        wl = (w_in[l], b_forget[l], w_branch[l], w_out[l], ln1_g[l], ln1_b[l],
              w_ffn_in[l], w_ffn_out[l], ln2_g[l], ln2_b[l])
        hp, sp = trunk_layer(hp, empty_past, ret_zero, *wl)
        past_l = (cache_sb_k[l], cache_sb_v[l], cache_fox_k[l], cache_fox_v[l], cache_fox_logf[l],
                  cache_dsa_k[l], cache_dsa_v[l], cache_dsa_kidx[l])
        hs, ss = trunk_layer(hs, past_l, state_ret[l], *wl)
        st_p.append(sp)
        st_s.append(ss)

    def stacked(states, i):
        return jnp.stack([s[i] for s in states])

    return (hp, hs,
            stacked(st_p, 0), stacked(st_p, 1), stacked(st_p, 2), stacked(st_p, 3), stacked(st_p, 4),
            stacked(st_p, 5), stacked(st_p, 6), stacked(st_p, 7), stacked(st_p, 8),
            stacked(st_s, 0), stacked(st_s, 1), stacked(st_s, 2), stacked(st_s, 3), stacked(st_s, 4),
            stacked(st_s, 5), stacked(st_s, 6), stacked(st_s, 7), stacked(st_s, 8))
```

```python
import math
import numpy as np
import ml_dtypes
import concourse.bass as bass
import concourse.mybir as mybir
from concourse.bass_utils import run_bass_kernel_spmd

F32 = mybir.dt.float32
BF16 = mybir.dt.bfloat16
AF = mybir.ActivationFunctionType
ALU = mybir.AluOpType

DM = 1024
KC = 8
FFN = 2816
NFC = 22
ALPHA = 4 ** 0.25
EPS = 1e-5
NEG = -1.0e30
GAMMA = [1.0 - 2.0 ** (-5.0 - h) for h in range(4)]

OFF = {}
_o = 0
for _n, _w in (('sb_q', 256), ('sb_k', 256), ('sb_v', 256), ('ret_q', 256), ('ret_k', 256), ('ret_v', 256),
               ('ret_g', 256), ('fox_q', 256), ('fox_k', 256), ('fox_v', 256), ('fox_f', 4), ('dsa_q', 256),
               ('dsa_k', 64), ('dsa_v', 64), ('idx_q', 256), ('idx_k', 64), ('idx_w', 4), ('merge_gate', 4096)):
    OFF[_n] = (_o, _w)
    _o += _w
IN_WIDTH = _o

TMG = [(0, 512), (512, 512), (1024, 200), (1224, 768)]
NTM = 1992
CH_SB = 0
CH_FOX = 4
CH_RET = 8
CH_DSA = 18
CH_GATE = 23
CH_OUT = 55
CH_FA = 63
CH_FU = 85
NCH = 107


def _cols(name):
    o, w = OFF[name]
    return np.arange(o, o + w)


def _swap_cols(name):
    o, w = OFF[name]
    idx = np.arange(w).reshape(4, 2, 32)[:, ::-1, :].reshape(-1)
    return o + idx


def prep_weights(w_in, w_branch, w_out, w_ffn_in, w_ffn_out, ln1_g, ln1_b, ln2_g, ln2_b, b_forget):
    L = w_in.shape[0]
    tm_cols = np.concatenate([_cols('sb_k'), _cols('sb_v'), _cols('fox_k'), _cols('fox_v'), _cols('dsa_k'),
                              _cols('dsa_v'), _cols('idx_k'), _cols('fox_f'), _cols('idx_w'), _cols('ret_v'),
                              _cols('ret_k'), _swap_cols('ret_k')])
    assert tm_cols.size == NTM
    wtm = w_in[:, :, tm_cols].reshape(L, KC, 128, NTM).transpose(0, 2, 1, 3)
    chunks = []

    def add(cols):
        assert cols.size == 128
        chunks.append(cols)
    sq, sk = _cols('sb_q'), _cols('sb_k')
    add(sq[:128]); add(sq[128:]); add(sk[:128]); add(sk[128:])
    fq, fk = _cols('fox_q'), _cols('fox_k')
    add(fq[:128]); add(fq[128:]); add(fk[:128]); add(fk[128:])
    for nm in ('ret_q', 'ret_k'):
        a, b = _cols(nm), _swap_cols(nm)
        add(a[:128]); add(a[128:]); add(b[:128]); add(b[128:])
    g = _cols('ret_g')
    add(g[:128]); add(g[128:])
    dq, iq = _cols('dsa_q'), _cols('idx_q')
    for h in range(4):
        add(np.concatenate([dq[h * 64:(h + 1) * 64], iq[h * 64:(h + 1) * 64]]))
    add(np.concatenate([_cols('dsa_k'), _cols('idx_k')]))
    mg = _cols('merge_gate')
    for i in range(32):
        add(mg[i * 128:(i + 1) * 128])
    assert len(chunks) == CH_OUT
    win_ch = np.stack([w_in[:, :, c] for c in chunks], axis=1)
    wo_ch = w_out.reshape(L, DM, 8, 128).transpose(0, 2, 1, 3)
    wf_ch = w_ffn_in.reshape(L, DM, 44, 128).transpose(0, 2, 1, 3)
    allch = np.concatenate([win_ch, wo_ch, wf_ch], axis=1)
    assert allch.shape[1] == NCH
    wf = allch.reshape(L, NCH, KC, 128, 128).transpose(0, 1, 3, 2, 4)
    w2 = w_ffn_out.reshape(L, NFC, 128, 8, 128).transpose(0, 3, 2, 1, 4)
    wb = w_branch.reshape(L, 4, 4, 64, 8, 128).transpose(0, 4, 3, 1, 2, 5).reshape(L, 8, 64, 16, 128)
    lnp = np.stack([ln1_g, ln1_b, ln2_g, ln2_b], axis=1).reshape(L, 4, KC, 128).transpose(3, 0, 1, 2)
    bfor = np.broadcast_to(b_forget[None], (128, L, 4))
    f = lambda a: np.ascontiguousarray(a, dtype=np.float32)
    return dict(wtm=f(wtm), wf=f(wf), w2=f(w2), wb=f(wb), lnp=f(lnp), bfor=f(bfor))


def rot_tables(P, T):
    half = 32
    inv_freq = (10000.0 ** (-np.arange(half, dtype=np.float32) / half)).astype(np.float32)
    pos = (P + np.arange(T)).astype(np.float32)
    ang = pos[:, None] * inv_freq[None, :]
    cos = np.cos(ang).astype(np.float64)
    sin = np.sin(ang).astype(np.float64)
    cosf = np.concatenate([cos, cos], 1)
    sinf = np.concatenate([-sin, sin], 1)
    QB = min(512, T)
    n = np.arange(T)
    fm = np.zeros((4, 256, T), np.float64)
    tm = np.zeros((2, T, 256), np.float64)
    for h in range(4):
        g = GAMMA[h]
        dq = g ** (n % QB).astype(np.float64)
        dk = g ** (127 - (n % 128)).astype(np.float64) * 0.125
        ds = g ** (T - 1 - n).astype(np.float64) * 0.125
        sl = slice(h * 64, (h + 1) * 64)
        fm[0, sl] = (cosf * dq[:, None]).T
        fm[1, sl] = (sinf * dq[:, None]).T
        fm[2, sl] = (cosf * dk[:, None]).T
        fm[3, sl] = (sinf * dk[:, None]).T
        tm[0, :, sl] = cosf * ds[:, None]
        tm[1, :, sl] = sinf * ds[:, None]
    fm = fm.reshape(4, 2, 128, T).transpose(0, 2, 1, 3)
    return np.ascontiguousarray(fm, np.float32), np.ascontiguousarray(tm, np.float32)


def const_tables():
    r = np.arange(128)[:, None]
    c = np.arange(512)[None, :]
    x = np.arange(896)[None, :]
    masks = np.zeros((128, 2, 896), np.float32)
    masks[:, 0, :] = (r <= x - 384)
    masks[:, 1, :] = (r < x - 384)
    j = np.arange(128)[:, None]
    s = np.arange(128)[None, :]
    mats = np.zeros((128, 9, 128), np.float32)
    mats[:, 0, :] = np.eye(128)
    mats[:, 1, :] = -1.0 * (j >= s)
    mats[:, 2, :] = -1.0 * (j < s)
    mats[:, 3, :] = (j <= s)
    mats[:, 4, :] = 1.0
    mats[:, 5, :] = (j == 127)
    mats[:, 6, :] = (j == 0)
    mats[:, 7, :] = (j == 16)
    mats[:, 8, :] = (0.5 ** (np.arange(128) + 1.0))[None, :]
    return masks.astype(ml_dtypes.bfloat16), mats


class Prog:
    CE = ('pe', 'act', 'dve', 'pool')

    def __init__(self, ndma=16, nsp=10):
        self.ops = {e: [] for e in self.CE + ('sp',)}
        self.seq = {e: 0 for e in self.CE}
        self.ndma = ndma
        self.dcnt = [0] * ndma
        self.dnext = {'sp': 0, 'pool': 0}
        self.nsp = nsp
        self.lastw = {}
        self.rd = {}
        self.known = {e: {} for e in self.CE + ('sp',)}
        self.snap = {}
        self.nops = 0

    def _collect(self, eng, reads, writes, acc):
        need = {}

        def add(ev, is_w=False):
            if ev is None:
                return
            c, v = ev
            if acc and is_w and c == eng:
                return
            if need.get(c, 0) < v:
                need[c] = v
        for r in reads:
            add(self.lastw.get(r))
        for w in writes:
            add(self.lastw.get(w), True)
            for c, v in self.rd.get(w, {}).items():
                add((c, v))
        kn = self.known[eng]
        waits = []
        for c, v in need.items():
            if kn.get(c, 0) >= v:
                continue
            waits.append((c, v))
        for c, v in waits:
            sn = self.snap.get((c, v))
            if sn:
                for c2, v2 in sn.items():
                    if kn.get(c2, 0) < v2:
                        kn[c2] = v2
            if kn.get(c, 0) < v:
                kn[c] = v
        return waits

    def _mark(self, ev, reads, writes):
        for w in writes:
            self.lastw[w] = ev
            self.rd[w] = {}
        for r in reads:
            d = self.rd.setdefault(r, {})
            if d.get(ev[0], 0) < ev[1]:
                d[ev[0]] = ev[1]

    def op(self, eng, fn, r=(), w=(), acc=False):
        isps = lambda x: isinstance(x, tuple) and x[0] in ('ps', 'psT')
        w = list(w) + [x for x in r if isps(x)]
        r = [x for x in r if not isps(x)]
        waits = self._collect(eng, r, w, acc)
        self.seq[eng] += 1
        ev = (eng, self.seq[eng])
        self.snap[ev] = dict(self.known[eng])
        self.ops[eng].append((waits, fn, eng, 1))
        self._mark(ev, r, w)
        self.nops += 1

    def dma(self, q, fn, r=(), w=()):
        if q == 'sp':
            slot = self.dnext['sp']
            self.dnext['sp'] = (slot + 1) % self.nsp
        else:
            slot = self.nsp + self.dnext['pool']
            self.dnext['pool'] = (self.dnext['pool'] + 1) % (self.ndma - self.nsp)
        clk = 'd%d' % slot
        waits = self._collect(q, r, w, False)
        if self.dcnt[slot] > 0 and self.known[q].get(clk, 0) < self.dcnt[slot]:
            waits.append((clk, self.dcnt[slot]))
            self.known[q][clk] = self.dcnt[slot]
        self.dcnt[slot] += 1
        ev = (clk, self.dcnt[slot])
        self.snap[ev] = dict(self.known[q])
        self.ops[q].append((waits, fn, clk, 16))
        self._mark(ev, r, w)
        self.nops += 1

    def barrier(self):
        allev = [(e, self.seq[e]) for e in self.CE if self.seq[e] > 0]
        allev += [('d%d' % s, self.dcnt[s]) for s in range(self.ndma) if self.dcnt[s] > 0]
        for e in self.CE + ('sp',):
            waits = [(c, v) for c, v in allev if c != e and self.known[e].get(c, 0) < v]
            for c, v in waits:
                self.known[e][c] = v
            if waits:
                self.ops[e].append((waits, None, None, 0))
        self.lastw = {}
        self.rd = {}

    def emit(self, nc, sems, block):
        engs = {'pe': (block.tensor, nc.tensor), 'act': (block.scalar, nc.scalar), 'dve': (block.vector, nc.vector),
                'pool': (block.gpsimd, nc.gpsimd), 'sp': (block.sync, nc.sync)}
        for e, (dec, engobj) in engs.items():
            ops = self.ops[e]

            def body(_eng, ops=ops, engobj=engobj):
                for waits, fn, clk, inc in ops:
                    for c, v in waits:
                        engobj.wait_ge(sems[c], v * (16 if c[0] == 'd' and c[1:].isdigit() else 1))
                    if fn is not None:
                        fn().then_inc(sems[clk], inc)
            dec(body)


def build_program(cfg, debug=False):
    BP, BS, TP, TS, PAST = cfg['BP'], cfg['BS'], cfg['TP'], cfg['TS'], cfg['PAST']
    nc = bass.Bass("TRN2", target_bir_lowering=False)
    P = Prog()

    def din(name, shape, dt=F32):
        return nc.dram_tensor(name, list(shape), dt, kind="ExternalInput").ap()

    def dout(name, shape, dt=F32):
        return nc.dram_tensor(name, list(shape), dt, kind="ExternalOutput").ap()

    def dscr(name, shape, dt, dbg=False):
        return nc.dram_tensor(name, list(shape), dt, kind=("ExternalOutput" if (dbg and debug) else "Internal")).ap()

    x_in = {'p': din('xp', [BP, TP, DM]), 's': din('xs', [BS, TS, DM])}
    c_sbk = din('c_sbk', [2, BS, PAST, 256]); c_sbv = din('c_sbv', [2, BS, PAST, 256])
    c_fk = din('c_fk', [2, BS, PAST, 256]); c_fv = din('c_fv', [2, BS, PAST, 256])
    c_flf = din('c_flf', [2, BS, PAST, 4])
    c_dk = din('c_dk', [2, BS, PAST, 64]); c_dv = din('c_dv', [2, BS, PAST, 64]); c_dki = din('c_dki', [2, BS, PAST, 64])
    st_ret = din('st_ret', [2, BS, 4, 64, 64])
    wtm_f = din('wtm', [2, 128, KC, NTM]); wf_f = din('wf', [2, NCH, 128, KC, 128])
    w2_f = din('w2', [2, 8, 128, NFC, 128]); wb_f = din('wb', [2, 8, 64, 16, 128])
    lnp_d = din('lnp', [128, 2, 4, 8]); bfor_d = din('bfor', [128, 2, 4])
    masks_d = din('masks', [128, 2, 896], BF16); mats_d = din('mats', [128, 9, 128])
    rotfm = {'p': din('rotfm_p', [4, 128, 2, TP]), 's': din('rotfm_s', [4, 128, 2, TS])}
    rottm = {'p': din('rottm_p', [2, TP, 256]), 's': din('rottm_s', [2, TS, 256])}

    outs = {}
    for g, B, T in (('p', BP, TP), ('s', BS, TS)):
        outs[g] = dict(
            y=dout('y_' + g, [B, T, DM]),
            sb_k=dout('sb_k_' + g, [2, B, T, 256]), sb_v=dout('sb_v_' + g, [2, B, T, 256]),
            ret=dout('ret_' + g, [2, B, 4, 64, 64]),
            fox_k=dout('fox_k_' + g, [2, B, T, 256]), fox_v=dout('fox_v_' + g, [2, B, T, 256]),
            fox_lf=dout('fox_lf_' + g, [2, B, T, 4]),
            dsa_k=dout('dsa_k_' + g, [2, B, T, 64]), dsa_v=dout('dsa_v_' + g, [2, B, T, 64]),
            dsa_ki=dout('dsa_ki_' + g, [2, B, T, 64]))
    TMAX = max(TP, TS)
    wtm_b = dscr('wtm_b', [2, 128, KC, NTM], BF16); wf_b = dscr('wf_b', [2, NCH, 128, KC, 128], BF16)
    w2_b = dscr('w2_b', [2, 8, 128, NFC, 128], BF16); wb_b = dscr('wb_b', [2, 8, 64, 16, 128], BF16)
    retv_s = dscr('retv_s', [TMAX, 256], F32)
    ysc = dscr('ysc', [4, 64, 4, TMAX], BF16)
    msc = dscr('msc', [DM, TMAX], BF16)
    h1sc = dscr('h1sc', [DM, TMAX], F32)
    gsc = dscr('gsc', [NFC * 128, TMAX], BF16)
    ydbg = dscr('ydbg', [2, 2, 4, 64, 4, TMAX], BF16, dbg=True) if debug else None

    NTMAX = (max(TP, PAST + TS) + 127) // 128
    A16 = 40960
    A32 = 7424
    import contextlib
    es = contextlib.ExitStack()
    with es:
        hT = es.enter_context(nc.sbuf_tensor("hT", [128, KC, TMAX], BF16))
        a16 = es.enter_context(nc.sbuf_tensor("a16", [128, A16], BF16))
        a32 = es.enter_context(nc.sbuf_tensor("a32", [128, A32], F32))
        masks = es.enter_context(nc.sbuf_tensor("masks_sb", [128, 2, 896], BF16))
        mats = es.enter_context(nc.sbuf_tensor("mats_sb", [128, 9, 128], F32))
        matb = es.enter_context(nc.sbuf_tensor("matb", [128, 4, 128], BF16))
        lnp = es.enter_context(nc.sbuf_tensor("lnp_sb", [128, 2, 4, 8], F32))
        bfor = es.enter_context(nc.sbuf_tensor("bfor_sb", [128, 2, 4], F32))
        LF = es.enter_context(nc.sbuf_tensor("LF", [128, NTMAX, 4], F32))
        CS = es.enter_context(nc.sbuf_tensor("CS", [128, NTMAX, 4], F32))
        IW = es.enter_context(nc.sbuf_tensor("IW", [128, 32, 4], F32))
        IWs = es.enter_context(nc.sbuf_tensor("IWs", [128, 32, 4], F32))
        ps = es.enter_context(nc.psum_tensor("ps", [128, 8, 512], F32))
        sems = {}
        for e in Prog.CE:
            sems[e] = es.enter_context(nc.semaphore("s_" + e))
        for s in range(P.ndma):
            sems['d%d' % s] = es.enter_context(nc.semaphore("s_d%d" % s))
        block = es.enter_context(nc.Block())

        ident_f = mats[:, 0, :]
        tri = mats[:, 3, :]
        ones_f = mats[:, 4, :]
        ident_b = matb[:, 0, :]
        negU = matb[:, 1, :]
        negL = matb[:, 2, :]

        class Arena:
            def __init__(self, t, size):
                self.t, self.size, self.o = t, size, 0

            def reset(self):
                self.o = 0

            def get(self, *shape):
                n = int(np.prod(shape))
                n = (n + 15) // 16 * 16
                assert self.o + n <= self.size, ("arena overflow", self.o, n, self.size)
                v = self.t[:, self.o:self.o + int(np.prod(shape))]
                self.o += n
                if len(shape) == 2:
                    return v.rearrange("p (a b) -> p a b", a=shape[0])
                if len(shape) == 3:
                    return v.rearrange("p (a b c) -> p a b c", a=shape[0], b=shape[1])
                return v
        ar16 = Arena(a16, A16)
        ar32 = Arena(a32, A32)

        rot = {'n': 0}

        def rr(lst, key):
            i = rot.get(key, 0)
            rot[key] = i + 1
            return lst[i % len(lst)]

        def evac_eng(key='ev'):
            return rr(['act', 'dve'], key)

        def copy_op(eng, out, in_, r, w, scale=None):
            if eng == 'act':
                if scale is None:
                    P.op('act', lambda: nc.scalar.activation(out=out, in_=in_, func=AF.Copy), r=r, w=w)
                else:
                    P.op('act', lambda: nc.scalar.activation(out=out, in_=in_, func=AF.Copy, scale=float(scale)), r=r, w=w)
            elif eng == 'dve':
                if scale is None:
                    P.op('dve', lambda: nc.vector.tensor_copy(out=out, in_=in_), r=r, w=w)
                else:
                    P.op('dve', lambda: nc.vector.tensor_scalar(out=out, in0=in_, scalar1=float(scale), scalar2=None,
                                                               op0=ALU.mult), r=r, w=w)
            else:
                if scale is None:
                    P.op('pool', lambda: nc.gpsimd.tensor_copy(out=out, in_=in_), r=r, w=w)
                else:
                    P.op('pool', lambda: nc.gpsimd.tensor_scalar(out=out, in0=in_, scalar1=float(scale), scalar2=None,
                                                                op0=ALU.mult), r=r, w=w)

        def mm(out, lhsT, rhs, start, stop, r, w, acc=False):
            P.op('pe', lambda: nc.tensor.matmul(out, lhsT=lhsT, rhs=rhs, start=start, stop=stop, skip_group_check=True), r=r, w=w, acc=acc)

        P.dma('sp', lambda: nc.sync.dma_start(out=masks[:], in_=masks_d[:, :, :]), w=['masks'])
        P.dma('sp', lambda: nc.sync.dma_start(out=mats[:], in_=mats_d[:, :, :]), w=['mats'])
        P.dma('sp', lambda: nc.sync.dma_start(out=lnp[:], in_=lnp_d[:, :, :, :]), w=['lnp'])
        P.dma('sp', lambda: nc.sync.dma_start(out=bfor[:], in_=bfor_d[:, :, :]), w=['bfor'])
        P.op('dve', lambda: nc.vector.tensor_copy(out=matb[:, 0:3, :], in_=mats[:, 0:3, :]), r=['mats'], w=['matb'])
        P.op('dve', lambda: nc.vector.tensor_copy(out=matb[:, 3, :], in_=mats[:, 4, :]), r=['mats', 'matb'], w=['matb'])
        for l in range(2):
            P.dma('pool', lambda l=l: nc.gpsimd.dma_start(out=wtm_b[l], in_=wtm_f[l]), w=[('wtm', l)])
            for c0 in range(0, NCH, 8):
                c1 = min(NCH, c0 + 8)
                P.dma('pool', lambda l=l, c0=c0, c1=c1: nc.gpsimd.dma_start(
                    out=wf_b[l, c0:c1].rearrange("c p k f -> (c p) (k f)"),
                    in_=wf_f[l, c0:c1].rearrange("c p k f -> (c p) (k f)")), w=[('wf', l, c) for c in range(c0, c1)])
            for oc in range(0, 8, 2):
                P.dma('pool', lambda l=l, oc=oc: nc.gpsimd.dma_start(
                    out=w2_b[l, oc:oc + 2].rearrange("c p k f -> (c p) (k f)"),
                    in_=w2_f[l, oc:oc + 2].rearrange("c p k f -> (c p) (k f)")), w=[('w2', l, oc), ('w2', l, oc + 1)])
            P.dma('pool', lambda l=l: nc.gpsimd.dma_start(
                out=wb_b[l].rearrange("c p k f -> (c p) (k f)"),
                in_=wb_f[l].rearrange("c p k f -> (c p) (k f)")), w=[('wb', l, c) for c in range(8)])
        P.barrier()

        def process_seq(g, b):
            T = TP if g == 'p' else TS
            PL = 0 if g == 'p' else PAST
            Ltot = PL + T
            NT = (Ltot + 127) // 128
            NTn = (T + 127) // 128
            JP = PL // 128
            QB = min(512, T)
            NQB = T // QB
            nq = min(128, T)
            NQT = T // nq
            topk = min(256, Ltot // 4)
            assert topk % 8 == 0 and T % 32 == 0 and PL % 128 == 0
            O = outs[g]
            rows_of = lambda j: min(128, T - 128 * j)

            ar16.reset(); ar32.reset()
            xst = [ar32.get(1024) for _ in range(2)]
            for j in range(NTn):
                rows = rows_of(j)
                xs_ = rr(xst, 'xst')
                xk = ('xst', id(xs_))
                P.dma('sp', lambda xs_=xs_, j=j, rows=rows: nc.sync.dma_start(
                    out=xs_[:rows, :], in_=x_in[g][b, j * 128:j * 128 + rows, :]), w=[xk])
                for half in range(2):
                    bank = rr([0, 1], 'tb')
                    for q in range(4):
                        kc = half * 4 + q
                        P.op('pe', lambda xs_=xs_, kc=kc, rows=rows, bank=bank, q=q: nc.tensor.transpose(
                            out=ps[:, bank, q * 128:q * 128 + rows], in_=xs_[:rows, kc * 128:(kc + 1) * 128],
                            identity=ident_f[:rows, :rows]), r=[xk, 'mats'], w=[('ps', bank)], acc=True)
                    src = ps[:, bank, :].rearrange("p (q r) -> p q r", q=4)[:, :, 0:rows]
                    dst = hT[:, half * 4:half * 4 + 4, j * 128:j * 128 + rows]
                    copy_op(evac_eng(), dst, src, r=[('ps', bank)], w=[('hT', j)])
            P.barrier()

            for l in range(2):
                layer(g, b, l, T, PL, Ltot, NT, NTn, JP, QB, NQB, nq, NQT, topk, O, rows_of)

        def layer(g, b, l, T, PL, Ltot, NT, NTn, JP, QB, NQB, nq, NQT, topk, O, rows_of):
            hT_all = [('hT', j) for j in range(NTn)]
            ar16.reset(); ar32.reset()
            wtm = [ar16.get(KC, 512) for _ in range(2)]
            wret = ar16.get(KC, 768)
            stg = [ar32.get(512) for _ in range(3)]
            rtm = [ar32.get(2, 256) for _ in range(2)]
            t1 = ar32.get(256); t2 = ar32.get(256)
            small = ar32.get(64)
            vb = [ar16.get(256) for _ in range(2)]
            kd = [ar16.get(256) for _ in range(2)]
            s0t = ar32.get(4, 64)
            stout = ar32.get(256)
            P.op('pool', lambda: nc.gpsimd.memset(LF[:], 0.0), w=['LF'])
            if PL > 0:
                P.dma('sp', lambda: nc.sync.dma_start(out=LF[:, 0:JP, :],
                                                      in_=c_flf[l, b].rearrange("(j p) h -> p j h", p=128)), w=['LF'])
            for gi in range(3):
                c0, gw = TMG[gi]
                wt = rr(wtm, 'wtm')
                wk = ('wtmb', id(wt))
                P.dma('sp', lambda wt=wt, c0=c0, gw=gw: nc.sync.dma_start(out=wt[:, :, 0:gw], in_=wtm_b[l, :, :, c0:c0 + gw]),
                      r=[('wtm', l)], w=[wk])
                for j in range(NTn):
                    rows = rows_of(j)
                    bank = rr([0, 1], 'p1b')
                    for kc in range(KC):
                        mm(ps[:rows, bank, 0:gw], hT[:, kc, j * 128:j * 128 + rows], wt[:, kc, 0:gw], kc == 0, kc == KC - 1,
                           r=[('hT', j), wk], w=[('ps', bank)], acc=True)
                    st_ = rr(stg, 'stg')
                    sk_ = ('stg', id(st_))
                    copy_op(evac_eng(), st_[:rows, 0:gw], ps[:rows, bank, 0:gw], r=[('ps', bank)], w=[sk_])
                    tok = slice(j * 128, j * 128 + rows)
                    if gi == 0:
                        P.dma('sp', lambda st_=st_, tok=tok, rows=rows: nc.sync.dma_start(out=O['sb_k'][l, b, tok, :], in_=st_[:rows, 0:256]),
                              r=[sk_], w=[('o_sbk', j)])
                        P.dma('sp', lambda st_=st_, tok=tok, rows=rows: nc.sync.dma_start(out=O['sb_v'][l, b, tok, :], in_=st_[:rows, 256:512]),
                              r=[sk_], w=[('o_sbv', j)])
                    elif gi == 1:
                        P.dma('sp', lambda st_=st_, tok=tok, rows=rows: nc.sync.dma_start(out=O['fox_k'][l, b, tok, :], in_=st_[:rows, 0:256]),
                              r=[sk_], w=[('o_fk', j)])
                        P.dma('sp', lambda st_=st_, tok=tok, rows=rows: nc.sync.dma_start(out=O['fox_v'][l, b, tok, :], in_=st_[:rows, 256:512]),
                              r=[sk_], w=[('o_fv', j)])
                    else:
                        P.dma('sp', lambda st_=st_, tok=tok, rows=rows: nc.sync.dma_start(out=O['dsa_k'][l, b, tok, :], in_=st_[:rows, 0:64]),
                              r=[sk_], w=[('o_dk', j)])
                        P.dma('sp', lambda st_=st_, tok=tok, rows=rows: nc.sync.dma_start(out=O['dsa_v'][l, b, tok, :], in_=st_[:rows, 64:128]),
                              r=[sk_], w=[('o_dv', j)])
                        P.dma('sp', lambda st_=st_, tok=tok, rows=rows: nc.sync.dma_start(out=O['dsa_ki'][l, b, tok, :], in_=st_[:rows, 128:192]),
                              r=[sk_], w=[('o_dki', j)])
                        sm = small
                        P.op('dve', lambda st_=st_, rows=rows: nc.vector.tensor_tensor(out=sm[:rows, 0:4], in0=st_[:rows, 192:196],
                                                                                        in1=bfor[:rows, l, :], op=ALU.add),
                             r=[sk_, 'bfor'], w=['small'])
                        P.op('act', lambda rows=rows: nc.scalar.activation(out=sm[:rows, 4:8], in_=sm[:rows, 0:4], func=AF.Exp, scale=-1.0),
                             r=['small'], w=['small2'])
                        P.op('act', lambda rows=rows: nc.scalar.activation(out=sm[:rows, 8:12], in_=sm[:rows, 4:8], func=AF.Ln, bias=1.0),
                             r=['small2'], w=['small3'])
                        P.op('dve', lambda rows=rows, j=j: nc.vector.tensor_scalar(out=LF[:rows, JP + j, :], in0=sm[:rows, 8:12], scalar1=-1.0,
                                                                                   scalar2=None, op0=ALU.mult),
                             r=['small3'], w=['LF'])
                        P.dma('sp', lambda tok=tok, rows=rows, j=j: nc.sync.dma_start(out=O['fox_lf'][l, b, tok, :], in_=LF[:rows, JP + j, :]),
                              r=['LF'], w=[('o_flf', j)])
                        P.op('dve', lambda st_=st_, rows=rows, j=j: nc.vector.tensor_scalar(out=IWs[:rows, j, :], in0=st_[:rows, 196:200], scalar1=0.0,
                                                                                            scalar2=2.0, op0=ALU.is_ge, op1=ALU.mult), r=[sk_], w=['IWs'])
                        P.op('dve', lambda rows=rows, j=j: nc.vector.tensor_scalar(out=IWs[:rows, j, :], in0=IWs[:rows, j, :], scalar1=-1.0,
                                                                                   scalar2=None, op0=ALU.add), r=['IWs'], w=['IWs'])
                        P.op('dve', lambda st_=st_, rows=rows, j=j: nc.vector.scalar_tensor_tensor(out=IW[:rows, j, :], in0=st_[:rows, 196:200], scalar=0.5,
                                                                                                   in1=IWs[:rows, j, :], op0=ALU.mult, op1=ALU.mult),
                             r=[sk_, 'IWs'], w=['IW'])
            c0, gw = TMG[3]
            P.dma('sp', lambda: nc.sync.dma_start(out=wret[:, :, :], in_=wtm_b[l, :, :, c0:c0 + gw]), r=[('wtm', l)], w=['wret'])
            for j in range(NTn):
                rows = rows_of(j)
                for kc in range(KC):
                    mm(ps[:rows, 2, 0:512], hT[:, kc, j * 128:j * 128 + rows], wret[:, kc, 0:512], kc == 0, kc == KC - 1,
                       r=[('hT', j), 'wret'], w=[('ps', 2)], acc=True)
                for kc in range(KC):
                    mm(ps[:rows, 3, 0:256], hT[:, kc, j * 128:j * 128 + rows], wret[:, kc, 512:768], kc == 0, kc == KC - 1,
                       r=[('hT', j), 'wret'], w=[('ps', 3)], acc=True)
                rt = rr(rtm, 'rtm'); rk = ('rtm', id(rt))
                P.dma('sp', lambda rt=rt, j=j, rows=rows: nc.sync.dma_start(
                    out=rt[:rows, :, :], in_=rottm[g][:, j * 128:j * 128 + rows, :].rearrange("a t f -> t a f")), w=[rk])
                v_ = rr(vb, 'vb'); vk = ('vb', id(v_))
                st_ = rr(stg, 'stg'); sk_ = ('stg', id(st_))
                copy_op('act', st_[:rows, 0:256], ps[:rows, 2, 0:256], r=[('ps', 2)], w=[sk_])
                P.dma('sp', lambda st_=st_, j=j, rows=rows: nc.sync.dma_start(out=retv_s[j * 128:j * 128 + rows, :], in_=st_[:rows, 0:256]),
                      r=[sk_], w=[('retv', j)])
                copy_op('pool', v_[:rows, :], st_[:rows, 0:256], r=[sk_], w=[vk])
                P.op('dve', lambda rt=rt, rows=rows: nc.vector.tensor_tensor(out=t1[:rows, :], in0=ps[:rows, 2, 256:512], in1=rt[:rows, 0, :], op=ALU.mult),
                     r=[('ps', 2), rk], w=['t1'])
                P.op('dve', lambda rt=rt, rows=rows: nc.vector.tensor_tensor(out=t2[:rows, :], in0=ps[:rows, 3, 0:256], in1=rt[:rows, 1, :], op=ALU.mult),
                     r=[('ps', 3), rk], w=['t2'])
                k_ = rr(kd, 'kd'); kk = ('kd', id(k_))
                P.op('pool', lambda k_=k_, rows=rows: nc.gpsimd.tensor_tensor(out=k_[:rows, :], in0=t1[:rows, :], in1=t2[:rows, :], op=ALU.add),
                     r=['t1', 't2'], w=[kk])
                for h in range(4):
                    mm(ps[0:64, 4, h * 64:(h + 1) * 64], k_[:rows, h * 64:(h + 1) * 64], v_[:rows, h * 64:(h + 1) * 64],
                       (j == 0 and h == 0), (j == NTn - 1 and h == 3), r=[kk, vk], w=[('ps', 4)], acc=True)
            if PL > 0:
                P.dma('sp', lambda: nc.sync.dma_start(out=s0t[0:64, :, :], in_=st_ret[l, b].rearrange("h d e -> d h e")), w=['s0t'])
                for h in range(4):
                    P.op('dve', lambda h=h: nc.vector.scalar_tensor_tensor(
                        out=stout[0:64, h * 64:(h + 1) * 64], in0=s0t[0:64, h, :], scalar=float(GAMMA[h] ** T),
                        in1=ps[0:64, 4, h * 64:(h + 1) * 64], op0=ALU.mult, op1=ALU.add), r=['s0t', ('ps', 4)], w=['stout'])
            else:
                copy_op('dve', stout[0:64, :], ps[0:64, 4, 0:256], r=[('ps', 4)], w=['stout'])
            P.dma('sp', lambda: nc.sync.dma_start(out=O['ret'][l, b].rearrange("h d e -> d h e"),
                                                  in_=stout[0:64, :].rearrange("p (h e) -> p h e", h=4)), r=['stout'], w=['o_ret'])
            P.barrier()

            for br in range(4):
                branch(br, g, b, l, T, PL, Ltot, NT, NTn, JP, QB, NQB, nq, NQT, topk, O, rows_of)
                P.barrier()
            phase3(g, b, l, T, NTn, O)
            P.barrier()

        def proj_fm(l, ch, T, QB, NQB, wbuf, evac, msl=None, bank_list=(0, 1), bkey='fmb'):
            wk = ('wch', id(wbuf))
            P.dma('sp', lambda: nc.sync.dma_start(out=wbuf[:, :, :], in_=wf_b[l, ch]), r=[('wf', l, ch)], w=[wk])
            for qi in range(NQB):
                bank = rr(list(bank_list), bkey)
                M = 128 if msl is None else 64
                for kc in range(KC):
                    lw = wbuf[:, kc, :] if msl is None else wbuf[:, kc, msl]
                    mm(ps[0:M, bank, 0:QB], lw, hT[:, kc, qi * QB:(qi + 1) * QB], kc == 0, kc == KC - 1,
                       r=[wk] + [('hT', jj) for jj in range(qi * QB // 128, max(qi * QB // 128 + 1, (qi + 1) * QB // 128))],
                       w=[('ps', bank)], acc=True)
                evac(qi, bank)

        def load_past_T(l, b, src, KT_dst_fn, ncol, JP, xst):
            for j in range(JP):
                xs_ = rr(xst, 'xst2'); xk = ('xst2', id(xs_))
                if isinstance(src, tuple):
                    P.dma('sp', lambda xs_=xs_, j=j: nc.sync.dma_start(out=xs_[:, 0:64], in_=src[0][l, b, j * 128:(j + 1) * 128, :]), w=[xk])
                    P.dma('sp', lambda xs_=xs_, j=j: nc.sync.dma_start(out=xs_[:, 64:128], in_=src[1][l, b, j * 128:(j + 1) * 128, :]), w=[xk + ('b',)])
                    rk = [xk, xk + ('b',)]
                    npair = 1
                else:
                    P.dma('sp', lambda xs_=xs_, j=j: nc.sync.dma_start(out=xs_[:, 0:256], in_=src[l, b, j * 128:(j + 1) * 128, :]), w=[xk])
                    rk = [xk]
                    npair = 2
                bank = rr([0, 1], 'ptb')
                for pr in range(npair):
                    P.op('pe', lambda xs_=xs_, pr=pr, bank=bank: nc.tensor.transpose(
                        out=ps[:, bank, pr * 128:(pr + 1) * 128], in_=xs_[:, pr * 128:(pr + 1) * 128], identity=ident_f),
                        r=rk + ['mats'], w=[('ps', bank)], acc=True)
                for pr in range(npair):
                    copy_op(evac_eng(), KT_dst_fn(pr, j), ps[:, bank, pr * 128:(pr + 1) * 128], r=[('ps', bank)], w=['KT'])

        def branch(br, g, b, l, T, PL, Ltot, NT, NTn, JP, QB, NQB, nq, NQT, topk, O, rows_of):
            ar16.reset(); ar32.reset()
            Lp = NT * 128
            kd_ = {0: 0, 1: 2, 2: 1, 3: 3}[br]
            wch = [ar16.get(KC, 128) for _ in range(3)]
            ybuf = ar16.get(4, QB)
            xst = [ar32.get(256) for _ in range(2)]
            srow = ar32.get(512); rbc = ar32.get(512)
            if kd_ in (0, 1, 2):
                Pk = 0 if kd_ == 2 else PL
                NTk = NTn if kd_ == 2 else NT
                QT = ar16.get(2, T)
                KT = ar16.get(2, NTk * 128)
                V = ar16.get(NTk, 5, 65)
                Vflat = V.rearrange("p j h d -> p j (h d)")
                Pt = [ar16.get(QB) for _ in range(4)]
                P.op('pool', lambda: nc.gpsimd.memset(KT[:, :, :], 0.0), w=['KT'])
                P.op('pool', lambda: nc.gpsimd.memset(V[:, :, :, :], 0.0), w=['V'])
                if kd_ == 1:
                    P.op('pool', lambda: nc.gpsimd.memset(V[:, :, 0:4, 64:65], 1.0), r=['V'], w=['V'])
                if kd_ == 0:
                    qc, kc_, vout, kcache, vcache = CH_SB, CH_SB + 2, 'sb_v', c_sbk, c_sbv
                    vkeys = [('o_sbv', j) for j in range(NTn)]
                elif kd_ == 1:
                    qc, kc_, vout, kcache, vcache = CH_FOX, CH_FOX + 2, 'fox_v', c_fk, c_fv
                    vkeys = [('o_fv', j) for j in range(NTn)]
                if kd_ in (0, 1):
                    for pr in range(2):
                        proj_fm(l, qc + pr, T, QB, NQB, rr(wch, 'wch'),
                                lambda qi, bank, pr=pr: copy_op(evac_eng(), QT[:, pr, qi * QB:(qi + 1) * QB], ps[:, bank, 0:QB],
                                                                r=[('ps', bank)], w=['QT'], scale=0.125))
                        proj_fm(l, kc_ + pr, T, QB, NQB, rr(wch, 'wch'),
                                lambda qi, bank, pr=pr: copy_op(evac_eng(), KT[:, pr, PL + qi * QB:PL + (qi + 1) * QB], ps[:, bank, 0:QB],
                                                                r=[('ps', bank)], w=['KT']))
                    if PL > 0:
                        load_past_T(l, b, kcache, lambda pr, j: KT[:, pr, j * 128:(j + 1) * 128], 256, JP, xst)
                        for j in range(JP):
                            P.dma('pool', lambda j=j: nc.gpsimd.dma_start(
                                out=V[:, j, 0:4, 0:64], in_=vcache[l, b, j * 128:(j + 1) * 128, :].rearrange("t (h d) -> t h d", h=4)), r=['V'], w=['V'])
                    for j in range(NTn):
                        rows = rows_of(j)
                        P.dma('pool', lambda j=j, rows=rows: nc.gpsimd.dma_start(
                            out=V[0:rows, JP + j, 0:4, 0:64], in_=O[vout][l, b, j * 128:j * 128 + rows, :].rearrange("t (h d) -> t h d", h=4)),
                            r=[vkeys[j], 'V'], w=['V'])
                else:
                    rtab = [ar32.get(QB) for _ in range(4)]
                    tq1 = ar32.get(QB); tq2 = ar32.get(QB)
                    for which, base, dst in ((0, CH_RET, QT), (1, CH_RET + 4, KT)):
                        for pr in range(2):
                            wa = rr(wch, 'wch'); wka = ('wch', id(wa))
                            wb_ = rr(wch, 'wch'); wkb = ('wch', id(wb_))
                            P.dma('sp', lambda wa=wa, ch=base + pr: nc.sync.dma_start(out=wa[:, :, :], in_=wf_b[l, ch]), r=[('wf', l, base + pr)], w=[wka])
                            P.dma('sp', lambda wb_=wb_, ch=base + 2 + pr: nc.sync.dma_start(out=wb_[:, :, :], in_=wf_b[l, ch]), r=[('wf', l, base + 2 + pr)], w=[wkb])
                            for qi in range(NQB):
                                blk = slice(qi * QB, (qi + 1) * QB)
                                hk = [('hT', jj) for jj in range(qi * QB // 128, max(qi * QB // 128 + 1, (qi + 1) * QB // 128))]
                                bA = rr([0, 1], 'rA'); bB = rr([2, 3], 'rB')
                                for kc in range(KC):
                                    mm(ps[:, bA, 0:QB], wa[:, kc, :], hT[:, kc, blk], kc == 0, kc == KC - 1, r=[wka] + hk, w=[('ps', bA)], acc=True)
                                for kc in range(KC):
                                    mm(ps[:, bB, 0:QB], wb_[:, kc, :], hT[:, kc, blk], kc == 0, kc == KC - 1, r=[wkb] + hk, w=[('ps', bB)], acc=True)
                                tc_ = rr(rtab, 'rtab'); tck = ('rtab', id(tc_))
                                ts_ = rr(rtab, 'rtab'); tsk = ('rtab', id(ts_))
                                P.dma('sp', lambda tc_=tc_, pr=pr, blk=blk, ti=which * 2: nc.sync.dma_start(out=tc_[:, :], in_=rotfm[g][ti, :, pr, blk]), w=[tck])
                                P.dma('sp', lambda ts_=ts_, pr=pr, blk=blk, ti=which * 2 + 1: nc.sync.dma_start(out=ts_[:, :], in_=rotfm[g][ti, :, pr, blk]), w=[tsk])
                                P.op('dve', lambda tc_=tc_, bA=bA: nc.vector.tensor_tensor(out=tq1[:, :], in0=ps[:, bA, 0:QB], in1=tc_[:, :], op=ALU.mult),
                                     r=[('ps', bA), tck], w=['tq1'])
                                P.op('dve', lambda ts_=ts_, bB=bB: nc.vector.tensor_tensor(out=tq2[:, :], in0=ps[:, bB, 0:QB], in1=ts_[:, :], op=ALU.mult),
                                     r=[('ps', bB), tsk], w=['tq2'])
                                P.op('pool', lambda dst=dst, pr=pr, blk=blk: nc.gpsimd.tensor_tensor(out=dst[:, pr, blk], in0=tq1[:, :], in1=tq2[:, :], op=ALU.add),
                                     r=['tq1', 'tq2'], w=['QT' if which == 0 else 'KT'])
                    for j in range(NTn):
                        rows = rows_of(j)
                        P.dma('pool', lambda j=j, rows=rows: nc.gpsimd.dma_start(
                            out=V[0:rows, j, 0:4, 0:64], in_=retv_s[j * 128:j * 128 + rows, :].rearrange("t (h d) -> t h d", h=4)),
                            r=[('retv', j), 'V'], w=['V'])
                    wg = [ar16.get(KC, 128) for _ in range(2)]
                    for pr in range(2):
                        P.dma('sp', lambda pr=pr: nc.sync.dma_start(out=wg[pr][:, :, :], in_=wf_b[l, CH_RET + 8 + pr]), r=[('wf', l, CH_RET + 8 + pr)], w=[('wg', pr)])
                    fsets = [[(ar32.get(QB), 'fA%d' % i_) for i_ in range(4)], [(rtab[i_], ('rtab', id(rtab[i_]))) for i_ in range(4)]]
                    fb16 = [(ar16.get(QB), 'fb16_%d' % i_) for i_ in range(2)]
                    if PL > 0:
                        assert NQB == 1
                        s0f = ar32.get(2, 64)
                        s0b = ar16.get(2, 64)
                        P.dma('sp', lambda: nc.sync.dma_start(out=s0f[:, :, :], in_=st_ret[l, b].rearrange("(hp hh) d e -> (hh d) hp e", hh=2)), w=['s0f'])
                        for h in range(4):
                            hp, hh = h // 2, h % 2
                            P.op('dve', lambda hp=hp, hh=hh, h=h: nc.vector.tensor_scalar(
                                out=s0b[hh * 64:(hh + 1) * 64, hp, :], in0=s0f[hh * 64:(hh + 1) * 64, hp, :], scalar1=float(GAMMA[h]), scalar2=None,
                                op0=ALU.mult), r=['s0f'], w=['s0b'])
                if kd_ == 0:
                    e32 = [ar32.get(QB) for _ in range(4)]
                    spb = [ar16.get(QB) for _ in range(6)]
                    exb = [ar16.get(QB) for _ in range(3)]
                if kd_ == 1:
                    biasF = ar32.get(NQB, NT * 4)
                    fsets = [[(ar32.get(QB), 'fF%d_%d' % (s_, i_)) for i_ in range(3)] for s_ in range(2)]
                    cst = ar32.get(NT * 4)
                    P.op('pe', lambda: nc.tensor.matmul(ps[:, 2, 0:NT * 4], lhsT=tri, rhs=LF[:, 0:NT, :].rearrange("p j h -> p (j h)"), start=True, stop=True),
                         r=['LF', 'mats'], w=[('ps', 2)])
                    copy_op('dve', cst[:, :], ps[:, 2, 0:NT * 4], r=[('ps', 2)], w=['cst'])
                    P.op('pe', lambda: nc.tensor.matmul(ps[:, 3, 0:NT * 4], lhsT=mats[:, 5, :], rhs=cst[:, :], start=True, stop=True),
                         r=['cst', 'mats'], w=[('ps', 3)])
                    tot = ar32.get(NT * 4); pre = ar32.get(NT * 4)
                    copy_op('dve', tot[:, :], ps[:, 3, 0:NT * 4], r=[('ps', 3)], w=['tot'])
                    totv = tot.rearrange("p (j h) -> p j h", h=4)
                    prev = pre.rearrange("p (j h) -> p j h", h=4)
                    for h in range(4):
                        P.op('dve', lambda h=h: nc.vector.tensor_tensor_scan(out=prev[:, :, h], data0=mats[:, 4, 0:NT], data1=totv[:, :, h], initial=0.0,
                                                                             op0=ALU.mult, op1=ALU.add), r=['tot', 'mats'], w=['pre'])
                    P.op('dve', lambda: nc.vector.tensor_tensor(out=pre[:, :], in0=pre[:, :], in1=tot[:, :], op=ALU.subtract), r=['pre', 'tot'], w=['pre'])
                    P.op('dve', lambda: nc.vector.tensor_tensor(out=CS[:, 0:NT, :].rearrange("p j h -> p (j h)"), in0=cst[:, :], in1=pre[:, :], op=ALU.add),
                         r=['cst', 'pre'], w=['CS'])
                    for qi in range(NQB):
                        tmid = PL + qi * QB + QB // 2
                        jm, rm = tmid // 128, tmid % 128
                        selm = {0: 6, 16: 7, 127: 5}[rm]
                        P.op('pe', lambda jm=jm, selm=selm: nc.tensor.matmul(ps[:, 2, 0:4], lhsT=mats[:, selm, :], rhs=CS[:, jm, :], start=True, stop=True),
                             r=['CS', 'mats'], w=[('ps', 2)])
                        copy_op('dve', srow[:, 0:4], ps[:, 2, 0:4], r=[('ps', 2)], w=['srow'])
                        P.op('dve', lambda qi=qi: nc.vector.tensor_tensor(
                            out=biasF[:, qi, :].rearrange("p (j h) -> p j h", h=4), in0=srow[:, 0:4].unsqueeze(1).to_broadcast([128, NT, 4]),
                            in1=CS[:, 0:NT, :], op=ALU.subtract), r=['srow', 'CS'], w=['biasF'])

                ybufs = [ybuf, ar16.get(4, QB)]
                groups = []
                for qi in range(NQB):
                    q0 = qi * QB
                    qg = Pk + q0
                    hsets = [(0, 1), (2, 3)]
                    for hs in hsets:
                        lists = []
                        for h in hs:
                            tl = []
                            for j in range(NTk):
                                d = 128 * j - qg
                                if kd_ == 0:
                                    if d >= QB - 1:
                                        continue
                                else:
                                    if d > QB - 1:
                                        continue
                                tl.append((j, d))
                            if kd_ == 0:
                                tl = tl[::-1]
                            lists.append([dict(qi=qi, q0=q0, h=h, j=j, d=d, n=n, last=(n == len(tl) - 1)) for n, (j, d) in enumerate(tl)])
                        for tup in zip(*lists):
                            groups.extend(tup)
                tasks = groups
                gstate = {}

                def stA(t):
                    h = t['h']; hp, hh = h // 2, h % 2
                    prt = slice(hh * 64, hh * 64 + 64)
                    zb = rr([0, 1], 'zb') if kd_ == 0 else rr([0, 1, 2, 3], 'zb')
                    t['zb'] = zb
                    q0 = t['q0']; j = t['j']
                    mm(ps[:, zb, 0:QB], KT[prt, hp, j * 128:(j + 1) * 128], QT[prt, hp, q0:q0 + QB], True, True,
                       r=['KT', 'QT'], w=[('ps', zb)])

                def stB(t):
                    zb = t['zb']; j = t['j']; d = t['d']; h = t['h']; qi = t['qi']
                    diag = d > -128
                    if kd_ == 1:
                        pt = rr(Pt, 'Pt'); pk = ('Pt', id(pt))
                        t['pt'] = (pt, pk)
                        P.op('act', lambda: nc.scalar.activation(
                            out=pt[:, :], in_=ps[:, zb, 0:QB], func=AF.Exp, bias=biasF[:, qi, j * 4 + h:j * 4 + h + 1]),
                            r=[('ps', zb), 'biasF'], w=[pk])
                        if diag:
                            P.op('pool', lambda: nc.gpsimd.tensor_tensor(out=pt[:, :], in0=pt[:, :], in1=masks[:, 0, 384 - d:384 - d + QB], op=ALU.mult),
                                 r=[pk, 'masks'], w=[pk])
                    elif kd_ == 2:
                        pt = rr(Pt, 'Pt'); pk = ('Pt', id(pt))
                        t['pt'] = (pt, pk)
                        cij = GAMMA[h] ** (t['q0'] - 128 * j - 127)
                        copy_op('dve', pt[:, :], ps[:, zb, 0:QB], r=[('ps', zb)], w=[pk], scale=cij)
                        if diag:
                            P.op('pool', lambda: nc.gpsimd.tensor_tensor(out=pt[:, :], in0=pt[:, :], in1=masks[:, 0, 384 - d:384 - d + QB], op=ALU.mult),
                                 r=[pk, 'masks'], w=[pk])
                    else:
                        e_ = rr(e32, 'e32'); ek = ('e32', id(e_))
                        P.op('act', lambda: nc.scalar.activation(out=e_[:, :], in_=ps[:, zb, 0:QB], func=AF.Exp), r=[('ps', zb)], w=[ek])
                        if diag:
                            P.op('pool', lambda: nc.gpsimd.tensor_tensor(out=e_[:, :], in0=e_[:, :], in1=masks[:, 1, 384 - d:384 - d + QB], op=ALU.mult),
                                 r=[ek, 'masks'], w=[ek])
                        sp_ = rr(spb, 'spb'); spk = ('spb', id(sp_))
                        P.op('act', lambda: nc.scalar.activation(out=sp_[:, :], in_=e_[:, :], func=AF.Ln, bias=1.0), r=[ek], w=[spk])
                        t['e'] = (e_, ek); t['sp'] = (sp_, spk)

                def stC1(t):
                    h = t['h']
                    xb = 2 + (h % 2)
                    key = (t['qi'], h)
                    prev_sp = gstate.get(('sp', key))
                    sp_, spk = t['sp']
                    e_, ek = t['e']
                    if prev_sp is not None:
                        mm(ps[:, xb, 0:QB], negL, prev_sp[0][:, :], False, False, r=[prev_sp[1], 'matb'], w=[('ps', xb)], acc=True)
                    mm(ps[:, xb, 0:QB], negU, sp_[:, :], t['n'] == 0, True, r=[spk, 'matb'], w=[('ps', xb)], acc=True)
                    gstate[('sp', key)] = (sp_, spk)
                    ex_ = rr(exb, 'exb'); exk = ('exb', id(ex_))
                    P.op('act', lambda: nc.scalar.activation(out=ex_[:, :], in_=ps[:, xb, 0:QB], func=AF.Exp), r=[('ps', xb)], w=[exk])
                    pt = rr(Pt, 'Pt'); pk = ('Pt', id(pt))
                    t['pt'] = (pt, pk)
                    P.op('pool', lambda: nc.gpsimd.tensor_tensor(out=pt[:, :], in0=e_[:, :], in1=ex_[:, :], op=ALU.mult),
                         r=[ek, exk], w=[pk])

                def stC(t):
                    h = t['h']; hp, hh = h // 2, h % 2
                    prt = slice(hh * 64, hh * 64 + 64)
                    qi = t['qi']; q0 = t['q0']; j = t['j']
                    key = (qi, h)
                    ob = 4 + (h % 2)
                    M = 65 if kd_ == 1 else 64
                    started = t['n'] > 0
                    if kd_ == 2 and PL > 0 and t['n'] == 0:
                        mm(ps[0:64, ob, 0:QB], s0b[prt, hp, :], QT[prt, hp, q0:q0 + QB], True, False, r=['s0b', 'QT'], w=[('ps', ob)], acc=True)
                        started = True
                    pt, pk = t['pt']
                    if kd_ == 2 and PL > 0:
                        mm(ps[0:M, ob, 0:QB], V[:, j, h, 0:M], pt[:, :], not started, t['last'], r=['V', pk], w=[('ps', ob)], acc=True)
                    else:
                        mm(ps[:, ob, 0:QB], Vflat[:, j, h * 65:h * 65 + 128], pt[:, :], not started, t['last'], r=['V', pk], w=[('ps', ob)], acc=True)
                    if not t['last']:
                        return
                    yb_ = ybufs[qi % 2]
                    yk = ('ybuf', qi % 2)
                    stages = []

                    def fin_done():
                        gstate[('done', qi)] = gstate.get(('done', qi), 0) + 1
                        if gstate[('done', qi)] == 4:
                            P.dma('sp', lambda: nc.sync.dma_start(out=ysc[br, :, :, q0:q0 + QB], in_=yb_[0:64, :, :]), r=[yk], w=[('ysc', br, qi)])
                            if debug:
                                P.dma('sp', lambda: nc.sync.dma_start(out=ydbg[0 if g == 'p' else 1, l, br, :, :, q0:q0 + QB], in_=yb_[0:64, :, :]),
                                      r=[yk], w=[('ydbg', br, qi, l, g)])
                    if kd_ == 0:
                        copy_op('dve', yb_[0:64, h, :], ps[0:64, ob, 0:QB], r=[('ps', ob)], w=[yk])
                        fin_done()
                    elif kd_ == 1:
                        fsi = gstate.get('fsel', 0) % 2
                        for st_l in list(pending):
                            if st_l[0] == fsi:
                                while len(st_l) > 1:
                                    st_l.pop(1)()
                                pending.remove(st_l)
                        fs = fsets[fsi]; gstate['fsel'] = gstate.get('fsel', 0) + 1
                        (sr, srk), (lnr, lnk), (rb, rbk) = fs

                        def f1():
                            copy_op('act', sr[64:65, 0:QB], ps[64:65, ob, 0:QB], r=[('ps', ob)], w=[srk])
                            P.op('pe', lambda: nc.tensor.matmul(ps[0:64, 6, 0:QB], lhsT=mats[64:65, 4, 0:64], rhs=sr[64:65, 0:QB], start=True, stop=True),
                                 r=[srk, 'mats'], w=[('ps', 6)])
                            P.op('dve', lambda: nc.vector.reciprocal(out=rb[0:64, 0:QB], in_=ps[0:64, 6, 0:QB]), r=[('ps', 6)], w=[rbk])

                        def f2():
                            P.op('dve', lambda: nc.vector.tensor_tensor(out=yb_[0:64, h, :], in0=ps[0:64, ob, 0:QB], in1=rb[0:64, 0:QB], op=ALU.mult),
                                 r=[('ps', ob), rbk], w=[yk])
                            fin_done()
                        f1()
                        stages = [fsi, f2]
                    else:
                        fsi = gstate.get('fsel', 0) % 2
                        for st_l in list(pending):
                            if st_l[0] == fsi:
                                while len(st_l) > 1:
                                    st_l.pop(1)()
                                pending.remove(st_l)
                        fs = fsets[fsi]
                        ob16, ob16k = fb16[fsi]
                        gstate['fsel'] = gstate.get('fsel', 0) + 1
                        (osb, osk), (xc, xck), (sd, sdk), (sg, sgk) = fs
                        ones_b = matb[0:64, 3, 0:64]

                        def f1():
                            copy_op('act', osb[0:64, :], ps[0:64, ob, 0:QB], r=[('ps', ob)], w=[osk])
                            copy_op('act', ob16[0:64, :], ps[0:64, ob, 0:QB], r=[('ps', ob)], w=[ob16k])
                            P.op('pe', lambda: nc.tensor.matmul(ps[0:64, 6, 0:QB], lhsT=ones_b, rhs=ob16[0:64, :], start=True, stop=True),
                                 r=[ob16k, 'matb'], w=[('ps', 6)])
                            P.op('dve', lambda: nc.vector.scalar_tensor_tensor(out=xc[0:64, :], in0=ps[0:64, 6, 0:QB], scalar=-1.0 / 64, in1=osb[0:64, :],
                                                                               op0=ALU.mult, op1=ALU.add), r=[('ps', 6), osk], w=[xck])

                        def f2():
                            P.op('dve', lambda: nc.vector.tensor_tensor(out=ob16[0:64, :], in0=xc[0:64, :], in1=xc[0:64, :], op=ALU.mult), r=[xck], w=[ob16k])
                            P.op('pe', lambda: nc.tensor.matmul(ps[0:64, 6, 0:QB], lhsT=ones_b, rhs=ob16[0:64, :], start=True, stop=True),
                                 r=[ob16k, 'matb'], w=[('ps', 6)])
                            P.op('act', lambda: nc.scalar.activation(out=sd[0:64, :], in_=ps[0:64, 6, 0:QB], func=AF.Ln, scale=1.0 / 64, bias=eps_t[0:64, 0:1]),
                                 r=[('ps', 6), 'eps'], w=[sdk])

                        def f3():
                            P.op('act', lambda: nc.scalar.activation(out=sd[0:64, :], in_=sd[0:64, :], func=AF.Exp, scale=-0.5), r=[sdk], w=[sdk])
                            P.op('dve', lambda: nc.vector.tensor_tensor(out=xc[0:64, :], in0=xc[0:64, :], in1=sd[0:64, :], op=ALU.mult), r=[xck, sdk], w=[xck])
                            for kc in range(KC):
                                mm(ps[0:64, 6, 0:QB], wg[hp][:, kc, hh * 64:(hh + 1) * 64], hT[:, kc, q0:q0 + QB], kc == 0, kc == KC - 1,
                                   r=[('wg', hp)] + [('hT', jj) for jj in range(NTn)], w=[('ps', 6)], acc=True)
                            P.op('act', lambda: nc.scalar.activation(out=sg[0:64, :], in_=ps[0:64, 6, 0:QB], func=AF.Exp, scale=-1.0), r=[('ps', 6)], w=[sgk])
                            copy_op('dve', sd[0:64, :], ps[0:64, 6, 0:QB], r=[('ps', 6), xck], w=[sdk])
                            P.op('act', lambda: nc.scalar.activation(out=sg[0:64, :], in_=sg[0:64, :], func=AF.Ln, bias=1.0), r=[sgk], w=[sgk])

                        def f4():
                            P.op('act', lambda: nc.scalar.activation(out=sg[0:64, :], in_=sg[0:64, :], func=AF.Exp, scale=-1.0), r=[sgk], w=[sgk])
                            P.op('dve', lambda: nc.vector.tensor_tensor(out=sd[0:64, :], in0=sd[0:64, :], in1=sg[0:64, :], op=ALU.mult),
                                 r=[sdk, sgk], w=[sdk])
                            P.op('dve', lambda: nc.vector.tensor_tensor(out=yb_[0:64, h, :], in0=xc[0:64, :], in1=sd[0:64, :], op=ALU.mult),
                                 r=[xck, sdk], w=[yk])
                            fin_done()
                        f1()
                        stages = [fsi, f2, f3, f4]
                    if stages:
                        pending.append(stages)

                pending = []
                NTK = len(tasks)
                if kd_ == 0:
                    steps = [[t_] for t_ in tasks]
                else:
                    steps = [tasks[i_:i_ + 2] for i_ in range(0, NTK, 2)]
                NS = len(steps)
                for s_ in range(NS + 2):
                    if s_ < NS:
                        for t_ in steps[s_]:
                            stA(t_)
                        for t_ in steps[s_]:
                            stB(t_)
                    if kd_ == 0 and 0 <= s_ - 1 < NS:
                        for t_ in steps[s_ - 1]:
                            stC1(t_)
                    sk_ = 2 if kd_ == 0 else 1
                    if 0 <= s_ - sk_ < NS:
                        for t_ in steps[s_ - sk_]:
                            stC(t_)
                    for st_l in list(pending):
                        st_l.pop(1)()
                        if len(st_l) == 1:
                            pending.remove(st_l)
                while pending:
                    for st_l in list(pending):
                        st_l.pop(1)()
                        if len(st_l) == 1:
                            pending.remove(st_l)
                return

            NIT = 20
            QT = ar16.get(NQT, 4, nq)
            KT = ar16.get(1, Lp)
            Vd = ar16.get(NT, 65)
            Pt = [ar16.get(4 * nq) for _ in range(4)]
            negsel = [ar16.get(Lp) for _ in range(2)]
            I4 = ar16.get(4 * nq)
            score = ar32.get(Lp)
            tmp = ar32.get(512)
            bs = ar32.get(8)
            wtab = ar32.get(NIT + 1)
            wtab2 = ar32.get(NIT + 1)
            P.op('pool', lambda: nc.gpsimd.memset(KT[:, :, :], 0.0), w=['KT'])
            P.op('pool', lambda: nc.gpsimd.memset(Vd[:, :, :], 0.0), w=['V'])
            P.op('pool', lambda: nc.gpsimd.memset(Vd[:, :, 64:65], 1.0), w=['V'])
            for h in range(4):
                P.op('dve', lambda h=h: nc.vector.tensor_copy(out=I4[0:nq, h * nq:(h + 1) * nq], in_=ident_b[0:nq, 0:nq]), r=['matb'], w=['I4'])
            tpb = QB // nq
            for h in range(4):
                proj_fm(l, CH_DSA + h, T, QB, NQB, rr(wch, 'wch'),
                        lambda qi, bank, h=h: copy_op(evac_eng(), QT[:, qi * tpb:(qi + 1) * tpb, h, :],
                                                      ps[:, bank, 0:QB].rearrange("p (t q) -> p t q", t=tpb), r=[('ps', bank)], w=['QT'], scale=0.125))
            proj_fm(l, CH_DSA + 4, T, QB, NQB, rr(wch, 'wch'),
                    lambda qi, bank: copy_op(evac_eng(), KT[:, 0, PL + qi * QB:PL + (qi + 1) * QB], ps[:, bank, 0:QB], r=[('ps', bank)], w=['KT']))
            if PL > 0:
                load_past_T(l, b, (c_dk, c_dki), lambda pr, j: KT[:, 0, j * 128:(j + 1) * 128], 128, JP, xst)
                P.dma('pool', lambda: nc.gpsimd.dma_start(out=Vd[:, 0:JP, 0:64], in_=c_dv[l, b].rearrange("(j p) d -> p j d", p=128)), r=['V'], w=['V'])
            vkeys = [('o_dv', j) for j in range(NTn)]
            if T >= 128:
                P.dma('pool', lambda: nc.gpsimd.dma_start(out=Vd[:, JP:JP + NTn, 0:64], in_=O['dsa_v'][l, b].rearrange("(j p) d -> p j d", p=128)),
                      r=vkeys + ['V'], w=['V'])
            else:
                P.dma('pool', lambda: nc.gpsimd.dma_start(out=Vd[0:T, JP, 0:64], in_=O['dsa_v'][l, b]), r=vkeys + ['V'], w=['V'])
            ybufs = [ybuf, ar16.get(4, QB)]

            def geom(qt):
                tg0 = PL + qt * nq
                nk = min(Ltot, ((tg0 + nq - 1) // 64 + 1) * 64)
                nkt = (nk + 127) // 128
                return nk, nkt, nkt * 128

            tmps = [tmp, ar32.get(512)]

            def prep(qt):
                nk, nkt, nkp = geom(qt)
                ns = negsel[qt % 2]; nsk = ('negsel', qt % 2)
                for kb in range((nkp + 511) // 512):
                    kw = min(512, nkp - kb * 512)
                    for h in range(4):
                        mm(ps[0:nq, h, 0:kw], QT[64:128, qt, h, :], KT[64:128, 0, kb * 512:kb * 512 + kw], True, True, r=['QT', 'KT'], w=[('ps', h)])
                    ksl = slice(kb * 512, kb * 512 + kw)
                    for h in range(4):
                        t_ = rr(tmps, 'tmps'); tk = ('tmps', id(t_))
                        P.op('act', lambda t_=t_, h=h, kw=kw: nc.scalar.activation(out=t_[0:nq, 0:kw], in_=ps[0:nq, h, 0:kw], func=AF.Relu, scale=IW[0:nq, qt, h:h + 1]),
                             r=[('ps', h), 'IW'], w=[tk])
                        if h == 0:
                            P.op('dve', lambda t_=t_, ksl=ksl, kw=kw: nc.vector.tensor_scalar(out=score[0:nq, ksl], in0=t_[0:nq, 0:kw], scalar1=IWs[0:nq, qt, 0:1], scalar2=None,
                                                                                              op0=ALU.mult), r=[tk, 'IWs'], w=['score'])
                        else:
                            P.op('dve', lambda t_=t_, ksl=ksl, kw=kw, h=h: nc.vector.scalar_tensor_tensor(out=score[0:nq, ksl], in0=t_[0:nq, 0:kw], scalar=IWs[0:nq, qt, h:h + 1],
                                                                                                         in1=score[0:nq, ksl], op0=ALU.mult, op1=ALU.add),
                                 r=[tk, 'IWs', 'score'], w=['score'])
                    yield
                if nkp > nk:
                    P.op('pool', lambda: nc.gpsimd.memset(score[0:nq, nk:nkp], NEG), r=['score'], w=['score'])
                if nq == 128:
                    P.op('pool', lambda: nc.gpsimd.memset(score[0:64, nk - 64:nk], NEG), r=['score'], w=['score'])
                    ncommon = nk - 64
                else:
                    ncommon = nk
                if nk <= topk:
                    P.op('dve', lambda: nc.vector.tensor_scalar(out=ns[0:nq, 0:nkp], in0=score[0:nq, 0:nkp], scalar1=-1.0e29, scalar2=-30000.0,
                                                                op0=ALU.is_lt, op1=ALU.mult), r=['score'], w=[nsk])
                    return
                assert ncommon >= topk
                n1 = nkp if nkp < 768 else ((nkp * 9 // 16) // 64) * 64
                n2 = nkp - n1
                nmin = min(ncommon, max(topk, 256))
                P.op('dve', lambda: nc.vector.tensor_reduce(out=bs[0:nq, 0:1], in_=score[0:nq, 0:nmin], axis=mybir.AxisListType.X, op=ALU.min),
                     r=['score'], w=['b_lo'])
                P.op('dve', lambda: nc.vector.tensor_reduce(out=bs[0:nq, 1:2], in_=score[0:nq, 0:nkp], axis=mybir.AxisListType.X, op=ALU.max),
                     r=['score'], w=['b_hi'])
                P.op('dve', lambda: nc.vector.tensor_tensor(out=bs[0:nq, 1:2], in0=bs[0:nq, 1:2], in1=bs[0:nq, 0:1], op=ALU.subtract), r=['b_hi', 'b_lo'], w=['b_hi'])
                P.op('dve', lambda: nc.vector.tensor_scalar(out=wtab[0:nq, 0:NIT + 1], in0=mats[0:nq, 8, 0:NIT + 1], scalar1=bs[0:nq, 1:2], scalar2=None, op0=ALU.mult),
                     r=['b_hi', 'mats'], w=['wtab'])
                P.op('dve', lambda: nc.vector.tensor_scalar(out=wtab2[0:nq, 0:NIT + 1], in0=mats[0:nq, 8, 0:NIT + 1], scalar1=bs[0:nq, 1:2], scalar2=2.0, op0=ALU.mult, op1=ALU.mult),
                     r=['b_hi', 'mats'], w=['wtab2'])
                P.op('dve', lambda: nc.vector.tensor_tensor(out=bs[0:nq, 2:3], in0=bs[0:nq, 0:1], in1=wtab[0:nq, 0:1], op=ALU.add), r=['b_lo', 'wtab'], w=['b_mid'])
                yield
                thr2 = 2.0 * (float(topk) - 0.5) - n2
                for it in range(NIT):
                    if n2 > 0:
                        P.op('act', lambda: nc.scalar.activation(out=ns[0:nq, n1:nkp], in_=score[0:nq, n1:nkp], func=AF.Sign, bias=bs[0:nq, 2:3], scale=-1.0,
                                                                 accum_out=bs[0:nq, 5:6]), r=['score', 'b_mid'], w=['b_acc', (nsk, 'b')])
                    P.op('dve', lambda: nc.vector.tensor_scalar(out=ns[0:nq, 0:n1], in0=score[0:nq, 0:n1], scalar1=bs[0:nq, 2:3], scalar2=0.0,
                                                                op0=ALU.is_ge, op1=ALU.add, accum_out=bs[0:nq, 3:4]), r=['score', 'b_mid'], w=['b_cnt', (nsk, 'a')])
                    if n2 > 0:
                        P.op('dve', lambda: nc.vector.scalar_tensor_tensor(out=bs[0:nq, 3:4], in0=bs[0:nq, 3:4], scalar=2.0, in1=bs[0:nq, 5:6],
                                                                           op0=ALU.mult, op1=ALU.subtract), r=['b_acc', 'b_cnt'], w=['b_cnt'])
                        thr_ = thr2
                    else:
                        thr_ = float(topk) - 0.5
                    P.op('dve', lambda it=it, thr_=thr_: nc.vector.tensor_scalar(out=bs[0:nq, 4:5], in0=bs[0:nq, 3:4], scalar1=thr_, scalar2=wtab2[0:nq, it + 1:it + 2],
                                                                                op0=ALU.is_ge, op1=ALU.mult), r=['b_cnt', 'wtab2'], w=['b_gw'])
                    P.op('dve', lambda it=it: nc.vector.scalar_tensor_tensor(out=bs[0:nq, 2:3], in0=bs[0:nq, 2:3], scalar=wtab[0:nq, it + 1:it + 2], in1=bs[0:nq, 4:5],
                                                                             op0=ALU.subtract, op1=ALU.add), r=['b_mid', 'b_gw', 'wtab'], w=['b_mid'])
                    yield
                P.op('dve', lambda: nc.vector.tensor_tensor(out=bs[0:nq, 0:1], in0=bs[0:nq, 2:3], in1=wtab[0:nq, NIT:NIT + 1], op=ALU.subtract), r=['b_mid', 'wtab'], w=['b_lo'])
                P.op('dve', lambda: nc.vector.tensor_scalar(out=ns[0:nq, 0:nkp], in0=score[0:nq, 0:nkp], scalar1=bs[0:nq, 0:1], scalar2=-30000.0,
                                                            op0=ALU.is_lt, op1=ALU.mult), r=['score', 'b_lo', (nsk, 'a'), (nsk, 'b')], w=[nsk, (nsk, 'a'), (nsk, 'b')])

            def attend(qt):
                nk, nkt, nkp = geom(qt)
                ns = negsel[qt % 2]; nsk = ('negsel', qt % 2)
                W4 = 4 * nq
                ob = 6
                qflat = QT[0:64, qt, :, :].rearrange("p h q -> p (h q)")
                tl = [dict(j=j) for j in range(nkt)]

                def sA(t):
                    j = t['j']
                    zb = rr([4, 5], 'zbd')
                    t['zb'] = zb
                    mm(ps[:, zb, 0:W4], KT[0:64, 0, j * 128:(j + 1) * 128], qflat, True, False, r=['KT', 'QT'], w=[('ps', zb)], acc=True)
                    mm(ps[:, zb, 0:W4], ns[0:nq, j * 128:(j + 1) * 128], I4[0:nq, 0:W4], False, True, r=[nsk, 'I4'], w=[('ps', zb)], acc=True)
                    pt = rr(Pt, 'Ptd'); pk = ('Ptd', id(pt))
                    t['pt'] = (pt, pk)
                    P.op('act', lambda: nc.scalar.activation(out=pt[:, 0:W4], in_=ps[:, zb, 0:W4], func=AF.Exp), r=[('ps', zb)], w=[pk])

                def sC(t):
                    j = t['j']
                    pt, pk = t['pt']
                    mm(ps[0:65, ob, 0:W4], Vd[:, j, 0:65], pt[:, 0:W4], j == 0, j == nkt - 1, r=['V', pk], w=[('ps', ob)], acc=True)
                for s_ in range(nkt + 1):
                    if s_ < nkt:
                        sA(tl[s_])
                    if s_ >= 1:
                        sC(tl[s_ - 1])
                    yield
                copy_op('act', srow[64:65, 0:W4], ps[64:65, ob, 0:W4], r=[('ps', ob)], w=['srow'])
                P.op('pe', lambda: nc.tensor.matmul(ps[0:64, 7, 0:W4], lhsT=mats[64:65, 4, 0:64], rhs=srow[64:65, 0:W4], start=True, stop=True),
                     r=['srow', 'mats'], w=[('ps', 7)])
                P.op('dve', lambda: nc.vector.reciprocal(out=rbc[0:64, 0:W4], in_=ps[0:64, 7, 0:W4]), r=[('ps', 7)], w=['rbc'])
                qi = qt // tpb
                yb_ = ybufs[qi % 2]; yk = ('ybuf', qi % 2)
                yb = yb_[0:64, :, (qt % tpb) * nq:(qt % tpb + 1) * nq]
                P.op('dve', lambda: nc.vector.tensor_tensor(
                    out=yb, in0=ps[0:64, ob, 0:W4].rearrange("p (h q) -> p h q", h=4), in1=rbc[0:64, 0:W4].rearrange("p (h q) -> p h q", h=4), op=ALU.mult),
                    r=[('ps', ob), 'rbc'], w=[yk])
                if qt % tpb == tpb - 1:
                    q0 = qi * QB
                    P.dma('sp', lambda: nc.sync.dma_start(out=ysc[3, :, :, q0:q0 + QB], in_=yb_[0:64, :, :]), r=[yk], w=[('ysc', 3, qi)])
                    if debug:
                        P.dma('sp', lambda: nc.sync.dma_start(out=ydbg[0 if g == 'p' else 1, l, 3, :, :, q0:q0 + QB], in_=yb_[0:64, :, :]), r=[yk], w=[('ydbg', 3, qi, l, g)])

            def drive(*gens):
                gens = [g_ for g_ in gens if g_ is not None]
                while gens:
                    for g_ in list(gens):
                        try:
                            next(g_)
                        except StopIteration:
                            gens.remove(g_)

            drive(prep(0))
            for qt in range(NQT):
                drive(prep(qt + 1) if qt + 1 < NQT else None, attend(qt))

        def ln_fm(rbuf, xcb, W, l, which, out32, out16, key, lnsd):
            for kc in range(KC):
                mm(ps[:, 4, 0:W], ones_f, rbuf[:, kc, 0:W], kc == 0, kc == KC - 1, r=[key + 'r', 'mats'], w=[('ps', 4)], acc=True)
            P.op('dve', lambda: nc.vector.scalar_tensor_tensor(out=xcb[:, :, 0:W], in0=ps[:, 4, 0:W].unsqueeze(1).to_broadcast([128, KC, W]), scalar=-1.0 / DM,
                                                               in1=rbuf[:, :, 0:W], op0=ALU.mult, op1=ALU.add), r=[('ps', 4), key + 'r'], w=[key + 'xc'])
            P.op('act', lambda: nc.scalar.activation(out=rbuf[:, :, 0:W], in_=xcb[:, :, 0:W], func=AF.Square), r=[key + 'xc'], w=[key + 'r'])
            for kc in range(KC):
                mm(ps[:, 4, 0:W], ones_f, rbuf[:, kc, 0:W], kc == 0, kc == KC - 1, r=[key + 'r', 'mats'], w=[('ps', 4)], acc=True)
            P.op('act', lambda: nc.scalar.activation(out=lnsd[:, 0:W], in_=ps[:, 4, 0:W], func=AF.Sqrt, scale=1.0 / DM, bias=eps_t[:, 0:1]),
                 r=[('ps', 4), 'eps'], w=['lnsd'])
            P.op('dve', lambda: nc.vector.reciprocal(out=lnsd[:, 0:W], in_=lnsd[:, 0:W]), r=['lnsd'], w=['lnsd'])
            P.op('dve', lambda: nc.vector.tensor_tensor(out=xcb[:, :, 0:W], in0=xcb[:, :, 0:W], in1=lnsd[:, 0:W].unsqueeze(1).to_broadcast([128, KC, W]), op=ALU.mult),
                 r=[key + 'xc', 'lnsd'], w=[key + 'xc'])
            for kc in range(KC):
                e = rr(['pool', 'dve'], 'lnaff')
                engo = nc.gpsimd if e == 'pool' else nc.vector
                P.op(e, lambda kc=kc, engo=engo: engo.tensor_scalar(out=out32[:, kc, 0:W], in0=xcb[:, kc, 0:W], scalar1=lnp[:, l, which, kc:kc + 1],
                                                                     scalar2=lnp[:, l, which + 1, kc:kc + 1], op0=ALU.mult, op1=ALU.add),
                     r=[key + 'xc', 'lnp'], w=[key + 'o32'])
            if out16 is not None:
                copy_op('act', out16, out32[:, :, 0:W], r=[key + 'o32'], w=[key + 'o16'])

        lnsd = None
        eps_t = None

        def ln_fm2(rb, rkey, sq16, sm, W, l, which, out32, okey):
            ones_b = matb[:, 3, :]
            P.op('act', lambda: nc.scalar.activation(out=sq16[:, :, 0:W], in_=rb[:, :, 0:W], func=AF.Square), r=[rkey], w=['sq16'])
            for kc in range(KC):
                mm(ps[:, 6, 0:W], ones_f, rb[:, kc, 0:W], kc == 0, kc == KC - 1, r=[rkey, 'mats'], w=[('ps', 6)], acc=True)
            for kc in range(KC):
                mm(ps[:, 7, 0:W], ones_b, sq16[:, kc, 0:W], kc == 0, kc == KC - 1, r=['sq16', 'matb'], w=[('ps', 7)], acc=True)
            copy_op('act', sm[:, 0, 0:W], ps[:, 6, 0:W], r=[('ps', 6)], w=['sm0'], scale=1.0 / DM)
            P.op('dve', lambda: nc.vector.tensor_tensor(out=sm[:, 1, 0:W], in0=sm[:, 0, 0:W], in1=sm[:, 0, 0:W], op=ALU.mult), r=['sm0'], w=['sm1'])
            P.op('dve', lambda: nc.vector.scalar_tensor_tensor(out=sm[:, 1, 0:W], in0=ps[:, 7, 0:W], scalar=1.0 / DM, in1=sm[:, 1, 0:W],
                                                               op0=ALU.mult, op1=ALU.subtract), r=[('ps', 7), 'sm1'], w=['sm1'])
            P.op('act', lambda: nc.scalar.activation(out=sm[:, 2, 0:W], in_=sm[:, 1, 0:W], func=AF.Ln, bias=eps_t[:, 0:1]), r=['sm1', 'eps'], w=['sm2'])
            P.op('act', lambda: nc.scalar.activation(out=sm[:, 2, 0:W], in_=sm[:, 2, 0:W], func=AF.Exp, scale=-0.5), r=['sm2'], w=['sm2'])
            P.op('dve', lambda: nc.vector.tensor_tensor(out=rb[:, :, 0:W], in0=rb[:, :, 0:W], in1=sm[:, 0, 0:W].unsqueeze(1).to_broadcast([128, KC, W]), op=ALU.subtract),
                 r=[rkey, 'sm0'], w=[rkey])
            P.op('dve', lambda: nc.vector.tensor_tensor(out=rb[:, :, 0:W], in0=rb[:, :, 0:W], in1=sm[:, 2, 0:W].unsqueeze(1).to_broadcast([128, KC, W]), op=ALU.mult),
                 r=[rkey, 'sm2'], w=[rkey])
            for kc in range(KC):
                e = rr(['pool', 'dve'], 'lnaff')
                engo = nc.gpsimd if e == 'pool' else nc.vector
                P.op(e, lambda kc=kc, engo=engo: engo.tensor_scalar(out=out32[:, kc, 0:W], in0=rb[:, kc, 0:W], scalar1=lnp[:, l, which, kc:kc + 1],
                                                                     scalar2=lnp[:, l, which + 1, kc:kc + 1], op0=ALU.mult, op1=ALU.add),
                     r=[rkey, 'lnp'], w=[okey])

        def phase3(g, b, l, T, NTn, O):
            W3 = min(512, T)
            NB3 = T // W3
            W = min(256, T)
            NB = T // W
            hkeys = lambda t0, w: [('hT', jj) for jj in range(t0 // 128, max(t0 // 128 + 1, (t0 + w) // 128))]
            ar16.reset(); ar32.reset()
            Ys = [ar16.get(16, W3) for _ in range(2)]
            WGs = [[ar16.get(KC, 128) for _ in range(4)] for _ in range(2)]
            WBs = [ar16.get(16, 128) for _ in range(2)]
            mouts = [ar16.get(W3) for _ in range(3)]
            sgb = [ar32.get(W3) for _ in range(2)]
            tmpb = [ar32.get(W3) for _ in range(2)]
            maccs = [ar32.get(W3) for _ in range(2)]
            for cc in range(KC):
                wb_ = WBs[cc % 2]; wbk = ('WB', cc % 2)
                P.dma('sp', lambda wb_=wb_, cc=cc: nc.sync.dma_start(out=wb_[0:64, :, :], in_=wb_b[l, cc]), r=[('wb', l, cc)], w=[wbk])
                wgs = WGs[cc % 2]
                for br in range(4):
                    P.dma('sp', lambda wg_=wgs[br], ch=CH_GATE + br * 8 + cc: nc.sync.dma_start(out=wg_[:, :, :], in_=wf_b[l, ch]),
                          r=[('wf', l, CH_GATE + br * 8 + cc)], w=[('WG', cc % 2, br)])
                for bi in range(NB3):
                    t0 = bi * W3
                    blk = slice(t0, t0 + W3)
                    hk = hkeys(t0, W3)
                    Y = rr(Ys, 'Ys'); yk_ = ('Y', id(Y))
                    for br in range(4):
                        P.dma('sp', lambda Y=Y, br=br, blk=blk: nc.sync.dma_start(out=Y[0:64, br * 4:(br + 1) * 4, :], in_=ysc[br, :, :, blk]),
                              r=[('ysc', br, t0 // min(512, T))], w=[yk_ + (br,)])
                    macc = rr(maccs, 'maccs'); mk = ('macc', id(macc))
                    for br in range(4):
                        bB = rr([0, 1], 'bB'); bG = rr([2, 3, 4, 5], 'bG')
                        for h in range(4):
                            mm(ps[:, bB, 0:W3], wb_[0:64, br * 4 + h, :], Y[0:64, br * 4 + h, :], h == 0, h == 3, r=[wbk, yk_ + (br,)], w=[('ps', bB)], acc=True)
                        for kc in range(KC):
                            mm(ps[:, bG, 0:W3], wgs[br][:, kc, :], hT[:, kc, blk], kc == 0, kc == KC - 1, r=[('WG', cc % 2, br)] + hk, w=[('ps', bG)], acc=True)
                        sg_ = rr(sgb, 'sgb'); sgk = ('sgb', id(sg_))
                        P.op('act', lambda sg_=sg_, bG=bG: nc.scalar.activation(out=sg_[:, :], in_=ps[:, bG, 0:W3], func=AF.Sigmoid), r=[('ps', bG)], w=[sgk])
                        if br == 0:
                            P.op('dve', lambda sg_=sg_, bB=bB, macc=macc: nc.vector.tensor_tensor(out=macc[:, :], in0=ps[:, bB, 0:W3], in1=sg_[:, :], op=ALU.mult),
                                 r=[('ps', bB), sgk], w=[mk])
                        else:
                            tb_ = rr(tmpb, 'tmpb'); tk = ('tmpb', id(tb_))
                            P.op('dve', lambda sg_=sg_, bB=bB, tb_=tb_: nc.vector.tensor_tensor(out=tb_[:, :], in0=ps[:, bB, 0:W3], in1=sg_[:, :], op=ALU.mult),
                                 r=[('ps', bB), sgk], w=[tk])
                            if br < 3:
                                P.op('pool', lambda macc=macc, tb_=tb_: nc.gpsimd.tensor_tensor(out=macc[:, :], in0=macc[:, :], in1=tb_[:, :], op=ALU.add), r=[tk, mk], w=[mk])
                            else:
                                mo = rr(mouts, 'mouts'); mok = ('mout', id(mo))
                                P.op('pool', lambda macc=macc, tb_=tb_, mo=mo: nc.gpsimd.tensor_tensor(out=mo[:, :], in0=macc[:, :], in1=tb_[:, :], op=ALU.add),
                                     r=[tk, mk], w=[mok])
                                P.dma('pool', lambda mo=mo, cc=cc, blk=blk: nc.gpsimd.dma_start(out=msc[cc * 128:(cc + 1) * 128, blk], in_=mo[:, :]), r=[mok], w=[('msc', cc, bi)])
            P.barrier()
            ar16.reset(); ar32.reset()
            WOa = ar16.get(KC * KC, 128)
            for oc in range(KC):
                P.dma('sp', lambda oc=oc: nc.sync.dma_start(out=WOa[:, oc * KC:(oc + 1) * KC, :], in_=wf_b[l, CH_OUT + oc]), r=[('wf', l, CH_OUT + oc)], w=[('WO', oc)])
            mblk = [ar16.get(KC, W) for _ in range(2)]
            sq16 = ar16.get(KC, W)
            rbufs = [ar32.get(KC, W) for _ in range(2)]
            h1 = ar32.get(KC, W)
            lnsm = ar32.get(4, W)
            for bi in range(NB):
                t0 = bi * W
                blk = slice(t0, t0 + W)
                hk = hkeys(t0, W)
                mb = rr(mblk, 'mblk'); mbk = ('mblk', id(mb))
                P.dma('sp', lambda mb=mb, blk=blk: nc.sync.dma_start(out=mb[:, :, :], in_=msc[:, blk].rearrange("(c p) t -> p c t", p=128)), w=[mbk])
                rbuf = rbufs[bi % 2]; rkey = 'L1r%d' % (bi % 2)
                for oc in range(KC):
                    bank = rr([0, 1], 'bO')
                    for kc in range(KC):
                        mm(ps[:, bank, 0:W], WOa[:, oc * KC + kc, :], mb[:, kc, :], kc == 0, kc == KC - 1, r=[('WO', oc), mbk], w=[('ps', bank)], acc=True)
                    P.op('dve', lambda oc=oc, bank=bank, blk=blk, rbuf=rbuf: nc.vector.scalar_tensor_tensor(out=rbuf[:, oc, :], in0=hT[:, oc, blk], scalar=float(ALPHA), in1=ps[:, bank, 0:W],
                                                                                                 op0=ALU.mult, op1=ALU.add), r=[('ps', bank)] + hk, w=[rkey])
                ln_fm2(rbuf, rkey, sq16, lnsm, W, l, 0, h1, 'L1o32')
                copy_op('act', hT[:, :, blk], h1[:, :, :], r=['L1o32'], w=hk)
                P.dma('pool', lambda blk=blk: nc.gpsimd.dma_start(out=h1sc[:, blk].rearrange("(c p) t -> p c t", p=128), in_=h1[:, :, :]), r=['L1o32'], w=[('h1sc', bi)])
            P.barrier()
            ar16.reset(); ar32.reset()
            WA = [ar16.get(KC, 128) for _ in range(2)]
            WU = [ar16.get(KC, 128) for _ in range(2)]
            gout = [ar16.get(W3) for _ in range(3)]
            sgb = [ar32.get(W3) for _ in range(3)]
            for fc in range(NFC):
                wa_ = rr(WA, 'WA'); wak = ('WA', id(wa_))
                wu_ = rr(WU, 'WU'); wuk = ('WU', id(wu_))
                P.dma('sp', lambda wa_=wa_, fc=fc: nc.sync.dma_start(out=wa_[:, :, :], in_=wf_b[l, CH_FA + fc]), r=[('wf', l, CH_FA + fc)], w=[wak])
                P.dma('sp', lambda wu_=wu_, fc=fc: nc.sync.dma_start(out=wu_[:, :, :], in_=wf_b[l, CH_FU + fc]), r=[('wf', l, CH_FU + fc)], w=[wuk])
                for bi in range(NB3):
                    t0 = bi * W3
                    blk = slice(t0, t0 + W3)
                    hk = hkeys(t0, W3)
                    bA = rr([0, 1, 2], 'bA'); bU = rr([3, 4, 5], 'bU')
                    for kc in range(KC):
                        mm(ps[:, bA, 0:W3], wa_[:, kc, :], hT[:, kc, blk], kc == 0, kc == KC - 1, r=[wak] + hk, w=[('ps', bA)], acc=True)
                    for kc in range(KC):
                        mm(ps[:, bU, 0:W3], wu_[:, kc, :], hT[:, kc, blk], kc == 0, kc == KC - 1, r=[wuk] + hk, w=[('ps', bU)], acc=True)
                    sg_ = rr(sgb, 'sgb3'); sgk = ('sgb3', id(sg_))
                    P.op('act', lambda sg_=sg_, bA=bA: nc.scalar.activation(out=sg_[:, :], in_=ps[:, bA, 0:W3], func=AF.Silu), r=[('ps', bA)], w=[sgk])
                    go = rr(gout, 'gout'); gok = ('gout', id(go))
                    P.op('dve', lambda sg_=sg_, bU=bU, go=go: nc.vector.tensor_tensor(out=go[:, :], in0=ps[:, bU, 0:W3], in1=sg_[:, :], op=ALU.mult),
                         r=[('ps', bU), sgk], w=[gok])
                    P.dma('pool', lambda go=go, fc=fc, blk=blk: nc.gpsimd.dma_start(out=gsc[fc * 128:(fc + 1) * 128, blk], in_=go[:, :]), r=[gok], w=[('gsc', fc, bi)])
            P.barrier()
            ar16.reset(); ar32.reset()
            W2a = ar16.get(KC * NFC, 128)
            for oc in range(KC):
                P.dma('sp', lambda oc=oc: nc.sync.dma_start(out=W2a[:, oc * NFC:(oc + 1) * NFC, :], in_=w2_b[l, oc]), r=[('w2', l, oc)], w=[('W2', oc)])
            gblk = [ar16.get(NFC, W) for _ in range(2)]
            sq16 = ar16.get(KC, W)
            rbufs = [ar32.get(KC, W) for _ in range(2)]
            h1 = ar32.get(KC, W)
            lnsm = ar32.get(4, W)
            ystage = ar16.get(2048)[:, 0:2048].bitcast(F32)
            for bi in range(NB):
                t0 = bi * W
                blk = slice(t0, t0 + W)
                hk = hkeys(t0, W)
                gb = rr(gblk, 'gblk'); gbk = ('gblk', id(gb))
                P.dma('sp', lambda gb=gb, blk=blk: nc.sync.dma_start(out=gb[:, :, :], in_=gsc[:, blk].rearrange("(c p) t -> p c t", p=128)), w=[gbk])
                rbuf = rbufs[bi % 2]; rkey = 'L2r%d' % (bi % 2)
                P.dma('sp', lambda blk=blk, rbuf=rbuf: nc.sync.dma_start(out=rbuf[:, :, :], in_=h1sc[:, blk].rearrange("(c p) t -> p c t", p=128)), w=[rkey])
                for oc in range(KC):
                    bank = rr([0, 1], 'bO')
                    for fc in range(NFC):
                        mm(ps[:, bank, 0:W], W2a[:, oc * NFC + fc, :], gb[:, fc, :], fc == 0, fc == NFC - 1, r=[('W2', oc), gbk], w=[('ps', bank)], acc=True)
                    P.op('dve', lambda oc=oc, bank=bank, rbuf=rbuf: nc.vector.scalar_tensor_tensor(out=rbuf[:, oc, :], in0=rbuf[:, oc, :], scalar=float(ALPHA), in1=ps[:, bank, 0:W],
                                                                                                   op0=ALU.mult, op1=ALU.add), r=[('ps', bank), rkey], w=[rkey])
                ln_fm2(rbuf, rkey, sq16, lnsm, W, l, 2, h1, 'L2o32')
                if l == 0:
                    copy_op('act', hT[:, :, blk], h1[:, :, :], r=['L2o32'], w=hk)
                else:
                    for tt in range((W + 127) // 128):
                        rows = min(128, W - tt * 128)
                        for half in range(2):
                            bank = rr([2, 3], 'tb3')
                            for q in range(4):
                                kc = half * 4 + q
                                P.op('pe', lambda kc=kc, tt=tt, rows=rows, bank=bank, q=q: nc.tensor.transpose(
                                    out=ps[0:rows, bank, q * 128:(q + 1) * 128], in_=h1[:, kc, tt * 128:tt * 128 + rows], identity=ident_f),
                                    r=['L2o32', 'mats'], w=[('ps', bank)], acc=True)
                            copy_op(evac_eng(), ystage[0:rows, half * 512:(half + 1) * 512], ps[0:rows, bank, 0:512], r=[('ps', bank)], w=['ystage'])
                        P.dma('pool', lambda tt=tt, rows=rows, t0=t0: nc.gpsimd.dma_start(out=O['y'][b, t0 + tt * 128:t0 + tt * 128 + rows, :], in_=ystage[0:rows, :]),
                              r=['ystage'], w=[('o_y', bi, tt)])

        eps_t = es.enter_context(nc.sbuf_tensor("eps_t", [128, 1], F32))
        P.op('pool', lambda: nc.gpsimd.memset(eps_t[:], EPS), w=['eps'])
        P.op('pool', lambda: nc.gpsimd.memset(IW[:], 0.0), w=['IW'])
        P.op('pool', lambda: nc.gpsimd.memset(IWs[:], 1.0), w=['IWs'])
        P.op('pool', lambda: nc.gpsimd.memset(CS[:], 0.0), w=['CS'])
        P.barrier()
        for g, B in (('p', BP), ('s', BS)):
            for b in range(B):
                process_seq(g, b)
        P.barrier()
        print("ops:", P.nops, {e: len(v) for e, v in P.ops.items()})
        P.emit(nc, sems, block)
    return nc


FULL_CFG = dict(NC=8, BP=2, BS=2, TP=4096, TS=32, PAST=2048)
_CACHE = {}


def run_cfg(inp, cfg, debug=False):
    NCc, BP, BS, TP, TS, PAST = cfg['NC'], cfg['BP'], cfg['BS'], cfg['TP'], cfg['TS'], cfg['PAST']
    f = lambda a: np.ascontiguousarray(np.asarray(a), dtype=np.float32)
    wd = prep_weights(f(inp['w_in']), f(inp['w_branch']), f(inp['w_out']), f(inp['w_ffn_in']), f(inp['w_ffn_out']),
                      f(inp['ln1_g']), f(inp['ln1_b']), f(inp['ln2_g']), f(inp['ln2_b']), f(inp['b_forget']))
    masks, mats = const_tables()
    rfp, rtp = rot_tables(0, TP)
    rfs, rts = rot_tables(PAST, TS)
    key = (tuple(sorted(cfg.items())), debug)
    if key not in _CACHE:
        _CACHE[key] = build_program(cfg, debug)
    nc = _CACHE[key]
    xp, xs = f(inp['x_prompt']), f(inp['x_sample'])
    in_maps = []
    for c in range(NCc):
        sp = slice(c * BP, (c + 1) * BP)
        ss = slice(c * BS, (c + 1) * BS)
        m = dict(xp=xp[sp], xs=xs[ss],
                 c_sbk=f(inp['cache_sb_k'])[:, ss].reshape(2, BS, PAST, 256), c_sbv=f(inp['cache_sb_v'])[:, ss].reshape(2, BS, PAST, 256),
                 c_fk=f(inp['cache_fox_k'])[:, ss].reshape(2, BS, PAST, 256), c_fv=f(inp['cache_fox_v'])[:, ss].reshape(2, BS, PAST, 256),
                 c_flf=f(inp['cache_fox_logf'])[:, ss], c_dk=f(inp['cache_dsa_k'])[:, ss], c_dv=f(inp['cache_dsa_v'])[:, ss],
                 c_dki=f(inp['cache_dsa_kidx'])[:, ss], st_ret=f(inp['state_ret'])[:, ss],
                 wtm=wd['wtm'], wf=wd['wf'], w2=wd['w2'], wb=wd['wb'], lnp=wd['lnp'], bfor=wd['bfor'],
                 masks=masks, mats=mats, rotfm_p=rfp, rotfm_s=rfs, rottm_p=rtp, rottm_s=rts)
        in_maps.append({k: np.ascontiguousarray(v) for k, v in m.items()})
    res = run_bass_kernel_spmd(nc, in_maps, core_ids=list(range(NCc)))
    R = res.results

    def cat(name, axis):
        return np.concatenate([np.asarray(R[c][name], dtype=np.float32) for c in range(NCc)], axis=axis)
    out = []
    out.append(cat('y_p', 0))
    out.append(cat('y_s', 0))
    for g, T in (('p', TP), ('s', TS)):
        Bt = (BP if g == 'p' else BS) * NCc
        out.append(cat('sb_k_' + g, 1).reshape(2, Bt, T, 4, 64))
        out.append(cat('sb_v_' + g, 1).reshape(2, Bt, T, 4, 64))
        out.append(cat('ret_' + g, 1))
        out.append(cat('fox_k_' + g, 1).reshape(2, Bt, T, 4, 64))
        out.append(cat('fox_v_' + g, 1).reshape(2, Bt, T, 4, 64))
        out.append(cat('fox_lf_' + g, 1))
        out.append(cat('dsa_k_' + g, 1))
        out.append(cat('dsa_v_' + g, 1))
        out.append(cat('dsa_ki_' + g, 1))
    if debug:
        return tuple(out), [np.asarray(R[c]['ydbg']) for c in range(NCc)]
    return tuple(out)


def kernel(**inputs):
    return run_cfg(inputs, FULL_CFG)
```

```python
import math
import numpy as np
import ml_dtypes
import concourse.bass as bass
import concourse.mybir as mybir
from concourse.bass_utils import run_bass_kernel_spmd

F32 = mybir.dt.float32
BF16 = mybir.dt.bfloat16
AF = mybir.ActivationFunctionType
ALU = mybir.AluOpType

DM = 1024
KC = 8
FFN = 2816
NFC = 22
ALPHA = 4 ** 0.25
EPS = 1e-5
NEG = -1.0e30
GAMMA = [1.0 - 2.0 ** (-5.0 - h) for h in range(4)]

OFF = {}
_o = 0
for _n, _w in (('sb_q', 256), ('sb_k', 256), ('sb_v', 256), ('ret_q', 256), ('ret_k', 256), ('ret_v', 256),
               ('ret_g', 256), ('fox_q', 256), ('fox_k', 256), ('fox_v', 256), ('fox_f', 4), ('dsa_q', 256),
               ('dsa_k', 64), ('dsa_v', 64), ('idx_q', 256), ('idx_k', 64), ('idx_w', 4), ('merge_gate', 4096)):
    OFF[_n] = (_o, _w)
    _o += _w
IN_WIDTH = _o

TMG = [(0, 512), (512, 512), (1024, 200), (1224, 768)]
NTM = 1992
CH_SB = 0
CH_FOX = 4
CH_RET = 8
CH_DSA = 18
CH_GATE = 23
CH_OUT = 55
CH_FA = 63
CH_FU = 85
NCH = 107


def _cols(name):
    o, w = OFF[name]
    return np.arange(o, o + w)


def _swap_cols(name):
    o, w = OFF[name]
    idx = np.arange(w).reshape(4, 2, 32)[:, ::-1, :].reshape(-1)
    return o + idx


def prep_weights(w_in, w_branch, w_out, w_ffn_in, w_ffn_out, ln1_g, ln1_b, ln2_g, ln2_b, b_forget):
    L = w_in.shape[0]
    tm_cols = np.concatenate([_cols('sb_k'), _cols('sb_v'), _cols('fox_k'), _cols('fox_v'), _cols('dsa_k'),
                              _cols('dsa_v'), _cols('idx_k'), _cols('fox_f'), _cols('idx_w'), _cols('ret_v'),
                              _cols('ret_k'), _swap_cols('ret_k')])
    assert tm_cols.size == NTM
    wtm = w_in[:, :, tm_cols].reshape(L, KC, 128, NTM).transpose(0, 2, 1, 3)
    chunks = []

    def add(cols):
        assert cols.size == 128
        chunks.append(cols)
    sq, sk = _cols('sb_q'), _cols('sb_k')
    add(sq[:128]); add(sq[128:]); add(sk[:128]); add(sk[128:])
    fq, fk = _cols('fox_q'), _cols('fox_k')
    add(fq[:128]); add(fq[128:]); add(fk[:128]); add(fk[128:])
    for nm in ('ret_q', 'ret_k'):
        a, b = _cols(nm), _swap_cols(nm)
        add(a[:128]); add(a[128:]); add(b[:128]); add(b[128:])
    g = _cols('ret_g')
    add(g[:128]); add(g[128:])
    dq, iq = _cols('dsa_q'), _cols('idx_q')
    for h in range(4):
        add(np.concatenate([dq[h * 64:(h + 1) * 64], iq[h * 64:(h + 1) * 64]]))
    add(np.concatenate([_cols('dsa_k'), _cols('idx_k')]))
    mg = _cols('merge_gate')
    for i in range(32):
        add(mg[i * 128:(i + 1) * 128])
    assert len(chunks) == CH_OUT
    win_ch = np.stack([w_in[:, :, c] for c in chunks], axis=1)
    wo_ch = w_out.reshape(L, DM, 8, 128).transpose(0, 2, 1, 3)
    wf_ch = w_ffn_in.reshape(L, DM, 44, 128).transpose(0, 2, 1, 3)
    allch = np.concatenate([win_ch, wo_ch, wf_ch], axis=1)
    assert allch.shape[1] == NCH
    wf = allch.reshape(L, NCH, KC, 128, 128).transpose(0, 1, 3, 2, 4)
    w2 = w_ffn_out.reshape(L, NFC, 128, 8, 128).transpose(0, 3, 2, 1, 4)
    wb = w_branch.reshape(L, 4, 4, 64, 8, 128).transpose(0, 4, 3, 1, 2, 5).reshape(L, 8, 64, 16, 128)
    lnp = np.stack([ln1_g, ln1_b, ln2_g, ln2_b], axis=1).reshape(L, 4, KC, 128).transpose(3, 0, 1, 2)
    bfor = np.broadcast_to(b_forget[None], (128, L, 4))
    f = lambda a: np.ascontiguousarray(a, dtype=np.float32)
    return dict(wtm=f(wtm), wf=f(wf), w2=f(w2), wb=f(wb), lnp=f(lnp), bfor=f(bfor))


def rot_tables(P, T):
    half = 32
    inv_freq = (10000.0 ** (-np.arange(half, dtype=np.float32) / half)).astype(np.float32)
    pos = (P + np.arange(T)).astype(np.float32)
    ang = pos[:, None] * inv_freq[None, :]
    cos = np.cos(ang).astype(np.float64)
    sin = np.sin(ang).astype(np.float64)
    cosf = np.concatenate([cos, cos], 1)
    sinf = np.concatenate([-sin, sin], 1)
    QB = min(512, T)
    n = np.arange(T)
    fm = np.zeros((4, 256, T), np.float64)
    tm = np.zeros((2, T, 256), np.float64)
    for h in range(4):
        g = GAMMA[h]
        dq = g ** (n % QB).astype(np.float64)
        dk = g ** (127 - (n % 128)).astype(np.float64) * 0.125
        ds = g ** (T - 1 - n).astype(np.float64) * 0.125
        sl = slice(h * 64, (h + 1) * 64)
        fm[0, sl] = (cosf * dq[:, None]).T
        fm[1, sl] = (sinf * dq[:, None]).T
        fm[2, sl] = (cosf * dk[:, None]).T
        fm[3, sl] = (sinf * dk[:, None]).T
        tm[0, :, sl] = cosf * ds[:, None]
        tm[1, :, sl] = sinf * ds[:, None]
    fm = fm.reshape(4, 2, 128, T).transpose(0, 2, 1, 3)
    return np.ascontiguousarray(fm, np.float32), np.ascontiguousarray(tm, np.float32)


def const_tables():
    r = np.arange(128)[:, None]
    c = np.arange(512)[None, :]
    x = np.arange(896)[None, :]
    masks = np.zeros((128, 2, 896), np.float32)
    masks[:, 0, :] = (r <= x - 384)
    masks[:, 1, :] = (r < x - 384)
    j = np.arange(128)[:, None]
    s = np.arange(128)[None, :]
    mats = np.zeros((128, 9, 128), np.float32)
    mats[:, 0, :] = np.eye(128)
    mats[:, 1, :] = -1.0 * (j >= s)
    mats[:, 2, :] = -1.0 * (j < s)
    mats[:, 3, :] = (j <= s)
    mats[:, 4, :] = 1.0
    mats[:, 5, :] = (j == 127)
    mats[:, 6, :] = (j == 0)
    mats[:, 7, :] = (j == 16)
    mats[:, 8, :] = (0.5 ** (np.arange(128) + 1.0))[None, :]
    return masks.astype(ml_dtypes.bfloat16), mats


class Prog:
    CE = ('pe', 'act', 'dve', 'pool')

    def __init__(self, ndma=16, nsp=10):
        self.ops = {e: [] for e in self.CE + ('sp',)}
        self.seq = {e: 0 for e in self.CE}
        self.ndma = ndma
        self.dcnt = [0] * ndma
        self.dnext = {'sp': 0, 'pool': 0}
        self.nsp = nsp
        self.lastw = {}
        self.rd = {}
        self.known = {e: {} for e in self.CE + ('sp',)}
        self.snap = {}
        self.nops = 0

    def _collect(self, eng, reads, writes, acc):
        need = {}

        def add(ev, is_w=False):
            if ev is None:
                return
            c, v = ev
            if acc and is_w and c == eng:
                return
            if need.get(c, 0) < v:
                need[c] = v
        for r in reads:
            add(self.lastw.get(r))
        for w in writes:
            add(self.lastw.get(w), True)
            for c, v in self.rd.get(w, {}).items():
                add((c, v))
        kn = self.known[eng]
        waits = []
        for c, v in need.items():
            if kn.get(c, 0) >= v:
                continue
            waits.append((c, v))
        for c, v in waits:
            sn = self.snap.get((c, v))
            if sn:
                for c2, v2 in sn.items():
                    if kn.get(c2, 0) < v2:
                        kn[c2] = v2
            if kn.get(c, 0) < v:
                kn[c] = v
        return waits

    def _mark(self, ev, reads, writes):
        for w in writes:
            self.lastw[w] = ev
            self.rd[w] = {}
        for r in reads:
            d = self.rd.setdefault(r, {})
            if d.get(ev[0], 0) < ev[1]:
                d[ev[0]] = ev[1]

    def op(self, eng, fn, r=(), w=(), acc=False):
        isps = lambda x: isinstance(x, tuple) and x[0] in ('ps', 'psT')
        w = list(w) + [x for x in r if isps(x)]
        r = [x for x in r if not isps(x)]
        waits = self._collect(eng, r, w, acc)
        self.seq[eng] += 1
        ev = (eng, self.seq[eng])
        self.snap[ev] = dict(self.known[eng])
        self.ops[eng].append((waits, fn, eng, 1))
        self._mark(ev, r, w)
        self.nops += 1

    def dma(self, q, fn, r=(), w=()):
        if q == 'sp':
            slot = self.dnext['sp']
            self.dnext['sp'] = (slot + 1) % self.nsp
        else:
            slot = self.nsp + self.dnext['pool']
            self.dnext['pool'] = (self.dnext['pool'] + 1) % (self.ndma - self.nsp)
        clk = 'd%d' % slot
        waits = self._collect(q, r, w, False)
        if self.dcnt[slot] > 0 and self.known[q].get(clk, 0) < self.dcnt[slot]:
            waits.append((clk, self.dcnt[slot]))
            self.known[q][clk] = self.dcnt[slot]
        self.dcnt[slot] += 1
        ev = (clk, self.dcnt[slot])
        self.snap[ev] = dict(self.known[q])
        self.ops[q].append((waits, fn, clk, 16))
        self._mark(ev, r, w)
        self.nops += 1

    def barrier(self):
        allev = [(e, self.seq[e]) for e in self.CE if self.seq[e] > 0]
        allev += [('d%d' % s, self.dcnt[s]) for s in range(self.ndma) if self.dcnt[s] > 0]
        for e in self.CE + ('sp',):
            waits = [(c, v) for c, v in allev if c != e and self.known[e].get(c, 0) < v]
            for c, v in waits:
                self.known[e][c] = v
            if waits:
                self.ops[e].append((waits, None, None, 0))
        self.lastw = {}
        self.rd = {}

    def emit(self, nc, sems, block):
        engs = {'pe': (block.tensor, nc.tensor), 'act': (block.scalar, nc.scalar), 'dve': (block.vector, nc.vector),
                'pool': (block.gpsimd, nc.gpsimd), 'sp': (block.sync, nc.sync)}
        for e, (dec, engobj) in engs.items():
            ops = self.ops[e]

            def body(_eng, ops=ops, engobj=engobj):
                for waits, fn, clk, inc in ops:
                    for c, v in waits:
                        engobj.wait_ge(sems[c], v * (16 if c[0] == 'd' and c[1:].isdigit() else 1))
                    if fn is not None:
                        fn().then_inc(sems[clk], inc)
            dec(body)


def build_program(cfg, debug=False):
    BP, BS, TP, TS, PAST = cfg['BP'], cfg['BS'], cfg['TP'], cfg['TS'], cfg['PAST']
    nc = bass.Bass("TRN2", target_bir_lowering=False)
    P = Prog()

    def din(name, shape, dt=F32):
        return nc.dram_tensor(name, list(shape), dt, kind="ExternalInput").ap()

    def dout(name, shape, dt=F32):
        return nc.dram_tensor(name, list(shape), dt, kind="ExternalOutput").ap()

    def dscr(name, shape, dt, dbg=False):
        return nc.dram_tensor(name, list(shape), dt, kind=("ExternalOutput" if (dbg and debug) else "Internal")).ap()

    x_in = {'p': din('xp', [BP, TP, DM]), 's': din('xs', [BS, TS, DM])}
    c_sbk = din('c_sbk', [2, BS, PAST, 256]); c_sbv = din('c_sbv', [2, BS, PAST, 256])
    c_fk = din('c_fk', [2, BS, PAST, 256]); c_fv = din('c_fv', [2, BS, PAST, 256])
    c_flf = din('c_flf', [2, BS, PAST, 4])
    c_dk = din('c_dk', [2, BS, PAST, 64]); c_dv = din('c_dv', [2, BS, PAST, 64]); c_dki = din('c_dki', [2, BS, PAST, 64])
    st_ret = din('st_ret', [2, BS, 4, 64, 64])
    wtm_f = din('wtm', [2, 128, KC, NTM]); wf_f = din('wf', [2, NCH, 128, KC, 128])
    w2_f = din('w2', [2, 8, 128, NFC, 128]); wb_f = din('wb', [2, 8, 64, 16, 128])
    lnp_d = din('lnp', [128, 2, 4, 8]); bfor_d = din('bfor', [128, 2, 4])
    masks_d = din('masks', [128, 2, 896], BF16); mats_d = din('mats', [128, 9, 128])
    rotfm = {'p': din('rotfm_p', [4, 128, 2, TP]), 's': din('rotfm_s', [4, 128, 2, TS])}
    rottm = {'p': din('rottm_p', [2, TP, 256]), 's': din('rottm_s', [2, TS, 256])}

    outs = {}
    for g, B, T in (('p', BP, TP), ('s', BS, TS)):
        outs[g] = dict(
            y=dout('y_' + g, [B, T, DM]),
            sb_k=dout('sb_k_' + g, [2, B, T, 256]), sb_v=dout('sb_v_' + g, [2, B, T, 256]),
            ret=dout('ret_' + g, [2, B, 4, 64, 64]),
            fox_k=dout('fox_k_' + g, [2, B, T, 256]), fox_v=dout('fox_v_' + g, [2, B, T, 256]),
            fox_lf=dout('fox_lf_' + g, [2, B, T, 4]),
            dsa_k=dout('dsa_k_' + g, [2, B, T, 64]), dsa_v=dout('dsa_v_' + g, [2, B, T, 64]),
            dsa_ki=dout('dsa_ki_' + g, [2, B, T, 64]))
    TMAX = max(TP, TS)
    wtm_b = dscr('wtm_b', [2, 128, KC, NTM], BF16); wf_b = dscr('wf_b', [2, NCH, 128, KC, 128], BF16)
    w2_b = dscr('w2_b', [2, 8, 128, NFC, 128], BF16); wb_b = dscr('wb_b', [2, 8, 64, 16, 128], BF16)
    retv_s = dscr('retv_s', [TMAX, 256], F32)
    ysc = dscr('ysc', [4, 64, 4, TMAX], BF16)
    msc = dscr('msc', [DM, TMAX], BF16)
    h1sc = dscr('h1sc', [DM, TMAX], F32)
    gsc = dscr('gsc', [NFC * 128, TMAX], BF16)
    ydbg = dscr('ydbg', [2, 2, 4, 64, 4, TMAX], BF16, dbg=True) if debug else None

    NTMAX = (max(TP, PAST + TS) + 127) // 128
    A16 = 40960
    A32 = 7424
    import contextlib
    es = contextlib.ExitStack()
    with es:
        hT = es.enter_context(nc.sbuf_tensor("hT", [128, KC, TMAX], BF16))
        a16 = es.enter_context(nc.sbuf_tensor("a16", [128, A16], BF16))
        a32 = es.enter_context(nc.sbuf_tensor("a32", [128, A32], F32))
        masks = es.enter_context(nc.sbuf_tensor("masks_sb", [128, 2, 896], BF16))
        mats = es.enter_context(nc.sbuf_tensor("mats_sb", [128, 9, 128], F32))
        matb = es.enter_context(nc.sbuf_tensor("matb", [128, 4, 128], BF16))
        lnp = es.enter_context(nc.sbuf_tensor("lnp_sb", [128, 2, 4, 8], F32))
        bfor = es.enter_context(nc.sbuf_tensor("bfor_sb", [128, 2, 4], F32))
        LF = es.enter_context(nc.sbuf_tensor("LF", [128, NTMAX, 4], F32))
        CS = es.enter_context(nc.sbuf_tensor("CS", [128, NTMAX, 4], F32))
        IW = es.enter_context(nc.sbuf_tensor("IW", [128, 32, 4], F32))
        IWs = es.enter_context(nc.sbuf_tensor("IWs", [128, 32, 4], F32))
        ps = es.enter_context(nc.psum_tensor("ps", [128, 8, 512], F32))
        sems = {}
        for e in Prog.CE:
            sems[e] = es.enter_context(nc.semaphore("s_" + e))
        for s in range(P.ndma):
            sems['d%d' % s] = es.enter_context(nc.semaphore("s_d%d" % s))
        block = es.enter_context(nc.Block())

        ident_f = mats[:, 0, :]
        tri = mats[:, 3, :]
        ones_f = mats[:, 4, :]
        ident_b = matb[:, 0, :]
        negU = matb[:, 1, :]
        negL = matb[:, 2, :]

        class Arena:
            def __init__(self, t, size):
                self.t, self.size, self.o = t, size, 0

            def reset(self):
                self.o = 0

            def get(self, *shape):
                n = int(np.prod(shape))
                n = (n + 15) // 16 * 16
                assert self.o + n <= self.size, ("arena overflow", self.o, n, self.size)
                v = self.t[:, self.o:self.o + int(np.prod(shape))]
                self.o += n
                if len(shape) == 2:
                    return v.rearrange("p (a b) -> p a b", a=shape[0])
                if len(shape) == 3:
                    return v.rearrange("p (a b c) -> p a b c", a=shape[0], b=shape[1])
                return v
        ar16 = Arena(a16, A16)
        ar32 = Arena(a32, A32)

        rot = {'n': 0}

        def rr(lst, key):
            i = rot.get(key, 0)
            rot[key] = i + 1
            return lst[i % len(lst)]

        def evac_eng(key='ev'):
            return rr(['act', 'dve'], key)

        def copy_op(eng, out, in_, r, w, scale=None):
            if eng == 'act':
                if scale is None:
                    P.op('act', lambda: nc.scalar.activation(out=out, in_=in_, func=AF.Copy), r=r, w=w)
                else:
                    P.op('act', lambda: nc.scalar.activation(out=out, in_=in_, func=AF.Copy, scale=float(scale)), r=r, w=w)
            elif eng == 'dve':
                if scale is None:
                    P.op('dve', lambda: nc.vector.tensor_copy(out=out, in_=in_), r=r, w=w)
                else:
                    P.op('dve', lambda: nc.vector.tensor_scalar(out=out, in0=in_, scalar1=float(scale), scalar2=None,
                                                               op0=ALU.mult), r=r, w=w)
            else:
                if scale is None:
                    P.op('pool', lambda: nc.gpsimd.tensor_copy(out=out, in_=in_), r=r, w=w)
                else:
                    P.op('pool', lambda: nc.gpsimd.tensor_scalar(out=out, in0=in_, scalar1=float(scale), scalar2=None,
                                                                op0=ALU.mult), r=r, w=w)

        def mm(out, lhsT, rhs, start, stop, r, w, acc=False):
            P.op('pe', lambda: nc.tensor.matmul(out, lhsT=lhsT, rhs=rhs, start=start, stop=stop, skip_group_check=True), r=r, w=w, acc=acc)

        P.dma('sp', lambda: nc.sync.dma_start(out=masks[:], in_=masks_d[:, :, :]), w=['masks'])
        P.dma('sp', lambda: nc.sync.dma_start(out=mats[:], in_=mats_d[:, :, :]), w=['mats'])
        P.dma('sp', lambda: nc.sync.dma_start(out=lnp[:], in_=lnp_d[:, :, :, :]), w=['lnp'])
        P.dma('sp', lambda: nc.sync.dma_start(out=bfor[:], in_=bfor_d[:, :, :]), w=['bfor'])
        P.op('dve', lambda: nc.vector.tensor_copy(out=matb[:, 0:3, :], in_=mats[:, 0:3, :]), r=['mats'], w=['matb'])
        P.op('dve', lambda: nc.vector.tensor_copy(out=matb[:, 3, :], in_=mats[:, 4, :]), r=['mats', 'matb'], w=['matb'])
        for l in range(2):
            P.dma('pool', lambda l=l: nc.gpsimd.dma_start(out=wtm_b[l], in_=wtm_f[l]), w=[('wtm', l)])
            for c0 in range(0, NCH, 8):
                c1 = min(NCH, c0 + 8)
                P.dma('pool', lambda l=l, c0=c0, c1=c1: nc.gpsimd.dma_start(
                    out=wf_b[l, c0:c1].rearrange("c p k f -> (c p) (k f)"),
                    in_=wf_f[l, c0:c1].rearrange("c p k f -> (c p) (k f)")), w=[('wf', l, c) for c in range(c0, c1)])
            for oc in range(0, 8, 2):
                P.dma('pool', lambda l=l, oc=oc: nc.gpsimd.dma_start(
                    out=w2_b[l, oc:oc + 2].rearrange("c p k f -> (c p) (k f)"),
                    in_=w2_f[l, oc:oc + 2].rearrange("c p k f -> (c p) (k f)")), w=[('w2', l, oc), ('w2', l, oc + 1)])
            P.dma('pool', lambda l=l: nc.gpsimd.dma_start(
                out=wb_b[l].rearrange("c p k f -> (c p) (k f)"),
                in_=wb_f[l].rearrange("c p k f -> (c p) (k f)")), w=[('wb', l, c) for c in range(8)])
        P.barrier()

        def process_seq(g, b):
            T = TP if g == 'p' else TS
            PL = 0 if g == 'p' else PAST
            Ltot = PL + T
            NT = (Ltot + 127) // 128
            NTn = (T + 127) // 128
            JP = PL // 128
            QB = min(512, T)
            NQB = T // QB
            nq = min(128, T)
            NQT = T // nq
            topk = min(256, Ltot // 4)
            assert topk % 8 == 0 and T % 32 == 0 and PL % 128 == 0
            O = outs[g]
            rows_of = lambda j: min(128, T - 128 * j)

            ar16.reset(); ar32.reset()
            xst = [ar32.get(1024) for _ in range(2)]
            for j in range(NTn):
                rows = rows_of(j)
                xs_ = rr(xst, 'xst')
                xk = ('xst', id(xs_))
                P.dma('sp', lambda xs_=xs_, j=j, rows=rows: nc.sync.dma_start(
                    out=xs_[:rows, :], in_=x_in[g][b, j * 128:j * 128 + rows, :]), w=[xk])
                for half in range(2):
                    bank = rr([0, 1], 'tb')
                    for q in range(4):
                        kc = half * 4 + q
                        P.op('pe', lambda xs_=xs_, kc=kc, rows=rows, bank=bank, q=q: nc.tensor.transpose(
                            out=ps[:, bank, q * 128:q * 128 + rows], in_=xs_[:rows, kc * 128:(kc + 1) * 128],
                            identity=ident_f[:rows, :rows]), r=[xk, 'mats'], w=[('ps', bank)], acc=True)
                    src = ps[:, bank, :].rearrange("p (q r) -> p q r", q=4)[:, :, 0:rows]
                    dst = hT[:, half * 4:half * 4 + 4, j * 128:j * 128 + rows]
                    copy_op(evac_eng(), dst, src, r=[('ps', bank)], w=[('hT', j)])
            P.barrier()

            for l in range(2):
                layer(g, b, l, T, PL, Ltot, NT, NTn, JP, QB, NQB, nq, NQT, topk, O, rows_of)

        def layer(g, b, l, T, PL, Ltot, NT, NTn, JP, QB, NQB, nq, NQT, topk, O, rows_of):
            hT_all = [('hT', j) for j in range(NTn)]
            ar16.reset(); ar32.reset()
            wtm = [ar16.get(KC, 512) for _ in range(2)]
            wret = ar16.get(KC, 768)
            stg = [ar32.get(512) for _ in range(3)]
            rtm = [ar32.get(2, 256) for _ in range(2)]
            t1 = ar32.get(256); t2 = ar32.get(256)
            small = ar32.get(64)
            vb = [ar16.get(256) for _ in range(2)]
            kd = [ar16.get(256) for _ in range(2)]
            s0t = ar32.get(4, 64)
            stout = ar32.get(256)
            P.op('pool', lambda: nc.gpsimd.memset(LF[:], 0.0), w=['LF'])
            if PL > 0:
                P.dma('sp', lambda: nc.sync.dma_start(out=LF[:, 0:JP, :],
                                                      in_=c_flf[l, b].rearrange("(j p) h -> p j h", p=128)), w=['LF'])
            for gi in range(3):
                c0, gw = TMG[gi]
                wt = rr(wtm, 'wtm')
                wk = ('wtmb', id(wt))
                P.dma('sp', lambda wt=wt, c0=c0, gw=gw: nc.sync.dma_start(out=wt[:, :, 0:gw], in_=wtm_b[l, :, :, c0:c0 + gw]),
                      r=[('wtm', l)], w=[wk])
                for j in range(NTn):
                    rows = rows_of(j)
                    bank = rr([0, 1], 'p1b')
                    for kc in range(KC):
                        mm(ps[:rows, bank, 0:gw], hT[:, kc, j * 128:j * 128 + rows], wt[:, kc, 0:gw], kc == 0, kc == KC - 1,
                           r=[('hT', j), wk], w=[('ps', bank)], acc=True)
                    st_ = rr(stg, 'stg')
                    sk_ = ('stg', id(st_))
                    copy_op(evac_eng(), st_[:rows, 0:gw], ps[:rows, bank, 0:gw], r=[('ps', bank)], w=[sk_])
                    tok = slice(j * 128, j * 128 + rows)
                    if gi == 0:
                        P.dma('sp', lambda st_=st_, tok=tok, rows=rows: nc.sync.dma_start(out=O['sb_k'][l, b, tok, :], in_=st_[:rows, 0:256]),
                              r=[sk_], w=[('o_sbk', j)])
                        P.dma('sp', lambda st_=st_, tok=tok, rows=rows: nc.sync.dma_start(out=O['sb_v'][l, b, tok, :], in_=st_[:rows, 256:512]),
                              r=[sk_], w=[('o_sbv', j)])
                    elif gi == 1:
                        P.dma('sp', lambda st_=st_, tok=tok, rows=rows: nc.sync.dma_start(out=O['fox_k'][l, b, tok, :], in_=st_[:rows, 0:256]),
                              r=[sk_], w=[('o_fk', j)])
                        P.dma('sp', lambda st_=st_, tok=tok, rows=rows: nc.sync.dma_start(out=O['fox_v'][l, b, tok, :], in_=st_[:rows, 256:512]),
                              r=[sk_], w=[('o_fv', j)])
                    else:
                        P.dma('sp', lambda st_=st_, tok=tok, rows=rows: nc.sync.dma_start(out=O['dsa_k'][l, b, tok, :], in_=st_[:rows, 0:64]),
                              r=[sk_], w=[('o_dk', j)])
                        P.dma('sp', lambda st_=st_, tok=tok, rows=rows: nc.sync.dma_start(out=O['dsa_v'][l, b, tok, :], in_=st_[:rows, 64:128]),
                              r=[sk_], w=[('o_dv', j)])
                        P.dma('sp', lambda st_=st_, tok=tok, rows=rows: nc.sync.dma_start(out=O['dsa_ki'][l, b, tok, :], in_=st_[:rows, 128:192]),
                              r=[sk_], w=[('o_dki', j)])
                        sm = small
                        P.op('dve', lambda st_=st_, rows=rows: nc.vector.tensor_tensor(out=sm[:rows, 0:4], in0=st_[:rows, 192:196],
                                                                                        in1=bfor[:rows, l, :], op=ALU.add),
                             r=[sk_, 'bfor'], w=['small'])
                        P.op('act', lambda rows=rows: nc.scalar.activation(out=sm[:rows, 4:8], in_=sm[:rows, 0:4], func=AF.Exp, scale=-1.0),
                             r=['small'], w=['small2'])
                        P.op('act', lambda rows=rows: nc.scalar.activation(out=sm[:rows, 8:12], in_=sm[:rows, 4:8], func=AF.Ln, bias=1.0),
                             r=['small2'], w=['small3'])
                        P.op('dve', lambda rows=rows, j=j: nc.vector.tensor_scalar(out=LF[:rows, JP + j, :], in0=sm[:rows, 8:12], scalar1=-1.0,
                                                                                   scalar2=None, op0=ALU.mult),
                             r=['small3'], w=['LF'])
                        P.dma('sp', lambda tok=tok, rows=rows, j=j: nc.sync.dma_start(out=O['fox_lf'][l, b, tok, :], in_=LF[:rows, JP + j, :]),
                              r=['LF'], w=[('o_flf', j)])
                        P.op('dve', lambda st_=st_, rows=rows, j=j: nc.vector.tensor_scalar(out=IWs[:rows, j, :], in0=st_[:rows, 196:200], scalar1=0.0,
                                                                                            scalar2=2.0, op0=ALU.is_ge, op1=ALU.mult), r=[sk_], w=['IWs'])
                        P.op('dve', lambda rows=rows, j=j: nc.vector.tensor_scalar(out=IWs[:rows, j, :], in0=IWs[:rows, j, :], scalar1=-1.0,
                                                                                   scalar2=None, op0=ALU.add), r=['IWs'], w=['IWs'])
                        P.op('dve', lambda st_=st_, rows=rows, j=j: nc.vector.scalar_tensor_tensor(out=IW[:rows, j, :], in0=st_[:rows, 196:200], scalar=0.5,
                                                                                                   in1=IWs[:rows, j, :], op0=ALU.mult, op1=ALU.mult),
                             r=[sk_, 'IWs'], w=['IW'])
            c0, gw = TMG[3]
            P.dma('sp', lambda: nc.sync.dma_start(out=wret[:, :, :], in_=wtm_b[l, :, :, c0:c0 + gw]), r=[('wtm', l)], w=['wret'])
            for j in range(NTn):
                rows = rows_of(j)
                for kc in range(KC):
                    mm(ps[:rows, 2, 0:512], hT[:, kc, j * 128:j * 128 + rows], wret[:, kc, 0:512], kc == 0, kc == KC - 1,
                       r=[('hT', j), 'wret'], w=[('ps', 2)], acc=True)
                for kc in range(KC):
                    mm(ps[:rows, 3, 0:256], hT[:, kc, j * 128:j * 128 + rows], wret[:, kc, 512:768], kc == 0, kc == KC - 1,
                       r=[('hT', j), 'wret'], w=[('ps', 3)], acc=True)
                rt = rr(rtm, 'rtm'); rk = ('rtm', id(rt))
                P.dma('sp', lambda rt=rt, j=j, rows=rows: nc.sync.dma_start(
                    out=rt[:rows, :, :], in_=rottm[g][:, j * 128:j * 128 + rows, :].rearrange("a t f -> t a f")), w=[rk])
                v_ = rr(vb, 'vb'); vk = ('vb', id(v_))
                st_ = rr(stg, 'stg'); sk_ = ('stg', id(st_))
                copy_op('act', st_[:rows, 0:256], ps[:rows, 2, 0:256], r=[('ps', 2)], w=[sk_])
                P.dma('sp', lambda st_=st_, j=j, rows=rows: nc.sync.dma_start(out=retv_s[j * 128:j * 128 + rows, :], in_=st_[:rows, 0:256]),
                      r=[sk_], w=[('retv', j)])
                copy_op('pool', v_[:rows, :], st_[:rows, 0:256], r=[sk_], w=[vk])
                P.op('dve', lambda rt=rt, rows=rows: nc.vector.tensor_tensor(out=t1[:rows, :], in0=ps[:rows, 2, 256:512], in1=rt[:rows, 0, :], op=ALU.mult),
                     r=[('ps', 2), rk], w=['t1'])
                P.op('dve', lambda rt=rt, rows=rows: nc.vector.tensor_tensor(out=t2[:rows, :], in0=ps[:rows, 3, 0:256], in1=rt[:rows, 1, :], op=ALU.mult),
                     r=[('ps', 3), rk], w=['t2'])
                k_ = rr(kd, 'kd'); kk = ('kd', id(k_))
                P.op('pool', lambda k_=k_, rows=rows: nc.gpsimd.tensor_tensor(out=k_[:rows, :], in0=t1[:rows, :], in1=t2[:rows, :], op=ALU.add),
                     r=['t1', 't2'], w=[kk])
                for h in range(4):
                    mm(ps[0:64, 4, h * 64:(h + 1) * 64], k_[:rows, h * 64:(h + 1) * 64], v_[:rows, h * 64:(h + 1) * 64],
                       (j == 0 and h == 0), (j == NTn - 1 and h == 3), r=[kk, vk], w=[('ps', 4)], acc=True)
            if PL > 0:
                P.dma('sp', lambda: nc.sync.dma_start(out=s0t[0:64, :, :], in_=st_ret[l, b].rearrange("h d e -> d h e")), w=['s0t'])
                for h in range(4):
                    P.op('dve', lambda h=h: nc.vector.scalar_tensor_tensor(
                        out=stout[0:64, h * 64:(h + 1) * 64], in0=s0t[0:64, h, :], scalar=float(GAMMA[h] ** T),
                        in1=ps[0:64, 4, h * 64:(h + 1) * 64], op0=ALU.mult, op1=ALU.add), r=['s0t', ('ps', 4)], w=['stout'])
            else:
                copy_op('dve', stout[0:64, :], ps[0:64, 4, 0:256], r=[('ps', 4)], w=['stout'])
            P.dma('sp', lambda: nc.sync.dma_start(out=O['ret'][l, b].rearrange("h d e -> d h e"),
                                                  in_=stout[0:64, :].rearrange("p (h e) -> p h e", h=4)), r=['stout'], w=['o_ret'])
            P.barrier()

            for br in range(4):
                branch(br, g, b, l, T, PL, Ltot, NT, NTn, JP, QB, NQB, nq, NQT, topk, O, rows_of)
                P.barrier()
            phase3(g, b, l, T, NTn, O)
            P.barrier()

        def proj_fm(l, ch, T, QB, NQB, wbuf, evac, msl=None, bank_list=(0, 1), bkey='fmb'):
            wk = ('wch', id(wbuf))
            P.dma('sp', lambda: nc.sync.dma_start(out=wbuf[:, :, :], in_=wf_b[l, ch]), r=[('wf', l, ch)], w=[wk])
            for qi in range(NQB):
                bank = rr(list(bank_list), bkey)
                M = 128 if msl is None else 64
                for kc in range(KC):
                    lw = wbuf[:, kc, :] if msl is None else wbuf[:, kc, msl]
                    mm(ps[0:M, bank, 0:QB], lw, hT[:, kc, qi * QB:(qi + 1) * QB], kc == 0, kc == KC - 1,
                       r=[wk] + [('hT', jj) for jj in range(qi * QB // 128, max(qi * QB // 128 + 1, (qi + 1) * QB // 128))],
                       w=[('ps', bank)], acc=True)
                evac(qi, bank)

        def load_past_T(l, b, src, KT_dst_fn, ncol, JP, xst):
            for j in range(JP):
                xs_ = rr(xst, 'xst2'); xk = ('xst2', id(xs_))
                if isinstance(src, tuple):
                    P.dma('sp', lambda xs_=xs_, j=j: nc.sync.dma_start(out=xs_[:, 0:64], in_=src[0][l, b, j * 128:(j + 1) * 128, :]), w=[xk])
                    P.dma('sp', lambda xs_=xs_, j=j: nc.sync.dma_start(out=xs_[:, 64:128], in_=src[1][l, b, j * 128:(j + 1) * 128, :]), w=[xk + ('b',)])
                    rk = [xk, xk + ('b',)]
                    npair = 1
                else:
                    P.dma('sp', lambda xs_=xs_, j=j: nc.sync.dma_start(out=xs_[:, 0:256], in_=src[l, b, j * 128:(j + 1) * 128, :]), w=[xk])
                    rk = [xk]
                    npair = 2
                bank = rr([0, 1], 'ptb')
                for pr in range(npair):
                    P.op('pe', lambda xs_=xs_, pr=pr, bank=bank: nc.tensor.transpose(
                        out=ps[:, bank, pr * 128:(pr + 1) * 128], in_=xs_[:, pr * 128:(pr + 1) * 128], identity=ident_f),
                        r=rk + ['mats'], w=[('ps', bank)], acc=True)
                for pr in range(npair):
                    copy_op(evac_eng(), KT_dst_fn(pr, j), ps[:, bank, pr * 128:(pr + 1) * 128], r=[('ps', bank)], w=['KT'])

        def branch(br, g, b, l, T, PL, Ltot, NT, NTn, JP, QB, NQB, nq, NQT, topk, O, rows_of):
            ar16.reset(); ar32.reset()
            Lp = NT * 128
            kd_ = {0: 0, 1: 2, 2: 1, 3: 3}[br]
            wch = [ar16.get(KC, 128) for _ in range(3)]
            ybuf = ar16.get(4, QB)
            xst = [ar32.get(256) for _ in range(2)]
            srow = ar32.get(512); rbc = ar32.get(512)
            if kd_ in (0, 1, 2):
                Pk = 0 if kd_ == 2 else PL
                NTk = NTn if kd_ == 2 else NT
                QT = ar16.get(2, T)
                KT = ar16.get(2, NTk * 128)
                V = ar16.get(NTk, 5, 65)
                Vflat = V.rearrange("p j h d -> p j (h d)")
                Pt = [ar16.get(QB) for _ in range(4)]
                P.op('pool', lambda: nc.gpsimd.memset(KT[:, :, :], 0.0), w=['KT'])
                Vall = [('V', j_) for j_ in range(NTk)]
                P.op('pool', lambda: nc.gpsimd.memset(V[:, :, :, :], 0.0), w=Vall)
                if kd_ == 1:
                    P.op('pool', lambda: nc.gpsimd.memset(V[:, :, 0:4, 64:65], 1.0), w=Vall)
                if kd_ == 0:
                    qc, kc_, vout, kcache, vcache = CH_SB, CH_SB + 2, 'sb_v', c_sbk, c_sbv
                    vkeys = [('o_sbv', j) for j in range(NTn)]
                elif kd_ == 1:
                    qc, kc_, vout, kcache, vcache = CH_FOX, CH_FOX + 2, 'fox_v', c_fk, c_fv
                    vkeys = [('o_fv', j) for j in range(NTn)]
                if kd_ in (0, 1):
                    for pr in range(2):
                        proj_fm(l, qc + pr, T, QB, NQB, rr(wch, 'wch'),
                                lambda qi, bank, pr=pr: copy_op(evac_eng(), QT[:, pr, qi * QB:(qi + 1) * QB], ps[:, bank, 0:QB],
                                                                r=[('ps', bank)], w=['QT'], scale=0.125))
                        proj_fm(l, kc_ + pr, T, QB, NQB, rr(wch, 'wch'),
                                lambda qi, bank, pr=pr: copy_op(evac_eng(), KT[:, pr, PL + qi * QB:PL + (qi + 1) * QB], ps[:, bank, 0:QB],
                                                                r=[('ps', bank)], w=['KT']))
                    if PL > 0:
                        load_past_T(l, b, kcache, lambda pr, j: KT[:, pr, j * 128:(j + 1) * 128], 256, JP, xst)
                        for j in range(JP):
                            P.dma('pool', lambda j=j: nc.gpsimd.dma_start(
                                out=V[:, j, 0:4, 0:64], in_=vcache[l, b, j * 128:(j + 1) * 128, :].rearrange("t (h d) -> t h d", h=4)), w=[('V', j)])
                    for j in range(NTn):
                        rows = rows_of(j)
                        P.dma('pool', lambda j=j, rows=rows: nc.gpsimd.dma_start(
                            out=V[0:rows, JP + j, 0:4, 0:64], in_=O[vout][l, b, j * 128:j * 128 + rows, :].rearrange("t (h d) -> t h d", h=4)),
                            r=[vkeys[j]], w=[('V', JP + j)])
                else:
                    rtab = [ar32.get(QB) for _ in range(4)]
                    tq1 = ar32.get(QB); tq2 = ar32.get(QB)
                    for which, base, dst in ((0, CH_RET, QT), (1, CH_RET + 4, KT)):
                        for pr in range(2):
                            wa = rr(wch, 'wch'); wka = ('wch', id(wa))
                            wb_ = rr(wch, 'wch'); wkb = ('wch', id(wb_))
                            P.dma('sp', lambda wa=wa, ch=base + pr: nc.sync.dma_start(out=wa[:, :, :], in_=wf_b[l, ch]), r=[('wf', l, base + pr)], w=[wka])
                            P.dma('sp', lambda wb_=wb_, ch=base + 2 + pr: nc.sync.dma_start(out=wb_[:, :, :], in_=wf_b[l, ch]), r=[('wf', l, base + 2 + pr)], w=[wkb])
                            for qi in range(NQB):
                                blk = slice(qi * QB, (qi + 1) * QB)
                                hk = [('hT', jj) for jj in range(qi * QB // 128, max(qi * QB // 128 + 1, (qi + 1) * QB // 128))]
                                bA = rr([0, 1], 'rA'); bB = rr([2, 3], 'rB')
                                for kc in range(KC):
                                    mm(ps[:, bA, 0:QB], wa[:, kc, :], hT[:, kc, blk], kc == 0, kc == KC - 1, r=[wka] + hk, w=[('ps', bA)], acc=True)
                                for kc in range(KC):
                                    mm(ps[:, bB, 0:QB], wb_[:, kc, :], hT[:, kc, blk], kc == 0, kc == KC - 1, r=[wkb] + hk, w=[('ps', bB)], acc=True)
                                tc_ = rr(rtab, 'rtab'); tck = ('rtab', id(tc_))
                                ts_ = rr(rtab, 'rtab'); tsk = ('rtab', id(ts_))
                                P.dma('sp', lambda tc_=tc_, pr=pr, blk=blk, ti=which * 2: nc.sync.dma_start(out=tc_[:, :], in_=rotfm[g][ti, :, pr, blk]), w=[tck])
                                P.dma('sp', lambda ts_=ts_, pr=pr, blk=blk, ti=which * 2 + 1: nc.sync.dma_start(out=ts_[:, :], in_=rotfm[g][ti, :, pr, blk]), w=[tsk])
                                P.op('dve', lambda tc_=tc_, bA=bA: nc.vector.tensor_tensor(out=tq1[:, :], in0=ps[:, bA, 0:QB], in1=tc_[:, :], op=ALU.mult),
                                     r=[('ps', bA), tck], w=['tq1'])
                                P.op('dve', lambda ts_=ts_, bB=bB: nc.vector.tensor_tensor(out=tq2[:, :], in0=ps[:, bB, 0:QB], in1=ts_[:, :], op=ALU.mult),
                                     r=[('ps', bB), tsk], w=['tq2'])
                                P.op('pool', lambda dst=dst, pr=pr, blk=blk: nc.gpsimd.tensor_tensor(out=dst[:, pr, blk], in0=tq1[:, :], in1=tq2[:, :], op=ALU.add),
                                     r=['tq1', 'tq2'], w=['QT' if which == 0 else 'KT'])
                    for j in range(NTn):
                        rows = rows_of(j)
                        P.dma('pool', lambda j=j, rows=rows: nc.gpsimd.dma_start(
                            out=V[0:rows, j, 0:4, 0:64], in_=retv_s[j * 128:j * 128 + rows, :].rearrange("t (h d) -> t h d", h=4)),
                            r=[('retv', j)], w=[('V', j)])
                    wg = [ar16.get(KC, 128) for _ in range(2)]
                    for pr in range(2):
                        P.dma('sp', lambda pr=pr: nc.sync.dma_start(out=wg[pr][:, :, :], in_=wf_b[l, CH_RET + 8 + pr]), r=[('wf', l, CH_RET + 8 + pr)], w=[('wg', pr)])
                    fsets = [[(ar32.get(QB), 'fA%d' % i_) for i_ in range(4)], [(rtab[i_], ('rtab', id(rtab[i_]))) for i_ in range(4)]]
                    fb16 = [(ar16.get(QB), 'fb16_%d' % i_) for i_ in range(2)]
                    if PL > 0:
                        assert NQB == 1
                        s0f = ar32.get(2, 64)
                        s0b = ar16.get(2, 64)
                        P.dma('sp', lambda: nc.sync.dma_start(out=s0f[:, :, :], in_=st_ret[l, b].rearrange("(hp hh) d e -> (hh d) hp e", hh=2)), w=['s0f'])
                        for h in range(4):
                            hp, hh = h // 2, h % 2
                            P.op('dve', lambda hp=hp, hh=hh, h=h: nc.vector.tensor_scalar(
                                out=s0b[hh * 64:(hh + 1) * 64, hp, :], in0=s0f[hh * 64:(hh + 1) * 64, hp, :], scalar1=float(GAMMA[h]), scalar2=None,
                                op0=ALU.mult), r=['s0f'], w=['s0b'])
                if kd_ == 0:
                    e32 = [ar32.get(QB) for _ in range(4)]
                    spb = [ar16.get(QB) for _ in range(6)]
                    exb = [ar16.get(QB) for _ in range(3)]
                if kd_ == 1:
                    biasF = ar32.get(NQB, NT * 4)
                    fsets = [[(ar32.get(QB), 'fF%d_%d' % (s_, i_)) for i_ in range(3)] for s_ in range(2)]
                    cst = ar32.get(NT * 4)
                    P.op('pe', lambda: nc.tensor.matmul(ps[:, 2, 0:NT * 4], lhsT=tri, rhs=LF[:, 0:NT, :].rearrange("p j h -> p (j h)"), start=True, stop=True),
                         r=['LF', 'mats'], w=[('ps', 2)])
                    copy_op('dve', cst[:, :], ps[:, 2, 0:NT * 4], r=[('ps', 2)], w=['cst'])
                    P.op('pe', lambda: nc.tensor.matmul(ps[:, 3, 0:NT * 4], lhsT=mats[:, 5, :], rhs=cst[:, :], start=True, stop=True),
                         r=['cst', 'mats'], w=[('ps', 3)])
                    tot = ar32.get(NT * 4); pre = ar32.get(NT * 4)
                    copy_op('dve', tot[:, :], ps[:, 3, 0:NT * 4], r=[('ps', 3)], w=['tot'])
                    totv = tot.rearrange("p (j h) -> p j h", h=4)
                    prev = pre.rearrange("p (j h) -> p j h", h=4)
                    for h in range(4):
                        P.op('dve', lambda h=h: nc.vector.tensor_tensor_scan(out=prev[:, :, h], data0=mats[:, 4, 0:NT], data1=totv[:, :, h], initial=0.0,
                                                                             op0=ALU.mult, op1=ALU.add), r=['tot', 'mats'], w=['pre'])
                    P.op('dve', lambda: nc.vector.tensor_tensor(out=pre[:, :], in0=pre[:, :], in1=tot[:, :], op=ALU.subtract), r=['pre', 'tot'], w=['pre'])
                    P.op('dve', lambda: nc.vector.tensor_tensor(out=CS[:, 0:NT, :].rearrange("p j h -> p (j h)"), in0=cst[:, :], in1=pre[:, :], op=ALU.add),
                         r=['cst', 'pre'], w=['CS'])
                    for qi in range(NQB):
                        tmid = PL + qi * QB + QB // 2
                        jm, rm = tmid // 128, tmid % 128
                        selm = {0: 6, 16: 7, 127: 5}[rm]
                        P.op('pe', lambda jm=jm, selm=selm: nc.tensor.matmul(ps[:, 2, 0:4], lhsT=mats[:, selm, :], rhs=CS[:, jm, :], start=True, stop=True),
                             r=['CS', 'mats'], w=[('ps', 2)])
                        copy_op('dve', srow[:, 0:4], ps[:, 2, 0:4], r=[('ps', 2)], w=['srow'])
                        P.op('dve', lambda qi=qi: nc.vector.tensor_tensor(
                            out=biasF[:, qi, :].rearrange("p (j h) -> p j h", h=4), in0=srow[:, 0:4].unsqueeze(1).to_broadcast([128, NT, 4]),
                            in1=CS[:, 0:NT, :], op=ALU.subtract), r=['srow', 'CS'], w=['biasF'])

                ybufs = [ybuf, ar16.get(4, QB)]
                groups = []
                for qi in range(NQB):
                    q0 = qi * QB
                    qg = Pk + q0
                    hsets = [(0, 1), (2, 3)]
                    for hs in hsets:
                        lists = []
                        for h in hs:
                            tl = []
                            for j in range(NTk):
                                d = 128 * j - qg
                                if kd_ == 0:
                                    if d >= QB - 1:
                                        continue
                                else:
                                    if d > QB - 1:
                                        continue
                                tl.append((j, d))
                            if kd_ == 0:
                                tl = tl[::-1]
                            lists.append([dict(qi=qi, q0=q0, h=h, j=j, d=d, n=n, last=(n == len(tl) - 1)) for n, (j, d) in enumerate(tl)])
                        for tup in zip(*lists):
                            groups.extend(tup)
                tasks = groups
                gstate = {}

                def stA(t):
                    h = t['h']; hp, hh = h // 2, h % 2
                    prt = slice(hh * 64, hh * 64 + 64)
                    zb = rr([0, 1], 'zb') if kd_ == 0 else rr([0, 1, 2, 3], 'zb')
                    t['zb'] = zb
                    q0 = t['q0']; j = t['j']
                    mm(ps[:, zb, 0:QB], KT[prt, hp, j * 128:(j + 1) * 128], QT[prt, hp, q0:q0 + QB], True, True,
                       r=['KT', 'QT'], w=[('ps', zb)])

                def stB(t):
                    zb = t['zb']; j = t['j']; d = t['d']; h = t['h']; qi = t['qi']
                    diag = d > -128
                    if kd_ == 1:
                        pt = rr(Pt, 'Pt'); pk = ('Pt', id(pt))
                        t['pt'] = (pt, pk)
                        P.op('act', lambda: nc.scalar.activation(
                            out=pt[:, :], in_=ps[:, zb, 0:QB], func=AF.Exp, bias=biasF[:, qi, j * 4 + h:j * 4 + h + 1]),
                            r=[('ps', zb), 'biasF'], w=[pk])
                        if diag:
                            P.op('pool', lambda: nc.gpsimd.tensor_tensor(out=pt[:, :], in0=pt[:, :], in1=masks[:, 0, 384 - d:384 - d + QB], op=ALU.mult),
                                 r=[pk, 'masks'], w=[pk])
                    elif kd_ == 2:
                        pt = rr(Pt, 'Pt'); pk = ('Pt', id(pt))
                        t['pt'] = (pt, pk)
                        cij = GAMMA[h] ** (t['q0'] - 128 * j - 127)
                        copy_op('dve', pt[:, :], ps[:, zb, 0:QB], r=[('ps', zb)], w=[pk], scale=cij)
                        if diag:
                            P.op('pool', lambda: nc.gpsimd.tensor_tensor(out=pt[:, :], in0=pt[:, :], in1=masks[:, 0, 384 - d:384 - d + QB], op=ALU.mult),
                                 r=[pk, 'masks'], w=[pk])
                    else:
                        e_ = rr(e32, 'e32'); ek = ('e32', id(e_))
                        P.op('act', lambda: nc.scalar.activation(out=e_[:, :], in_=ps[:, zb, 0:QB], func=AF.Exp), r=[('ps', zb)], w=[ek])
                        if diag:
                            P.op('pool', lambda: nc.gpsimd.tensor_tensor(out=e_[:, :], in0=e_[:, :], in1=masks[:, 1, 384 - d:384 - d + QB], op=ALU.mult),
                                 r=[ek, 'masks'], w=[ek])
                        sp_ = rr(spb, 'spb'); spk = ('spb', id(sp_))
                        P.op('act', lambda: nc.scalar.activation(out=sp_[:, :], in_=e_[:, :], func=AF.Ln, bias=1.0), r=[ek], w=[spk])
                        t['e'] = (e_, ek); t['sp'] = (sp_, spk)

                def stC1(t):
                    h = t['h']
                    xb = 2 + (h % 2)
                    key = (t['qi'], h)
                    prev_sp = gstate.get(('sp', key))
                    sp_, spk = t['sp']
                    e_, ek = t['e']
                    if prev_sp is not None:
                        mm(ps[:, xb, 0:QB], negL, prev_sp[0][:, :], False, False, r=[prev_sp[1], 'matb'], w=[('ps', xb)], acc=True)
                    mm(ps[:, xb, 0:QB], negU, sp_[:, :], t['n'] == 0, True, r=[spk, 'matb'], w=[('ps', xb)], acc=True)
                    gstate[('sp', key)] = (sp_, spk)
                    ex_ = rr(exb, 'exb'); exk = ('exb', id(ex_))
                    P.op('act', lambda: nc.scalar.activation(out=ex_[:, :], in_=ps[:, xb, 0:QB], func=AF.Exp), r=[('ps', xb)], w=[exk])
                    pt = rr(Pt, 'Pt'); pk = ('Pt', id(pt))
                    t['pt'] = (pt, pk)
                    P.op('pool', lambda: nc.gpsimd.tensor_tensor(out=pt[:, :], in0=e_[:, :], in1=ex_[:, :], op=ALU.mult),
                         r=[ek, exk], w=[pk])

                def stC(t):
                    h = t['h']; hp, hh = h // 2, h % 2
                    prt = slice(hh * 64, hh * 64 + 64)
                    qi = t['qi']; q0 = t['q0']; j = t['j']
                    key = (qi, h)
                    ob = 4 + (h % 2)
                    M = 65 if kd_ == 1 else 64
                    started = t['n'] > 0
                    if kd_ == 2 and PL > 0 and t['n'] == 0:
                        mm(ps[0:64, ob, 0:QB], s0b[prt, hp, :], QT[prt, hp, q0:q0 + QB], True, False, r=['s0b', 'QT'], w=[('ps', ob)], acc=True)
                        started = True
                    pt, pk = t['pt']
                    if kd_ == 2 and PL > 0:
                        mm(ps[0:M, ob, 0:QB], V[:, j, h, 0:M], pt[:, :], not started, t['last'], r=[('V', j), pk], w=[('ps', ob)], acc=True)
                    else:
                        mm(ps[:, ob, 0:QB], Vflat[:, j, h * 65:h * 65 + 128], pt[:, :], not started, t['last'], r=[('V', j), pk], w=[('ps', ob)], acc=True)
                    if not t['last']:
                        return
                    yb_ = ybufs[qi % 2]
                    yk = ('ybuf', qi % 2)
                    stages = []

                    def fin_done():
                        gstate[('done', qi)] = gstate.get(('done', qi), 0) + 1
                        if gstate[('done', qi)] == 4:
                            P.dma('sp', lambda: nc.sync.dma_start(out=ysc[br, :, :, q0:q0 + QB], in_=yb_[0:64, :, :]), r=[yk], w=[('ysc', br, qi)])
                            if debug:
                                P.dma('sp', lambda: nc.sync.dma_start(out=ydbg[0 if g == 'p' else 1, l, br, :, :, q0:q0 + QB], in_=yb_[0:64, :, :]),
                                      r=[yk], w=[('ydbg', br, qi, l, g)])
                    if kd_ == 0:
                        copy_op('dve', yb_[0:64, h, :], ps[0:64, ob, 0:QB], r=[('ps', ob)], w=[yk])
                        fin_done()
                    elif kd_ == 1:
                        fsi = gstate.get('fsel', 0) % 2
                        for st_l in list(pending):
                            if st_l[0] == fsi:
                                while len(st_l) > 1:
                                    st_l.pop(1)()
                                pending.remove(st_l)
                        fs = fsets[fsi]; gstate['fsel'] = gstate.get('fsel', 0) + 1
                        (sr, srk), (lnr, lnk), (rb, rbk) = fs

                        def f1():
                            copy_op('act', sr[64:65, 0:QB], ps[64:65, ob, 0:QB], r=[('ps', ob)], w=[srk])
                            P.op('pe', lambda: nc.tensor.matmul(ps[0:64, 6, 0:QB], lhsT=mats[64:65, 4, 0:64], rhs=sr[64:65, 0:QB], start=True, stop=True),
                                 r=[srk, 'mats'], w=[('ps', 6)])
                            P.op('dve', lambda: nc.vector.reciprocal(out=rb[0:64, 0:QB], in_=ps[0:64, 6, 0:QB]), r=[('ps', 6)], w=[rbk])

                        def f2():
                            P.op('dve', lambda: nc.vector.tensor_tensor(out=yb_[0:64, h, :], in0=ps[0:64, ob, 0:QB], in1=rb[0:64, 0:QB], op=ALU.mult),
                                 r=[('ps', ob), rbk], w=[yk])
                            fin_done()
                        f1()
                        stages = [fsi, f2]
                    else:
                        fsi = gstate.get('fsel', 0) % 2
                        for st_l in list(pending):
                            if st_l[0] == fsi:
                                while len(st_l) > 1:
                                    st_l.pop(1)()
                                pending.remove(st_l)
                        fs = fsets[fsi]
                        ob16, ob16k = fb16[fsi]
                        gstate['fsel'] = gstate.get('fsel', 0) + 1
                        (osb, osk), (xc, xck), (sd, sdk), (sg, sgk) = fs
                        ones_b = matb[0:64, 3, 0:64]

                        def f1():
                            copy_op('act', osb[0:64, :], ps[0:64, ob, 0:QB], r=[('ps', ob)], w=[osk])
                            copy_op('act', ob16[0:64, :], ps[0:64, ob, 0:QB], r=[('ps', ob)], w=[ob16k])
                            P.op('pe', lambda: nc.tensor.matmul(ps[0:64, 6, 0:QB], lhsT=ones_b, rhs=ob16[0:64, :], start=True, stop=True),
                                 r=[ob16k, 'matb'], w=[('ps', 6)])
                            P.op('dve', lambda: nc.vector.scalar_tensor_tensor(out=xc[0:64, :], in0=ps[0:64, 6, 0:QB], scalar=-1.0 / 64, in1=osb[0:64, :],
                                                                               op0=ALU.mult, op1=ALU.add), r=[('ps', 6), osk], w=[xck])

                        def f2():
                            P.op('dve', lambda: nc.vector.tensor_tensor(out=ob16[0:64, :], in0=xc[0:64, :], in1=xc[0:64, :], op=ALU.mult), r=[xck], w=[ob16k])
                            P.op('pe', lambda: nc.tensor.matmul(ps[0:64, 6, 0:QB], lhsT=ones_b, rhs=ob16[0:64, :], start=True, stop=True),
                                 r=[ob16k, 'matb'], w=[('ps', 6)])
                            P.op('act', lambda: nc.scalar.activation(out=sd[0:64, :], in_=ps[0:64, 6, 0:QB], func=AF.Ln, scale=1.0 / 64, bias=eps_t[0:64, 0:1]),
                                 r=[('ps', 6), 'eps'], w=[sdk])

                        def f3():
                            P.op('act', lambda: nc.scalar.activation(out=sd[0:64, :], in_=sd[0:64, :], func=AF.Exp, scale=-0.5), r=[sdk], w=[sdk])
                            P.op('dve', lambda: nc.vector.tensor_tensor(out=xc[0:64, :], in0=xc[0:64, :], in1=sd[0:64, :], op=ALU.mult), r=[xck, sdk], w=[xck])
                            for kc in range(KC):
                                mm(ps[0:64, 6, 0:QB], wg[hp][:, kc, hh * 64:(hh + 1) * 64], hT[:, kc, q0:q0 + QB], kc == 0, kc == KC - 1,
                                   r=[('wg', hp)] + [('hT', jj) for jj in range(NTn)], w=[('ps', 6)], acc=True)
                            P.op('act', lambda: nc.scalar.activation(out=sg[0:64, :], in_=ps[0:64, 6, 0:QB], func=AF.Exp, scale=-1.0), r=[('ps', 6)], w=[sgk])
                            copy_op('dve', sd[0:64, :], ps[0:64, 6, 0:QB], r=[('ps', 6), xck], w=[sdk])
                            P.op('act', lambda: nc.scalar.activation(out=sg[0:64, :], in_=sg[0:64, :], func=AF.Ln, bias=1.0), r=[sgk], w=[sgk])

                        def f4():
                            P.op('act', lambda: nc.scalar.activation(out=sg[0:64, :], in_=sg[0:64, :], func=AF.Exp, scale=-1.0), r=[sgk], w=[sgk])
                            P.op('dve', lambda: nc.vector.tensor_tensor(out=sd[0:64, :], in0=sd[0:64, :], in1=sg[0:64, :], op=ALU.mult),
                                 r=[sdk, sgk], w=[sdk])
                            P.op('dve', lambda: nc.vector.tensor_tensor(out=yb_[0:64, h, :], in0=xc[0:64, :], in1=sd[0:64, :], op=ALU.mult),
                                 r=[xck, sdk], w=[yk])
                            fin_done()
                        f1()
                        stages = [fsi, f2, f3, f4]
                    if stages:
                        pending.append(stages)

                pending = []
                NTK = len(tasks)
                if kd_ == 0:
                    steps = [[t_] for t_ in tasks]
                else:
                    steps = [tasks[i_:i_ + 2] for i_ in range(0, NTK, 2)]
                NS = len(steps)
                for s_ in range(NS + 2):
                    if s_ < NS:
                        for t_ in steps[s_]:
                            stA(t_)
                        for t_ in steps[s_]:
                            stB(t_)
                    if kd_ == 0 and 0 <= s_ - 1 < NS:
                        for t_ in steps[s_ - 1]:
                            stC1(t_)
                    sk_ = 2 if kd_ == 0 else 1
                    if 0 <= s_ - sk_ < NS:
                        for t_ in steps[s_ - sk_]:
                            stC(t_)
                    for st_l in list(pending):
                        st_l.pop(1)()
                        if len(st_l) == 1:
                            pending.remove(st_l)
                while pending:
                    for st_l in list(pending):
                        st_l.pop(1)()
                        if len(st_l) == 1:
                            pending.remove(st_l)
                return

            NIT = 20
            QT = ar16.get(NQT, 4, nq)
            KT = ar16.get(1, Lp)
            Vd = ar16.get(NT, 65)
            Pt = [ar16.get(4 * nq) for _ in range(4)]
            negsel = [ar16.get(Lp) for _ in range(2)]
            I4 = ar16.get(4 * nq)
            score = ar32.get(Lp)
            tmp = ar32.get(512)
            bs = ar32.get(8)
            wtab = ar32.get(NIT + 1)
            wtab2 = ar32.get(NIT + 1)
            P.op('pool', lambda: nc.gpsimd.memset(KT[:, :, :], 0.0), w=['KT'])
            P.op('pool', lambda: nc.gpsimd.memset(Vd[:, :, :], 0.0), w=['V'])
            P.op('pool', lambda: nc.gpsimd.memset(Vd[:, :, 64:65], 1.0), w=['V'])
            for h in range(4):
                P.op('dve', lambda h=h: nc.vector.tensor_copy(out=I4[0:nq, h * nq:(h + 1) * nq], in_=ident_b[0:nq, 0:nq]), r=['matb'], w=['I4'])
            tpb = QB // nq
            for h in range(4):
                proj_fm(l, CH_DSA + h, T, QB, NQB, rr(wch, 'wch'),
                        lambda qi, bank, h=h: copy_op(evac_eng(), QT[:, qi * tpb:(qi + 1) * tpb, h, :],
                                                      ps[:, bank, 0:QB].rearrange("p (t q) -> p t q", t=tpb), r=[('ps', bank)], w=['QT'], scale=0.125))
            proj_fm(l, CH_DSA + 4, T, QB, NQB, rr(wch, 'wch'),
                    lambda qi, bank: copy_op(evac_eng(), KT[:, 0, PL + qi * QB:PL + (qi + 1) * QB], ps[:, bank, 0:QB], r=[('ps', bank)], w=['KT']))
            if PL > 0:
                load_past_T(l, b, (c_dk, c_dki), lambda pr, j: KT[:, 0, j * 128:(j + 1) * 128], 128, JP, xst)
                P.dma('pool', lambda: nc.gpsimd.dma_start(out=Vd[:, 0:JP, 0:64], in_=c_dv[l, b].rearrange("(j p) d -> p j d", p=128)), r=['V'], w=['V'])
            vkeys = [('o_dv', j) for j in range(NTn)]
            if T >= 128:
                P.dma('pool', lambda: nc.gpsimd.dma_start(out=Vd[:, JP:JP + NTn, 0:64], in_=O['dsa_v'][l, b].rearrange("(j p) d -> p j d", p=128)),
                      r=vkeys + ['V'], w=['V'])
            else:
                P.dma('pool', lambda: nc.gpsimd.dma_start(out=Vd[0:T, JP, 0:64], in_=O['dsa_v'][l, b]), r=vkeys + ['V'], w=['V'])
            ybufs = [ybuf, ar16.get(4, QB)]

            def geom(qt):
                tg0 = PL + qt * nq
                nk = min(Ltot, ((tg0 + nq - 1) // 64 + 1) * 64)
                nkt = (nk + 127) // 128
                return nk, nkt, nkt * 128

            tmps = [tmp, ar32.get(512)]

            def prep(qt):
                nk, nkt, nkp = geom(qt)
                ns = negsel[qt % 2]; nsk = ('negsel', qt % 2)
                for kb in range((nkp + 511) // 512):
                    kw = min(512, nkp - kb * 512)
                    for h in range(4):
                        mm(ps[0:nq, h, 0:kw], QT[64:128, qt, h, :], KT[64:128, 0, kb * 512:kb * 512 + kw], True, True, r=['QT', 'KT'], w=[('ps', h)])
                    ksl = slice(kb * 512, kb * 512 + kw)
                    for h in range(4):
                        t_ = rr(tmps, 'tmps'); tk = ('tmps', id(t_))
                        P.op('act', lambda t_=t_, h=h, kw=kw: nc.scalar.activation(out=t_[0:nq, 0:kw], in_=ps[0:nq, h, 0:kw], func=AF.Relu, scale=IW[0:nq, qt, h:h + 1]),
                             r=[('ps', h), 'IW'], w=[tk])
                        if h == 0:
                            P.op('dve', lambda t_=t_, ksl=ksl, kw=kw: nc.vector.tensor_scalar(out=score[0:nq, ksl], in0=t_[0:nq, 0:kw], scalar1=IWs[0:nq, qt, 0:1], scalar2=None,
                                                                                              op0=ALU.mult), r=[tk, 'IWs'], w=['score'])
                        else:
                            P.op('dve', lambda t_=t_, ksl=ksl, kw=kw, h=h: nc.vector.scalar_tensor_tensor(out=score[0:nq, ksl], in0=t_[0:nq, 0:kw], scalar=IWs[0:nq, qt, h:h + 1],
                                                                                                         in1=score[0:nq, ksl], op0=ALU.mult, op1=ALU.add),
                                 r=[tk, 'IWs', 'score'], w=['score'])
                    yield
                if nkp > nk:
                    P.op('pool', lambda: nc.gpsimd.memset(score[0:nq, nk:nkp], NEG), r=['score'], w=['score'])
                if nq == 128:
                    P.op('pool', lambda: nc.gpsimd.memset(score[0:64, nk - 64:nk], NEG), r=['score'], w=['score'])
                    ncommon = nk - 64
                else:
                    ncommon = nk
                if nk <= topk:
                    P.op('dve', lambda: nc.vector.tensor_scalar(out=ns[0:nq, 0:nkp], in0=score[0:nq, 0:nkp], scalar1=-1.0e29, scalar2=-30000.0,
                                                                op0=ALU.is_lt, op1=ALU.mult), r=['score'], w=[nsk])
                    return
                assert ncommon >= topk
                n1 = nkp if nkp < 768 else ((nkp * 9 // 16) // 64) * 64
                n2 = nkp - n1
                nmin = min(ncommon, max(topk, 256))
                P.op('dve', lambda: nc.vector.tensor_reduce(out=bs[0:nq, 0:1], in_=score[0:nq, 0:nmin], axis=mybir.AxisListType.X, op=ALU.min),
                     r=['score'], w=['b_lo'])
                P.op('dve', lambda: nc.vector.tensor_reduce(out=bs[0:nq, 1:2], in_=score[0:nq, 0:nkp], axis=mybir.AxisListType.X, op=ALU.max),
                     r=['score'], w=['b_hi'])
                P.op('dve', lambda: nc.vector.tensor_tensor(out=bs[0:nq, 1:2], in0=bs[0:nq, 1:2], in1=bs[0:nq, 0:1], op=ALU.subtract), r=['b_hi', 'b_lo'], w=['b_hi'])
                P.op('dve', lambda: nc.vector.tensor_scalar(out=wtab[0:nq, 0:NIT + 1], in0=mats[0:nq, 8, 0:NIT + 1], scalar1=bs[0:nq, 1:2], scalar2=None, op0=ALU.mult),
                     r=['b_hi', 'mats'], w=['wtab'])
                P.op('dve', lambda: nc.vector.tensor_scalar(out=wtab2[0:nq, 0:NIT + 1], in0=mats[0:nq, 8, 0:NIT + 1], scalar1=bs[0:nq, 1:2], scalar2=2.0, op0=ALU.mult, op1=ALU.mult),
                     r=['b_hi', 'mats'], w=['wtab2'])
                P.op('dve', lambda: nc.vector.tensor_tensor(out=bs[0:nq, 2:3], in0=bs[0:nq, 0:1], in1=wtab[0:nq, 0:1], op=ALU.add), r=['b_lo', 'wtab'], w=['b_mid'])
                yield
                thr2 = 2.0 * (float(topk) - 0.5) - n2
                for it in range(NIT):
                    if n2 > 0:
                        P.op('act', lambda: nc.scalar.activation(out=ns[0:nq, n1:nkp], in_=score[0:nq, n1:nkp], func=AF.Sign, bias=bs[0:nq, 2:3], scale=-1.0,
                                                                 accum_out=bs[0:nq, 5:6]), r=['score', 'b_mid'], w=['b_acc', (nsk, 'b')])
                    P.op('dve', lambda: nc.vector.tensor_scalar(out=ns[0:nq, 0:n1], in0=score[0:nq, 0:n1], scalar1=bs[0:nq, 2:3], scalar2=0.0,
                                                                op0=ALU.is_ge, op1=ALU.add, accum_out=bs[0:nq, 3:4]), r=['score', 'b_mid'], w=['b_cnt', (nsk, 'a')])
                    if n2 > 0:
                        P.op('dve', lambda: nc.vector.scalar_tensor_tensor(out=bs[0:nq, 3:4], in0=bs[0:nq, 3:4], scalar=2.0, in1=bs[0:nq, 5:6],
                                                                           op0=ALU.mult, op1=ALU.subtract), r=['b_acc', 'b_cnt'], w=['b_cnt'])
                        thr_ = thr2
                    else:
                        thr_ = float(topk) - 0.5
                    P.op('dve', lambda it=it, thr_=thr_: nc.vector.tensor_scalar(out=bs[0:nq, 4:5], in0=bs[0:nq, 3:4], scalar1=thr_, scalar2=wtab2[0:nq, it + 1:it + 2],
                                                                                op0=ALU.is_ge, op1=ALU.mult), r=['b_cnt', 'wtab2'], w=['b_gw'])
                    P.op('dve', lambda it=it: nc.vector.scalar_tensor_tensor(out=bs[0:nq, 2:3], in0=bs[0:nq, 2:3], scalar=wtab[0:nq, it + 1:it + 2], in1=bs[0:nq, 4:5],
                                                                             op0=ALU.subtract, op1=ALU.add), r=['b_mid', 'b_gw', 'wtab'], w=['b_mid'])
                    yield
                P.op('dve', lambda: nc.vector.tensor_tensor(out=bs[0:nq, 0:1], in0=bs[0:nq, 2:3], in1=wtab[0:nq, NIT:NIT + 1], op=ALU.subtract), r=['b_mid', 'wtab'], w=['b_lo'])
                P.op('dve', lambda: nc.vector.tensor_scalar(out=ns[0:nq, 0:nkp], in0=score[0:nq, 0:nkp], scalar1=bs[0:nq, 0:1], scalar2=-30000.0,
                                                            op0=ALU.is_lt, op1=ALU.mult), r=['score', 'b_lo', (nsk, 'a'), (nsk, 'b')], w=[nsk, (nsk, 'a'), (nsk, 'b')])

            def attend(qt):
                nk, nkt, nkp = geom(qt)
                ns = negsel[qt % 2]; nsk = ('negsel', qt % 2)
                W4 = 4 * nq
                ob = 6
                qflat = QT[0:64, qt, :, :].rearrange("p h q -> p (h q)")
                tl = [dict(j=j) for j in range(nkt)]

                def sA(t):
                    j = t['j']
                    zb = rr([4, 5], 'zbd')
                    t['zb'] = zb
                    mm(ps[:, zb, 0:W4], KT[0:64, 0, j * 128:(j + 1) * 128], qflat, True, False, r=['KT', 'QT'], w=[('ps', zb)], acc=True)
                    mm(ps[:, zb, 0:W4], ns[0:nq, j * 128:(j + 1) * 128], I4[0:nq, 0:W4], False, True, r=[nsk, 'I4'], w=[('ps', zb)], acc=True)
                    pt = rr(Pt, 'Ptd'); pk = ('Ptd', id(pt))
                    t['pt'] = (pt, pk)
                    P.op('act', lambda: nc.scalar.activation(out=pt[:, 0:W4], in_=ps[:, zb, 0:W4], func=AF.Exp), r=[('ps', zb)], w=[pk])

                def sC(t):
                    j = t['j']
                    pt, pk = t['pt']
                    mm(ps[0:65, ob, 0:W4], Vd[:, j, 0:65], pt[:, 0:W4], j == 0, j == nkt - 1, r=['V', pk], w=[('ps', ob)], acc=True)
                for s_ in range(nkt + 1):
                    if s_ < nkt:
                        sA(tl[s_])
                    if s_ >= 1:
                        sC(tl[s_ - 1])
                    yield
                copy_op('act', srow[64:65, 0:W4], ps[64:65, ob, 0:W4], r=[('ps', ob)], w=['srow'])
                P.op('pe', lambda: nc.tensor.matmul(ps[0:64, 7, 0:W4], lhsT=mats[64:65, 4, 0:64], rhs=srow[64:65, 0:W4], start=True, stop=True),
                     r=['srow', 'mats'], w=[('ps', 7)])
                P.op('dve', lambda: nc.vector.reciprocal(out=rbc[0:64, 0:W4], in_=ps[0:64, 7, 0:W4]), r=[('ps', 7)], w=['rbc'])
                qi = qt // tpb
                yb_ = ybufs[qi % 2]; yk = ('ybuf', qi % 2)
                yb = yb_[0:64, :, (qt % tpb) * nq:(qt % tpb + 1) * nq]
                P.op('dve', lambda: nc.vector.tensor_tensor(
                    out=yb, in0=ps[0:64, ob, 0:W4].rearrange("p (h q) -> p h q", h=4), in1=rbc[0:64, 0:W4].rearrange("p (h q) -> p h q", h=4), op=ALU.mult),
                    r=[('ps', ob), 'rbc'], w=[yk])
                if qt % tpb == tpb - 1:
                    q0 = qi * QB
                    P.dma('sp', lambda: nc.sync.dma_start(out=ysc[3, :, :, q0:q0 + QB], in_=yb_[0:64, :, :]), r=[yk], w=[('ysc', 3, qi)])
                    if debug:
                        P.dma('sp', lambda: nc.sync.dma_start(out=ydbg[0 if g == 'p' else 1, l, 3, :, :, q0:q0 + QB], in_=yb_[0:64, :, :]), r=[yk], w=[('ydbg', 3, qi, l, g)])

            def drive(*gens):
                gens = [g_ for g_ in gens if g_ is not None]
                while gens:
                    for g_ in list(gens):
                        try:
                            next(g_)
                        except StopIteration:
                            gens.remove(g_)

            drive(prep(0))
            for qt in range(NQT):
                drive(prep(qt + 1) if qt + 1 < NQT else None, attend(qt))

        def ln_fm(rbuf, xcb, W, l, which, out32, out16, key, lnsd):
            for kc in range(KC):
                mm(ps[:, 4, 0:W], ones_f, rbuf[:, kc, 0:W], kc == 0, kc == KC - 1, r=[key + 'r', 'mats'], w=[('ps', 4)], acc=True)
            P.op('dve', lambda: nc.vector.scalar_tensor_tensor(out=xcb[:, :, 0:W], in0=ps[:, 4, 0:W].unsqueeze(1).to_broadcast([128, KC, W]), scalar=-1.0 / DM,
                                                               in1=rbuf[:, :, 0:W], op0=ALU.mult, op1=ALU.add), r=[('ps', 4), key + 'r'], w=[key + 'xc'])
            P.op('act', lambda: nc.scalar.activation(out=rbuf[:, :, 0:W], in_=xcb[:, :, 0:W], func=AF.Square), r=[key + 'xc'], w=[key + 'r'])
            for kc in range(KC):
                mm(ps[:, 4, 0:W], ones_f, rbuf[:, kc, 0:W], kc == 0, kc == KC - 1, r=[key + 'r', 'mats'], w=[('ps', 4)], acc=True)
            P.op('act', lambda: nc.scalar.activation(out=lnsd[:, 0:W], in_=ps[:, 4, 0:W], func=AF.Sqrt, scale=1.0 / DM, bias=eps_t[:, 0:1]),
                 r=[('ps', 4), 'eps'], w=['lnsd'])
            P.op('dve', lambda: nc.vector.reciprocal(out=lnsd[:, 0:W], in_=lnsd[:, 0:W]), r=['lnsd'], w=['lnsd'])
            P.op('dve', lambda: nc.vector.tensor_tensor(out=xcb[:, :, 0:W], in0=xcb[:, :, 0:W], in1=lnsd[:, 0:W].unsqueeze(1).to_broadcast([128, KC, W]), op=ALU.mult),
                 r=[key + 'xc', 'lnsd'], w=[key + 'xc'])
            for kc in range(KC):
                e = rr(['pool', 'dve'], 'lnaff')
                engo = nc.gpsimd if e == 'pool' else nc.vector
                P.op(e, lambda kc=kc, engo=engo: engo.tensor_scalar(out=out32[:, kc, 0:W], in0=xcb[:, kc, 0:W], scalar1=lnp[:, l, which, kc:kc + 1],
                                                                     scalar2=lnp[:, l, which + 1, kc:kc + 1], op0=ALU.mult, op1=ALU.add),
                     r=[key + 'xc', 'lnp'], w=[key + 'o32'])
            if out16 is not None:
                copy_op('act', out16, out32[:, :, 0:W], r=[key + 'o32'], w=[key + 'o16'])

        lnsd = None
        eps_t = None

        def ln_fm2(rb, rkey, sq16, sm, W, l, which, out32, okey):
            ones_b = matb[:, 3, :]
            P.op('act', lambda: nc.scalar.activation(out=sq16[:, :, 0:W], in_=rb[:, :, 0:W], func=AF.Square), r=[rkey], w=['sq16'])
            for kc in range(KC):
                mm(ps[:, 6, 0:W], ones_f, rb[:, kc, 0:W], kc == 0, kc == KC - 1, r=[rkey, 'mats'], w=[('ps', 6)], acc=True)
            for kc in range(KC):
                mm(ps[:, 7, 0:W], ones_b, sq16[:, kc, 0:W], kc == 0, kc == KC - 1, r=['sq16', 'matb'], w=[('ps', 7)], acc=True)
            copy_op('act', sm[:, 0, 0:W], ps[:, 6, 0:W], r=[('ps', 6)], w=['sm0'], scale=1.0 / DM)
            P.op('dve', lambda: nc.vector.tensor_tensor(out=sm[:, 1, 0:W], in0=sm[:, 0, 0:W], in1=sm[:, 0, 0:W], op=ALU.mult), r=['sm0'], w=['sm1'])
            P.op('dve', lambda: nc.vector.scalar_tensor_tensor(out=sm[:, 1, 0:W], in0=ps[:, 7, 0:W], scalar=1.0 / DM, in1=sm[:, 1, 0:W],
                                                               op0=ALU.mult, op1=ALU.subtract), r=[('ps', 7), 'sm1'], w=['sm1'])
            P.op('act', lambda: nc.scalar.activation(out=sm[:, 2, 0:W], in_=sm[:, 1, 0:W], func=AF.Ln, bias=eps_t[:, 0:1]), r=['sm1', 'eps'], w=['sm2'])
            P.op('act', lambda: nc.scalar.activation(out=sm[:, 2, 0:W], in_=sm[:, 2, 0:W], func=AF.Exp, scale=-0.5), r=['sm2'], w=['sm2'])
            P.op('dve', lambda: nc.vector.tensor_tensor(out=rb[:, :, 0:W], in0=rb[:, :, 0:W], in1=sm[:, 0, 0:W].unsqueeze(1).to_broadcast([128, KC, W]), op=ALU.subtract),
                 r=[rkey, 'sm0'], w=[rkey])
            P.op('dve', lambda: nc.vector.tensor_tensor(out=rb[:, :, 0:W], in0=rb[:, :, 0:W], in1=sm[:, 2, 0:W].unsqueeze(1).to_broadcast([128, KC, W]), op=ALU.mult),
                 r=[rkey, 'sm2'], w=[rkey])
            for kc in range(KC):
                e = rr(['pool', 'dve'], 'lnaff')
                engo = nc.gpsimd if e == 'pool' else nc.vector
                P.op(e, lambda kc=kc, engo=engo: engo.tensor_scalar(out=out32[:, kc, 0:W], in0=rb[:, kc, 0:W], scalar1=lnp[:, l, which, kc:kc + 1],
                                                                     scalar2=lnp[:, l, which + 1, kc:kc + 1], op0=ALU.mult, op1=ALU.add),
                     r=[rkey, 'lnp'], w=[okey])

        def phase3(g, b, l, T, NTn, O):
            W3 = min(512, T)
            NB3 = T // W3
            W = min(256, T)
            NB = T // W
            hkeys = lambda t0, w: [('hT', jj) for jj in range(t0 // 128, max(t0 // 128 + 1, (t0 + w) // 128))]
            ar16.reset(); ar32.reset()
            Ys = [ar16.get(16, W3) for _ in range(2)]
            WGs = [[ar16.get(KC, 128) for _ in range(4)] for _ in range(2)]
            WBs = [ar16.get(16, 128) for _ in range(2)]
            mouts = [ar16.get(W3) for _ in range(3)]
            sgb = [ar32.get(W3) for _ in range(2)]
            tmpb = [ar32.get(W3) for _ in range(2)]
            maccs = [ar32.get(W3) for _ in range(2)]
            for cc in range(KC):
                wb_ = WBs[cc % 2]; wbk = ('WB', cc % 2)
                P.dma('sp', lambda wb_=wb_, cc=cc: nc.sync.dma_start(out=wb_[0:64, :, :], in_=wb_b[l, cc]), r=[('wb', l, cc)], w=[wbk])
                wgs = WGs[cc % 2]
                for br in range(4):
                    P.dma('sp', lambda wg_=wgs[br], ch=CH_GATE + br * 8 + cc: nc.sync.dma_start(out=wg_[:, :, :], in_=wf_b[l, ch]),
                          r=[('wf', l, CH_GATE + br * 8 + cc)], w=[('WG', cc % 2, br)])
                for bi in range(NB3):
                    t0 = bi * W3
                    blk = slice(t0, t0 + W3)
                    hk = hkeys(t0, W3)
                    Y = rr(Ys, 'Ys'); yk_ = ('Y', id(Y))
                    for br in range(4):
                        P.dma('sp', lambda Y=Y, br=br, blk=blk: nc.sync.dma_start(out=Y[0:64, br * 4:(br + 1) * 4, :], in_=ysc[br, :, :, blk]),
                              r=[('ysc', br, t0 // min(512, T))], w=[yk_ + (br,)])
                    macc = rr(maccs, 'maccs'); mk = ('macc', id(macc))
                    for br in range(4):
                        bB = rr([0, 1], 'bB'); bG = rr([2, 3, 4, 5], 'bG')
                        for h in range(4):
                            mm(ps[:, bB, 0:W3], wb_[0:64, br * 4 + h, :], Y[0:64, br * 4 + h, :], h == 0, h == 3, r=[wbk, yk_ + (br,)], w=[('ps', bB)], acc=True)
                        for kc in range(KC):
                            mm(ps[:, bG, 0:W3], wgs[br][:, kc, :], hT[:, kc, blk], kc == 0, kc == KC - 1, r=[('WG', cc % 2, br)] + hk, w=[('ps', bG)], acc=True)
                        sg_ = rr(sgb, 'sgb'); sgk = ('sgb', id(sg_))
                        P.op('act', lambda sg_=sg_, bG=bG: nc.scalar.activation(out=sg_[:, :], in_=ps[:, bG, 0:W3], func=AF.Sigmoid), r=[('ps', bG)], w=[sgk])
                        if br == 0:
                            P.op('dve', lambda sg_=sg_, bB=bB, macc=macc: nc.vector.tensor_tensor(out=macc[:, :], in0=ps[:, bB, 0:W3], in1=sg_[:, :], op=ALU.mult),
                                 r=[('ps', bB), sgk], w=[mk])
                        else:
                            tb_ = rr(tmpb, 'tmpb'); tk = ('tmpb', id(tb_))
                            P.op('dve', lambda sg_=sg_, bB=bB, tb_=tb_: nc.vector.tensor_tensor(out=tb_[:, :], in0=ps[:, bB, 0:W3], in1=sg_[:, :], op=ALU.mult),
                                 r=[('ps', bB), sgk], w=[tk])
                            if br < 3:
                                P.op('pool', lambda macc=macc, tb_=tb_: nc.gpsimd.tensor_tensor(out=macc[:, :], in0=macc[:, :], in1=tb_[:, :], op=ALU.add), r=[tk, mk], w=[mk])
                            else:
                                mo = rr(mouts, 'mouts'); mok = ('mout', id(mo))
                                P.op('pool', lambda macc=macc, tb_=tb_, mo=mo: nc.gpsimd.tensor_tensor(out=mo[:, :], in0=macc[:, :], in1=tb_[:, :], op=ALU.add),
                                     r=[tk, mk], w=[mok])
                                P.dma('pool', lambda mo=mo, cc=cc, blk=blk: nc.gpsimd.dma_start(out=msc[cc * 128:(cc + 1) * 128, blk], in_=mo[:, :]), r=[mok], w=[('msc', cc, bi)])
            P.barrier()
            ar16.reset(); ar32.reset()
            WOa = ar16.get(KC * KC, 128)
            for oc in range(KC):
                P.dma('sp', lambda oc=oc: nc.sync.dma_start(out=WOa[:, oc * KC:(oc + 1) * KC, :], in_=wf_b[l, CH_OUT + oc]), r=[('wf', l, CH_OUT + oc)], w=[('WO', oc)])
            mblk = [ar16.get(KC, W) for _ in range(2)]
            sq16 = ar16.get(KC, W)
            rbufs = [ar32.get(KC, W) for _ in range(3)]
            lnsm = ar32.get(4, W)
            for bi in range(NB):
                t0 = bi * W
                blk = slice(t0, t0 + W)
                hk = hkeys(t0, W)
                mb = rr(mblk, 'mblk'); mbk = ('mblk', id(mb))
                P.dma('sp', lambda mb=mb, blk=blk: nc.sync.dma_start(out=mb[:, :, :], in_=msc[:, blk].rearrange("(c p) t -> p c t", p=128)), w=[mbk])
                rbuf = rbufs[bi % 3]; rkey = 'L1r%d' % (bi % 3)
                for oc in range(KC):
                    bank = rr([0, 1], 'bO')
                    for kc in range(KC):
                        mm(ps[:, bank, 0:W], WOa[:, oc * KC + kc, :], mb[:, kc, :], kc == 0, kc == KC - 1, r=[('WO', oc), mbk], w=[('ps', bank)], acc=True)
                    P.op('dve', lambda oc=oc, bank=bank, blk=blk, rbuf=rbuf: nc.vector.scalar_tensor_tensor(out=rbuf[:, oc, :], in0=hT[:, oc, blk], scalar=float(ALPHA), in1=ps[:, bank, 0:W],
                                                                                                 op0=ALU.mult, op1=ALU.add), r=[('ps', bank)] + hk, w=[rkey])
                ln_fm2(rbuf, rkey, sq16, lnsm, W, l, 0, rbuf, rkey)
                copy_op('act', hT[:, :, blk], rbuf[:, :, :], r=[rkey], w=hk)
                P.dma('pool', lambda blk=blk, rbuf=rbuf: nc.gpsimd.dma_start(out=h1sc[:, blk].rearrange("(c p) t -> p c t", p=128), in_=rbuf[:, :, :]), r=[rkey], w=[('h1sc', bi)])
            P.barrier()
            ar16.reset(); ar32.reset()
            WA = [ar16.get(KC, 128) for _ in range(2)]
            WU = [ar16.get(KC, 128) for _ in range(2)]
            gout = [ar16.get(W3) for _ in range(3)]
            sgb = [ar32.get(W3) for _ in range(3)]
            for fc in range(NFC):
                wa_ = rr(WA, 'WA'); wak = ('WA', id(wa_))
                wu_ = rr(WU, 'WU'); wuk = ('WU', id(wu_))
                P.dma('sp', lambda wa_=wa_, fc=fc: nc.sync.dma_start(out=wa_[:, :, :], in_=wf_b[l, CH_FA + fc]), r=[('wf', l, CH_FA + fc)], w=[wak])
                P.dma('sp', lambda wu_=wu_, fc=fc: nc.sync.dma_start(out=wu_[:, :, :], in_=wf_b[l, CH_FU + fc]), r=[('wf', l, CH_FU + fc)], w=[wuk])
                for bi in range(NB3):
                    t0 = bi * W3
                    blk = slice(t0, t0 + W3)
                    hk = hkeys(t0, W3)
                    bA = rr([0, 1, 2], 'bA'); bU = rr([3, 4, 5], 'bU')
                    for kc in range(KC):
                        mm(ps[:, bA, 0:W3], wa_[:, kc, :], hT[:, kc, blk], kc == 0, kc == KC - 1, r=[wak] + hk, w=[('ps', bA)], acc=True)
                    for kc in range(KC):
                        mm(ps[:, bU, 0:W3], wu_[:, kc, :], hT[:, kc, blk], kc == 0, kc == KC - 1, r=[wuk] + hk, w=[('ps', bU)], acc=True)
                    sg_ = rr(sgb, 'sgb3'); sgk = ('sgb3', id(sg_))
                    P.op('act', lambda sg_=sg_, bA=bA: nc.scalar.activation(out=sg_[:, :], in_=ps[:, bA, 0:W3], func=AF.Silu), r=[('ps', bA)], w=[sgk])
                    go = rr(gout, 'gout'); gok = ('gout', id(go))
                    P.op('dve', lambda sg_=sg_, bU=bU, go=go: nc.vector.tensor_tensor(out=go[:, :], in0=ps[:, bU, 0:W3], in1=sg_[:, :], op=ALU.mult),
                         r=[('ps', bU), sgk], w=[gok])
                    P.dma('pool', lambda go=go, fc=fc, blk=blk: nc.gpsimd.dma_start(out=gsc[fc * 128:(fc + 1) * 128, blk], in_=go[:, :]), r=[gok], w=[('gsc', fc, bi)])
            P.barrier()
            ar16.reset(); ar32.reset()
            W2a = ar16.get(KC * NFC, 128)
            for oc in range(KC):
                P.dma('sp', lambda oc=oc: nc.sync.dma_start(out=W2a[:, oc * NFC:(oc + 1) * NFC, :], in_=w2_b[l, oc]), r=[('w2', l, oc)], w=[('W2', oc)])
            gblk = [ar16.get(NFC, W) for _ in range(2)]
            sq16 = ar16.get(KC, W)
            rbufs = [ar32.get(KC, W) for _ in range(3)]
            lnsm = ar32.get(4, W)
            ystage = ar16.get(2048)[:, 0:2048].bitcast(F32)
            for bi in range(NB):
                t0 = bi * W
                blk = slice(t0, t0 + W)
                hk = hkeys(t0, W)
                gb = rr(gblk, 'gblk'); gbk = ('gblk', id(gb))
                P.dma('sp', lambda gb=gb, blk=blk: nc.sync.dma_start(out=gb[:, :, :], in_=gsc[:, blk].rearrange("(c p) t -> p c t", p=128)), w=[gbk])
                rbuf = rbufs[bi % 3]; rkey = 'L2r%d' % (bi % 3)
                P.dma('sp', lambda blk=blk, rbuf=rbuf: nc.sync.dma_start(out=rbuf[:, :, :], in_=h1sc[:, blk].rearrange("(c p) t -> p c t", p=128)), w=[rkey])
                for oc in range(KC):
                    bank = rr([0, 1], 'bO')
                    for fc in range(NFC):
                        mm(ps[:, bank, 0:W], W2a[:, oc * NFC + fc, :], gb[:, fc, :], fc == 0, fc == NFC - 1, r=[('W2', oc), gbk], w=[('ps', bank)], acc=True)
                    P.op('dve', lambda oc=oc, bank=bank, rbuf=rbuf: nc.vector.scalar_tensor_tensor(out=rbuf[:, oc, :], in0=rbuf[:, oc, :], scalar=float(ALPHA), in1=ps[:, bank, 0:W],
                                                                                                   op0=ALU.mult, op1=ALU.add), r=[('ps', bank), rkey], w=[rkey])
                ln_fm2(rbuf, rkey, sq16, lnsm, W, l, 2, rbuf, rkey)
                if l == 0:
                    copy_op('act', hT[:, :, blk], rbuf[:, :, :], r=[rkey], w=hk)
                else:
                    for tt in range((W + 127) // 128):
                        rows = min(128, W - tt * 128)
                        for half in range(2):
                            bank = rr([2, 3], 'tb3')
                            for q in range(4):
                                kc = half * 4 + q
                                P.op('pe', lambda kc=kc, tt=tt, rows=rows, bank=bank, q=q, rbuf=rbuf: nc.tensor.transpose(
                                    out=ps[0:rows, bank, q * 128:(q + 1) * 128], in_=rbuf[:, kc, tt * 128:tt * 128 + rows], identity=ident_f),
                                    r=[rkey, 'mats'], w=[('ps', bank)], acc=True)
                            copy_op(evac_eng(), ystage[0:rows, half * 512:(half + 1) * 512], ps[0:rows, bank, 0:512], r=[('ps', bank)], w=['ystage'])
                        P.dma('pool', lambda tt=tt, rows=rows, t0=t0: nc.gpsimd.dma_start(out=O['y'][b, t0 + tt * 128:t0 + tt * 128 + rows, :], in_=ystage[0:rows, :]),
                              r=['ystage'], w=[('o_y', bi, tt)])

        eps_t = es.enter_context(nc.sbuf_tensor("eps_t", [128, 1], F32))
        P.op('pool', lambda: nc.gpsimd.memset(eps_t[:], EPS), w=['eps'])
        P.op('pool', lambda: nc.gpsimd.memset(IW[:], 0.0), w=['IW'])
        P.op('pool', lambda: nc.gpsimd.memset(IWs[:], 1.0), w=['IWs'])
        P.op('pool', lambda: nc.gpsimd.memset(CS[:], 0.0), w=['CS'])
        P.barrier()
        for g, B in (('p', BP), ('s', BS)):
            for b in range(B):
                process_seq(g, b)
        P.barrier()
        print("ops:", P.nops, {e: len(v) for e, v in P.ops.items()})
        P.emit(nc, sems, block)
    return nc


FULL_CFG = dict(NC=8, BP=2, BS=2, TP=4096, TS=32, PAST=2048)
_CACHE = {}


def run_cfg(inp, cfg, debug=False):
    NCc, BP, BS, TP, TS, PAST = cfg['NC'], cfg['BP'], cfg['BS'], cfg['TP'], cfg['TS'], cfg['PAST']
    f = lambda a: np.ascontiguousarray(np.asarray(a), dtype=np.float32)
    wd = prep_weights(f(inp['w_in']), f(inp['w_branch']), f(inp['w_out']), f(inp['w_ffn_in']), f(inp['w_ffn_out']),
                      f(inp['ln1_g']), f(inp['ln1_b']), f(inp['ln2_g']), f(inp['ln2_b']), f(inp['b_forget']))
    masks, mats = const_tables()
    rfp, rtp = rot_tables(0, TP)
    rfs, rts = rot_tables(PAST, TS)
    key = (tuple(sorted(cfg.items())), debug)
    if key not in _CACHE:
        _CACHE[key] = build_program(cfg, debug)
    nc = _CACHE[key]
    xp, xs = f(inp['x_prompt']), f(inp['x_sample'])
    in_maps = []
    for c in range(NCc):
        sp = slice(c * BP, (c + 1) * BP)
        ss = slice(c * BS, (c + 1) * BS)
        m = dict(xp=xp[sp], xs=xs[ss],
                 c_sbk=f(inp['cache_sb_k'])[:, ss].reshape(2, BS, PAST, 256), c_sbv=f(inp['cache_sb_v'])[:, ss].reshape(2, BS, PAST, 256),
                 c_fk=f(inp['cache_fox_k'])[:, ss].reshape(2, BS, PAST, 256), c_fv=f(inp['cache_fox_v'])[:, ss].reshape(2, BS, PAST, 256),
                 c_flf=f(inp['cache_fox_logf'])[:, ss], c_dk=f(inp['cache_dsa_k'])[:, ss], c_dv=f(inp['cache_dsa_v'])[:, ss],
                 c_dki=f(inp['cache_dsa_kidx'])[:, ss], st_ret=f(inp['state_ret'])[:, ss],
                 wtm=wd['wtm'], wf=wd['wf'], w2=wd['w2'], wb=wd['wb'], lnp=wd['lnp'], bfor=wd['bfor'],
                 masks=masks, mats=mats, rotfm_p=rfp, rotfm_s=rfs, rottm_p=rtp, rottm_s=rts)
        in_maps.append({k: np.ascontiguousarray(v) for k, v in m.items()})
    res = run_bass_kernel_spmd(nc, in_maps, core_ids=list(range(NCc)))
    R = res.results

    def cat(name, axis):
        return np.concatenate([np.asarray(R[c][name], dtype=np.float32) for c in range(NCc)], axis=axis)
    out = []
    out.append(cat('y_p', 0))
    out.append(cat('y_s', 0))
    for g, T in (('p', TP), ('s', TS)):
        Bt = (BP if g == 'p' else BS) * NCc
        out.append(cat('sb_k_' + g, 1).reshape(2, Bt, T, 4, 64))
        out.append(cat('sb_v_' + g, 1).reshape(2, Bt, T, 4, 64))
        out.append(cat('ret_' + g, 1))
        out.append(cat('fox_k_' + g, 1).reshape(2, Bt, T, 4, 64))
        out.append(cat('fox_v_' + g, 1).reshape(2, Bt, T, 4, 64))
        out.append(cat('fox_lf_' + g, 1))
        out.append(cat('dsa_k_' + g, 1))
        out.append(cat('dsa_v_' + g, 1))
        out.append(cat('dsa_ki_' + g, 1))
    if debug:
        return tuple(out), [np.asarray(R[c]['ydbg']) for c in range(NCc)]
    return tuple(out)


def kernel(**inputs):
    return run_cfg(inputs, FULL_CFG)
```

```python
import math
import numpy as np
import ml_dtypes
import concourse.bass as bass
import concourse.mybir as mybir
from concourse.bass_utils import run_bass_kernel_spmd

F32 = mybir.dt.float32
BF16 = mybir.dt.bfloat16
AF = mybir.ActivationFunctionType
ALU = mybir.AluOpType

DM = 1024
KC = 8
FFN = 2816
NFC = 22
ALPHA = 4 ** 0.25
EPS = 1e-5
NEG = -1.0e30
GAMMA = [1.0 - 2.0 ** (-5.0 - h) for h in range(4)]

OFF = {}
_o = 0
for _n, _w in (('sb_q', 256), ('sb_k', 256), ('sb_v', 256), ('ret_q', 256), ('ret_k', 256), ('ret_v', 256),
               ('ret_g', 256), ('fox_q', 256), ('fox_k', 256), ('fox_v', 256), ('fox_f', 4), ('dsa_q', 256),
               ('dsa_k', 64), ('dsa_v', 64), ('idx_q', 256), ('idx_k', 64), ('idx_w', 4), ('merge_gate', 4096)):
    OFF[_n] = (_o, _w)
    _o += _w
IN_WIDTH = _o

TMG = [(0, 512), (512, 512), (1024, 200), (1224, 768)]
NTM = 1992
CH_SB = 0
CH_FOX = 4
CH_RET = 8
CH_DSA = 18
CH_GATE = 23
CH_OUT = 55
CH_FA = 63
CH_FU = 85
NCH = 107


def _cols(name):
    o, w = OFF[name]
    return np.arange(o, o + w)


def _swap_cols(name):
    o, w = OFF[name]
    idx = np.arange(w).reshape(4, 2, 32)[:, ::-1, :].reshape(-1)
    return o + idx


def prep_weights(w_in, w_branch, w_out, w_ffn_in, w_ffn_out, ln1_g, ln1_b, ln2_g, ln2_b, b_forget):
    L = w_in.shape[0]
    tm_cols = np.concatenate([_cols('sb_k'), _cols('sb_v'), _cols('fox_k'), _cols('fox_v'), _cols('dsa_k'),
                              _cols('dsa_v'), _cols('idx_k'), _cols('fox_f'), _cols('idx_w'), _cols('ret_v'),
                              _cols('ret_k'), _swap_cols('ret_k')])
    assert tm_cols.size == NTM
    wtm = w_in[:, :, tm_cols].reshape(L, KC, 128, NTM).transpose(0, 2, 1, 3)
    chunks = []

    def add(cols):
        assert cols.size == 128
        chunks.append(cols)
    sq, sk = _cols('sb_q'), _cols('sb_k')
    add(sq[:128]); add(sq[128:]); add(sk[:128]); add(sk[128:])
    fq, fk = _cols('fox_q'), _cols('fox_k')
    add(fq[:128]); add(fq[128:]); add(fk[:128]); add(fk[128:])
    for nm in ('ret_q', 'ret_k'):
        a, b = _cols(nm), _swap_cols(nm)
        add(a[:128]); add(a[128:]); add(b[:128]); add(b[128:])
    g = _cols('ret_g')
    add(g[:128]); add(g[128:])
    dq, iq = _cols('dsa_q'), _cols('idx_q')
    for h in range(4):
        add(np.concatenate([dq[h * 64:(h + 1) * 64], iq[h * 64:(h + 1) * 64]]))
    add(np.concatenate([_cols('dsa_k'), _cols('idx_k')]))
    mg = _cols('merge_gate')
    for i in range(32):
        add(mg[i * 128:(i + 1) * 128])
    assert len(chunks) == CH_OUT
    win_ch = np.stack([w_in[:, :, c] for c in chunks], axis=1)
    wo_ch = w_out.reshape(L, DM, 8, 128).transpose(0, 2, 1, 3)
    wf_ch = w_ffn_in.reshape(L, DM, 44, 128).transpose(0, 2, 1, 3)
    allch = np.concatenate([win_ch, wo_ch, wf_ch], axis=1)
    assert allch.shape[1] == NCH
    wf = allch.reshape(L, NCH, KC, 128, 128).transpose(0, 1, 3, 2, 4)
    w2 = w_ffn_out.reshape(L, NFC, 128, 8, 128).transpose(0, 3, 2, 1, 4)
    wb = w_branch.reshape(L, 4, 4, 64, 8, 128).transpose(0, 4, 3, 1, 2, 5).reshape(L, 8, 64, 16, 128)
    lnp = np.stack([ln1_g, ln1_b, ln2_g, ln2_b], axis=1).reshape(L, 4, KC, 128).transpose(3, 0, 1, 2)
    bfor = np.broadcast_to(b_forget[None], (128, L, 4))
    f = lambda a: np.ascontiguousarray(a, dtype=np.float32)
    return dict(wtm=f(wtm), wf=f(wf), w2=f(w2), wb=f(wb), lnp=f(lnp), bfor=f(bfor))


def rot_tables(P, T):
    half = 32
    inv_freq = (10000.0 ** (-np.arange(half, dtype=np.float32) / half)).astype(np.float32)
    pos = (P + np.arange(T)).astype(np.float32)
    ang = pos[:, None] * inv_freq[None, :]
    cos = np.cos(ang).astype(np.float64)
    sin = np.sin(ang).astype(np.float64)
    cosf = np.concatenate([cos, cos], 1)
    sinf = np.concatenate([-sin, sin], 1)
    QB = min(512, T)
    n = np.arange(T)
    fm = np.zeros((4, 256, T), np.float64)
    tm = np.zeros((2, T, 256), np.float64)
    for h in range(4):
        g = GAMMA[h]
        dq = g ** (n % QB).astype(np.float64)
        dk = g ** (127 - (n % 128)).astype(np.float64) * 0.125
        ds = g ** (T - 1 - n).astype(np.float64) * 0.125
        sl = slice(h * 64, (h + 1) * 64)
        fm[0, sl] = (cosf * dq[:, None]).T
        fm[1, sl] = (sinf * dq[:, None]).T
        fm[2, sl] = (cosf * dk[:, None]).T
        fm[3, sl] = (sinf * dk[:, None]).T
        tm[0, :, sl] = cosf * ds[:, None]
        tm[1, :, sl] = sinf * ds[:, None]
    fm = fm.reshape(4, 2, 128, T).transpose(0, 2, 1, 3)
    return np.ascontiguousarray(fm, np.float32), np.ascontiguousarray(tm, np.float32)


def const_tables():
    r = np.arange(128)[:, None]
    c = np.arange(512)[None, :]
    x = np.arange(896)[None, :]
    masks = np.zeros((128, 2, 896), np.float32)
    masks[:, 0, :] = (r <= x - 384)
    masks[:, 1, :] = (r < x - 384)
    j = np.arange(128)[:, None]
    s = np.arange(128)[None, :]
    mats = np.zeros((128, 9, 128), np.float32)
    mats[:, 0, :] = np.eye(128)
    mats[:, 1, :] = -1.0 * (j >= s)
    mats[:, 2, :] = -1.0 * (j < s)
    mats[:, 3, :] = (j <= s)
    mats[:, 4, :] = 1.0
    mats[:, 5, :] = (j == 127)
    mats[:, 6, :] = (j == 0)
    mats[:, 7, :] = (j == 16)
    mats[:, 8, :] = (0.5 ** (np.arange(128) + 1.0))[None, :]
    return masks.astype(ml_dtypes.bfloat16), mats


class Prog:
    CE = ('pe', 'act', 'dve', 'pool')

    def __init__(self, ndma=16, nsp=10):
        self.ops = {e: [] for e in self.CE + ('sp',)}
        self.seq = {e: 0 for e in self.CE}
        self.ndma = ndma
        self.dcnt = [0] * ndma
        self.dnext = {'sp': 0, 'pool': 0}
        self.nsp = nsp
        self.lastw = {}
        self.rd = {}
        self.known = {e: {} for e in self.CE + ('sp',)}
        self.snap = {}
        self.nops = 0

    def _collect(self, eng, reads, writes, acc):
        need = {}

        def add(ev, is_w=False):
            if ev is None:
                return
            c, v = ev
            if acc and is_w and c == eng:
                return
            if need.get(c, 0) < v:
                need[c] = v
        for r in reads:
            add(self.lastw.get(r))
        for w in writes:
            add(self.lastw.get(w), True)
            for c, v in self.rd.get(w, {}).items():
                add((c, v))
        kn = self.known[eng]
        waits = []
        for c, v in need.items():
            if kn.get(c, 0) >= v:
                continue
            waits.append((c, v))
        for c, v in waits:
            sn = self.snap.get((c, v))
            if sn:
                for c2, v2 in sn.items():
                    if kn.get(c2, 0) < v2:
                        kn[c2] = v2
            if kn.get(c, 0) < v:
                kn[c] = v
        return waits

    def _mark(self, ev, reads, writes):
        for w in writes:
            self.lastw[w] = ev
            self.rd[w] = {}
        for r in reads:
            d = self.rd.setdefault(r, {})
            if d.get(ev[0], 0) < ev[1]:
                d[ev[0]] = ev[1]

    def op(self, eng, fn, r=(), w=(), acc=False):
        isps = lambda x: isinstance(x, tuple) and x[0] in ('ps', 'psT')
        w = list(w) + [x for x in r if isps(x)]
        r = [x for x in r if not isps(x)]
        waits = self._collect(eng, r, w, acc)
        self.seq[eng] += 1
        ev = (eng, self.seq[eng])
        self.snap[ev] = dict(self.known[eng])
        self.ops[eng].append((waits, fn, eng, 1))
        self._mark(ev, r, w)
        self.nops += 1

    def dma(self, q, fn, r=(), w=()):
        if q == 'sp':
            slot = self.dnext['sp']
            self.dnext['sp'] = (slot + 1) % self.nsp
        else:
            slot = self.nsp + self.dnext['pool']
            self.dnext['pool'] = (self.dnext['pool'] + 1) % (self.ndma - self.nsp)
        clk = 'd%d' % slot
        waits = self._collect(q, r, w, False)
        if self.dcnt[slot] > 0 and self.known[q].get(clk, 0) < self.dcnt[slot]:
            waits.append((clk, self.dcnt[slot]))
            self.known[q][clk] = self.dcnt[slot]
        self.dcnt[slot] += 1
        ev = (clk, self.dcnt[slot])
        self.snap[ev] = dict(self.known[q])
        self.ops[q].append((waits, fn, clk, 16))
        self._mark(ev, r, w)
        self.nops += 1

    def barrier(self):
        allev = [(e, self.seq[e]) for e in self.CE if self.seq[e] > 0]
        allev += [('d%d' % s, self.dcnt[s]) for s in range(self.ndma) if self.dcnt[s] > 0]
        for e in self.CE + ('sp',):
            waits = [(c, v) for c, v in allev if c != e and self.known[e].get(c, 0) < v]
            for c, v in waits:
                self.known[e][c] = v
            if waits:
                self.ops[e].append((waits, None, None, 0))
        self.lastw = {}
        self.rd = {}

    def emit(self, nc, sems, block):
        engs = {'pe': (block.tensor, nc.tensor), 'act': (block.scalar, nc.scalar), 'dve': (block.vector, nc.vector),
                'pool': (block.gpsimd, nc.gpsimd), 'sp': (block.sync, nc.sync)}
        for e, (dec, engobj) in engs.items():
            ops = self.ops[e]

            def body(_eng, ops=ops, engobj=engobj):
                for waits, fn, clk, inc in ops:
                    for c, v in waits:
                        engobj.wait_ge(sems[c], v * (16 if c[0] == 'd' and c[1:].isdigit() else 1))
                    if fn is not None:
                        fn().then_inc(sems[clk], inc)
            dec(body)


def build_program(cfg, debug=False):
    BP, BS, TP, TS, PAST = cfg['BP'], cfg['BS'], cfg['TP'], cfg['TS'], cfg['PAST']
    nc = bass.Bass("TRN2", target_bir_lowering=False)
    P = Prog()

    def din(name, shape, dt=F32):
        return nc.dram_tensor(name, list(shape), dt, kind="ExternalInput").ap()

    def dout(name, shape, dt=F32):
        return nc.dram_tensor(name, list(shape), dt, kind="ExternalOutput").ap()

    def dscr(name, shape, dt, dbg=False):
        return nc.dram_tensor(name, list(shape), dt, kind=("ExternalOutput" if (dbg and debug) else "Internal")).ap()

    x_in = {'p': din('xp', [BP, TP, DM]), 's': din('xs', [BS, TS, DM])}
    c_sbk = din('c_sbk', [2, BS, PAST, 256]); c_sbv = din('c_sbv', [2, BS, PAST, 256])
    c_fk = din('c_fk', [2, BS, PAST, 256]); c_fv = din('c_fv', [2, BS, PAST, 256])
    c_flf = din('c_flf', [2, BS, PAST, 4])
    c_dk = din('c_dk', [2, BS, PAST, 64]); c_dv = din('c_dv', [2, BS, PAST, 64]); c_dki = din('c_dki', [2, BS, PAST, 64])
    st_ret = din('st_ret', [2, BS, 4, 64, 64])
    wtm_f = din('wtm', [2, 128, KC, NTM]); wf_f = din('wf', [2, NCH, 128, KC, 128])
    w2_f = din('w2', [2, 8, 128, NFC, 128]); wb_f = din('wb', [2, 8, 64, 16, 128])
    lnp_d = din('lnp', [128, 2, 4, 8]); bfor_d = din('bfor', [128, 2, 4])
    masks_d = din('masks', [128, 2, 896], BF16); mats_d = din('mats', [128, 9, 128])
    rotfm = {'p': din('rotfm_p', [4, 128, 2, TP]), 's': din('rotfm_s', [4, 128, 2, TS])}
    rottm = {'p': din('rottm_p', [2, TP, 256]), 's': din('rottm_s', [2, TS, 256])}

    outs = {}
    for g, B, T in (('p', BP, TP), ('s', BS, TS)):
        outs[g] = dict(
            y=dout('y_' + g, [B, T, DM]),
            sb_k=dout('sb_k_' + g, [2, B, T, 256]), sb_v=dout('sb_v_' + g, [2, B, T, 256]),
            ret=dout('ret_' + g, [2, B, 4, 64, 64]),
            fox_k=dout('fox_k_' + g, [2, B, T, 256]), fox_v=dout('fox_v_' + g, [2, B, T, 256]),
            fox_lf=dout('fox_lf_' + g, [2, B, T, 4]),
            dsa_k=dout('dsa_k_' + g, [2, B, T, 64]), dsa_v=dout('dsa_v_' + g, [2, B, T, 64]),
            dsa_ki=dout('dsa_ki_' + g, [2, B, T, 64]))
    TMAX = max(TP, TS)
    wtm_b = dscr('wtm_b', [2, 128, KC, NTM], BF16); wf_b = dscr('wf_b', [2, NCH, 128, KC, 128], BF16)
    w2_b = dscr('w2_b', [2, 8, 128, NFC, 128], BF16); wb_b = dscr('wb_b', [2, 8, 64, 16, 128], BF16)
    retv_s = dscr('retv_s', [TMAX, 256], F32)
    ysc = dscr('ysc', [4, 64, 4, TMAX], BF16)
    msc = dscr('msc', [DM, TMAX], BF16)
    h1sc = dscr('h1sc', [DM, TMAX], F32)
    gsc = dscr('gsc', [NFC * 128, TMAX], BF16)
    ydbg = dscr('ydbg', [2, 2, 4, 64, 4, TMAX], BF16, dbg=True) if debug else None

    NTMAX = (max(TP, PAST + TS) + 127) // 128
    A16 = 40960
    A32 = 7424
    import contextlib
    es = contextlib.ExitStack()
    with es:
        hT = es.enter_context(nc.sbuf_tensor("hT", [128, KC, TMAX], BF16))
        a16 = es.enter_context(nc.sbuf_tensor("a16", [128, A16], BF16))
        a32 = es.enter_context(nc.sbuf_tensor("a32", [128, A32], F32))
        masks = es.enter_context(nc.sbuf_tensor("masks_sb", [128, 2, 896], BF16))
        mats = es.enter_context(nc.sbuf_tensor("mats_sb", [128, 9, 128], F32))
        matb = es.enter_context(nc.sbuf_tensor("matb", [128, 4, 128], BF16))
        lnp = es.enter_context(nc.sbuf_tensor("lnp_sb", [128, 2, 4, 8], F32))
        bfor = es.enter_context(nc.sbuf_tensor("bfor_sb", [128, 2, 4], F32))
        LF = es.enter_context(nc.sbuf_tensor("LF", [128, NTMAX, 4], F32))
        CS = es.enter_context(nc.sbuf_tensor("CS", [128, NTMAX, 4], F32))
        IW = es.enter_context(nc.sbuf_tensor("IW", [128, 32, 4], F32))
        IWs = es.enter_context(nc.sbuf_tensor("IWs", [128, 32, 4], F32))
        ps = es.enter_context(nc.psum_tensor("ps", [128, 8, 512], F32))
        sems = {}
        for e in Prog.CE:
            sems[e] = es.enter_context(nc.semaphore("s_" + e))
        for s in range(P.ndma):
            sems['d%d' % s] = es.enter_context(nc.semaphore("s_d%d" % s))
        block = es.enter_context(nc.Block())

        ident_f = mats[:, 0, :]
        tri = mats[:, 3, :]
        ones_f = mats[:, 4, :]
        ident_b = matb[:, 0, :]
        negU = matb[:, 1, :]
        negL = matb[:, 2, :]

        class Arena:
            def __init__(self, t, size):
                self.t, self.size, self.o = t, size, 0

            def reset(self):
                self.o = 0

            def get(self, *shape):
                n = int(np.prod(shape))
                n = (n + 15) // 16 * 16
                assert self.o + n <= self.size, ("arena overflow", self.o, n, self.size)
                v = self.t[:, self.o:self.o + int(np.prod(shape))]
                self.o += n
                if len(shape) == 2:
                    return v.rearrange("p (a b) -> p a b", a=shape[0])
                if len(shape) == 3:
                    return v.rearrange("p (a b c) -> p a b c", a=shape[0], b=shape[1])
                return v
        ar16 = Arena(a16, A16)
        ar32 = Arena(a32, A32)

        rot = {'n': 0}

        def rr(lst, key):
            i = rot.get(key, 0)
            rot[key] = i + 1
            return lst[i % len(lst)]

        def evac_eng(key='ev'):
            return rr(['act', 'dve'], key)

        def copy_op(eng, out, in_, r, w, scale=None):
            if eng == 'act':
                if scale is None:
                    P.op('act', lambda: nc.scalar.activation(out=out, in_=in_, func=AF.Copy), r=r, w=w)
                else:
                    P.op('act', lambda: nc.scalar.activation(out=out, in_=in_, func=AF.Copy, scale=float(scale)), r=r, w=w)
            elif eng == 'dve':
                if scale is None:
                    P.op('dve', lambda: nc.vector.tensor_copy(out=out, in_=in_), r=r, w=w)
                else:
                    P.op('dve', lambda: nc.vector.tensor_scalar(out=out, in0=in_, scalar1=float(scale), scalar2=None,
                                                               op0=ALU.mult), r=r, w=w)
            else:
                if scale is None:
                    P.op('pool', lambda: nc.gpsimd.tensor_copy(out=out, in_=in_), r=r, w=w)
                else:
                    P.op('pool', lambda: nc.gpsimd.tensor_scalar(out=out, in0=in_, scalar1=float(scale), scalar2=None,
                                                                op0=ALU.mult), r=r, w=w)

        def mm(out, lhsT, rhs, start, stop, r, w, acc=False):
            P.op('pe', lambda: nc.tensor.matmul(out, lhsT=lhsT, rhs=rhs, start=start, stop=stop, skip_group_check=True), r=r, w=w, acc=acc)

        P.dma('sp', lambda: nc.sync.dma_start(out=masks[:], in_=masks_d[:, :, :]), w=['masks'])
        P.dma('sp', lambda: nc.sync.dma_start(out=mats[:], in_=mats_d[:, :, :]), w=['mats'])
        P.dma('sp', lambda: nc.sync.dma_start(out=lnp[:], in_=lnp_d[:, :, :, :]), w=['lnp'])
        P.dma('sp', lambda: nc.sync.dma_start(out=bfor[:], in_=bfor_d[:, :, :]), w=['bfor'])
        P.op('dve', lambda: nc.vector.tensor_copy(out=matb[:, 0:3, :], in_=mats[:, 0:3, :]), r=['mats'], w=['matb'])
        P.op('dve', lambda: nc.vector.tensor_copy(out=matb[:, 3, :], in_=mats[:, 4, :]), r=['mats', 'matb'], w=['matb'])
        for l in range(2):
            P.dma('pool', lambda l=l: nc.gpsimd.dma_start(out=wtm_b[l], in_=wtm_f[l]), w=[('wtm', l)])
            for c0 in range(0, NCH, 8):
                c1 = min(NCH, c0 + 8)
                P.dma('pool', lambda l=l, c0=c0, c1=c1: nc.gpsimd.dma_start(
                    out=wf_b[l, c0:c1].rearrange("c p k f -> (c p) (k f)"),
                    in_=wf_f[l, c0:c1].rearrange("c p k f -> (c p) (k f)")), w=[('wf', l, c) for c in range(c0, c1)])
            for oc in range(0, 8, 2):
                P.dma('pool', lambda l=l, oc=oc: nc.gpsimd.dma_start(
                    out=w2_b[l, oc:oc + 2].rearrange("c p k f -> (c p) (k f)"),
                    in_=w2_f[l, oc:oc + 2].rearrange("c p k f -> (c p) (k f)")), w=[('w2', l, oc), ('w2', l, oc + 1)])
            P.dma('pool', lambda l=l: nc.gpsimd.dma_start(
                out=wb_b[l].rearrange("c p k f -> (c p) (k f)"),
                in_=wb_f[l].rearrange("c p k f -> (c p) (k f)")), w=[('wb', l, c) for c in range(8)])
        P.barrier()

        def process_seq(g, b):
            T = TP if g == 'p' else TS
            PL = 0 if g == 'p' else PAST
            Ltot = PL + T
            NT = (Ltot + 127) // 128
            NTn = (T + 127) // 128
            JP = PL // 128
            QB = min(512, T)
            NQB = T // QB
            nq = min(128, T)
            NQT = T // nq
            topk = min(256, Ltot // 4)
            assert topk % 8 == 0 and T % 32 == 0 and PL % 128 == 0
            O = outs[g]
            rows_of = lambda j: min(128, T - 128 * j)

            ar16.reset(); ar32.reset()
            xst = [ar32.get(1024) for _ in range(2)]
            for j in range(NTn):
                rows = rows_of(j)
                xs_ = rr(xst, 'xst')
                xk = ('xst', id(xs_))
                P.dma('sp', lambda xs_=xs_, j=j, rows=rows: nc.sync.dma_start(
                    out=xs_[:rows, :], in_=x_in[g][b, j * 128:j * 128 + rows, :]), w=[xk])
                for half in range(2):
                    bank = rr([0, 1], 'tb')
                    for q in range(4):
                        kc = half * 4 + q
                        P.op('pe', lambda xs_=xs_, kc=kc, rows=rows, bank=bank, q=q: nc.tensor.transpose(
                            out=ps[:, bank, q * 128:q * 128 + rows], in_=xs_[:rows, kc * 128:(kc + 1) * 128],
                            identity=ident_f[:rows, :rows]), r=[xk, 'mats'], w=[('ps', bank)], acc=True)
                    src = ps[:, bank, :].rearrange("p (q r) -> p q r", q=4)[:, :, 0:rows]
                    dst = hT[:, half * 4:half * 4 + 4, j * 128:j * 128 + rows]
                    copy_op(evac_eng(), dst, src, r=[('ps', bank)], w=[('hT', j)])
            P.barrier()

            for l in range(2):
                layer(g, b, l, T, PL, Ltot, NT, NTn, JP, QB, NQB, nq, NQT, topk, O, rows_of)

        def layer(g, b, l, T, PL, Ltot, NT, NTn, JP, QB, NQB, nq, NQT, topk, O, rows_of):
            hT_all = [('hT', j) for j in range(NTn)]
            ar16.reset(); ar32.reset()
            wtm = [ar16.get(KC, 512) for _ in range(2)]
            wret = ar16.get(KC, 768)
            stg = [ar32.get(512) for _ in range(3)]
            rtm = [ar32.get(2, 256) for _ in range(2)]
            t1 = ar32.get(256); t2 = ar32.get(256)
            small = ar32.get(64)
            vb = [ar16.get(256) for _ in range(2)]
            kd = [ar16.get(256) for _ in range(2)]
            s0t = ar32.get(4, 64)
            stout = ar32.get(256)
            P.op('pool', lambda: nc.gpsimd.memset(LF[:], 0.0), w=['LF'])
            if PL > 0:
                P.dma('sp', lambda: nc.sync.dma_start(out=LF[:, 0:JP, :],
                                                      in_=c_flf[l, b].rearrange("(j p) h -> p j h", p=128)), w=['LF'])
            for gi in range(3):
                c0, gw = TMG[gi]
                wt = rr(wtm, 'wtm')
                wk = ('wtmb', id(wt))
                P.dma('sp', lambda wt=wt, c0=c0, gw=gw: nc.sync.dma_start(out=wt[:, :, 0:gw], in_=wtm_b[l, :, :, c0:c0 + gw]),
                      r=[('wtm', l)], w=[wk])
                for j in range(NTn):
                    rows = rows_of(j)
                    bank = rr([0, 1], 'p1b')
                    for kc in range(KC):
                        mm(ps[:rows, bank, 0:gw], hT[:, kc, j * 128:j * 128 + rows], wt[:, kc, 0:gw], kc == 0, kc == KC - 1,
                           r=[('hT', j), wk], w=[('ps', bank)], acc=True)
                    st_ = rr(stg, 'stg')
                    sk_ = ('stg', id(st_))
                    copy_op(evac_eng(), st_[:rows, 0:gw], ps[:rows, bank, 0:gw], r=[('ps', bank)], w=[sk_])
                    tok = slice(j * 128, j * 128 + rows)
                    if gi == 0:
                        P.dma('sp', lambda st_=st_, tok=tok, rows=rows: nc.sync.dma_start(out=O['sb_k'][l, b, tok, :], in_=st_[:rows, 0:256]),
                              r=[sk_], w=[('o_sbk', j)])
                        P.dma('sp', lambda st_=st_, tok=tok, rows=rows: nc.sync.dma_start(out=O['sb_v'][l, b, tok, :], in_=st_[:rows, 256:512]),
                              r=[sk_], w=[('o_sbv', j)])
                    elif gi == 1:
                        P.dma('sp', lambda st_=st_, tok=tok, rows=rows: nc.sync.dma_start(out=O['fox_k'][l, b, tok, :], in_=st_[:rows, 0:256]),
                              r=[sk_], w=[('o_fk', j)])
                        P.dma('sp', lambda st_=st_, tok=tok, rows=rows: nc.sync.dma_start(out=O['fox_v'][l, b, tok, :], in_=st_[:rows, 256:512]),
                              r=[sk_], w=[('o_fv', j)])
                    else:
                        P.dma('sp', lambda st_=st_, tok=tok, rows=rows: nc.sync.dma_start(out=O['dsa_k'][l, b, tok, :], in_=st_[:rows, 0:64]),
                              r=[sk_], w=[('o_dk', j)])
                        P.dma('sp', lambda st_=st_, tok=tok, rows=rows: nc.sync.dma_start(out=O['dsa_v'][l, b, tok, :], in_=st_[:rows, 64:128]),
                              r=[sk_], w=[('o_dv', j)])
                        P.dma('sp', lambda st_=st_, tok=tok, rows=rows: nc.sync.dma_start(out=O['dsa_ki'][l, b, tok, :], in_=st_[:rows, 128:192]),
                              r=[sk_], w=[('o_dki', j)])
                        sm = small
                        P.op('dve', lambda st_=st_, rows=rows: nc.vector.tensor_tensor(out=sm[:rows, 0:4], in0=st_[:rows, 192:196],
                                                                                        in1=bfor[:rows, l, :], op=ALU.add),
                             r=[sk_, 'bfor'], w=['small'])
                        P.op('act', lambda rows=rows: nc.scalar.activation(out=sm[:rows, 4:8], in_=sm[:rows, 0:4], func=AF.Exp, scale=-1.0),
                             r=['small'], w=['small2'])
                        P.op('act', lambda rows=rows: nc.scalar.activation(out=sm[:rows, 8:12], in_=sm[:rows, 4:8], func=AF.Ln, bias=1.0),
                             r=['small2'], w=['small3'])
                        P.op('dve', lambda rows=rows, j=j: nc.vector.tensor_scalar(out=LF[:rows, JP + j, :], in0=sm[:rows, 8:12], scalar1=-1.0,
                                                                                   scalar2=None, op0=ALU.mult),
                             r=['small3'], w=['LF'])
                        P.dma('sp', lambda tok=tok, rows=rows, j=j: nc.sync.dma_start(out=O['fox_lf'][l, b, tok, :], in_=LF[:rows, JP + j, :]),
                              r=['LF'], w=[('o_flf', j)])
                        P.op('dve', lambda st_=st_, rows=rows, j=j: nc.vector.tensor_scalar(out=IWs[:rows, j, :], in0=st_[:rows, 196:200], scalar1=0.0,
                                                                                            scalar2=2.0, op0=ALU.is_ge, op1=ALU.mult), r=[sk_], w=['IWs'])
                        P.op('dve', lambda rows=rows, j=j: nc.vector.tensor_scalar(out=IWs[:rows, j, :], in0=IWs[:rows, j, :], scalar1=-1.0,
                                                                                   scalar2=None, op0=ALU.add), r=['IWs'], w=['IWs'])
                        P.op('dve', lambda st_=st_, rows=rows, j=j: nc.vector.scalar_tensor_tensor(out=IW[:rows, j, :], in0=st_[:rows, 196:200], scalar=0.5,
                                                                                                   in1=IWs[:rows, j, :], op0=ALU.mult, op1=ALU.mult),
                             r=[sk_, 'IWs'], w=['IW'])
            c0, gw = TMG[3]
            P.dma('sp', lambda: nc.sync.dma_start(out=wret[:, :, :], in_=wtm_b[l, :, :, c0:c0 + gw]), r=[('wtm', l)], w=['wret'])
            for j in range(NTn):
                rows = rows_of(j)
                for kc in range(KC):
                    mm(ps[:rows, 2, 0:512], hT[:, kc, j * 128:j * 128 + rows], wret[:, kc, 0:512], kc == 0, kc == KC - 1,
                       r=[('hT', j), 'wret'], w=[('ps', 2)], acc=True)
                for kc in range(KC):
                    mm(ps[:rows, 3, 0:256], hT[:, kc, j * 128:j * 128 + rows], wret[:, kc, 512:768], kc == 0, kc == KC - 1,
                       r=[('hT', j), 'wret'], w=[('ps', 3)], acc=True)
                rt = rr(rtm, 'rtm'); rk = ('rtm', id(rt))
                P.dma('sp', lambda rt=rt, j=j, rows=rows: nc.sync.dma_start(
                    out=rt[:rows, :, :], in_=rottm[g][:, j * 128:j * 128 + rows, :].rearrange("a t f -> t a f")), w=[rk])
                v_ = rr(vb, 'vb'); vk = ('vb', id(v_))
                st_ = rr(stg, 'stg'); sk_ = ('stg', id(st_))
                copy_op('act', st_[:rows, 0:256], ps[:rows, 2, 0:256], r=[('ps', 2)], w=[sk_])
                P.dma('sp', lambda st_=st_, j=j, rows=rows: nc.sync.dma_start(out=retv_s[j * 128:j * 128 + rows, :], in_=st_[:rows, 0:256]),
                      r=[sk_], w=[('retv', j)])
                copy_op('pool', v_[:rows, :], st_[:rows, 0:256], r=[sk_], w=[vk])
                P.op('dve', lambda rt=rt, rows=rows: nc.vector.tensor_tensor(out=t1[:rows, :], in0=ps[:rows, 2, 256:512], in1=rt[:rows, 0, :], op=ALU.mult),
                     r=[('ps', 2), rk], w=['t1'])
                P.op('dve', lambda rt=rt, rows=rows: nc.vector.tensor_tensor(out=t2[:rows, :], in0=ps[:rows, 3, 0:256], in1=rt[:rows, 1, :], op=ALU.mult),
                     r=[('ps', 3), rk], w=['t2'])
                k_ = rr(kd, 'kd'); kk = ('kd', id(k_))
                P.op('pool', lambda k_=k_, rows=rows: nc.gpsimd.tensor_tensor(out=k_[:rows, :], in0=t1[:rows, :], in1=t2[:rows, :], op=ALU.add),
                     r=['t1', 't2'], w=[kk])
                for h in range(4):
                    mm(ps[0:64, 4, h * 64:(h + 1) * 64], k_[:rows, h * 64:(h + 1) * 64], v_[:rows, h * 64:(h + 1) * 64],
                       (j == 0 and h == 0), (j == NTn - 1 and h == 3), r=[kk, vk], w=[('ps', 4)], acc=True)
            if PL > 0:
                P.dma('sp', lambda: nc.sync.dma_start(out=s0t[0:64, :, :], in_=st_ret[l, b].rearrange("h d e -> d h e")), w=['s0t'])
                for h in range(4):
                    P.op('dve', lambda h=h: nc.vector.scalar_tensor_tensor(
                        out=stout[0:64, h * 64:(h + 1) * 64], in0=s0t[0:64, h, :], scalar=float(GAMMA[h] ** T),
                        in1=ps[0:64, 4, h * 64:(h + 1) * 64], op0=ALU.mult, op1=ALU.add), r=['s0t', ('ps', 4)], w=['stout'])
            else:
                copy_op('dve', stout[0:64, :], ps[0:64, 4, 0:256], r=[('ps', 4)], w=['stout'])
            P.dma('sp', lambda: nc.sync.dma_start(out=O['ret'][l, b].rearrange("h d e -> d h e"),
                                                  in_=stout[0:64, :].rearrange("p (h e) -> p h e", h=4)), r=['stout'], w=['o_ret'])
            P.barrier()

            for br in range(4):
                branch(br, g, b, l, T, PL, Ltot, NT, NTn, JP, QB, NQB, nq, NQT, topk, O, rows_of)
                P.barrier()
            phase3(g, b, l, T, NTn, O)
            P.barrier()

        def proj_fm(l, ch, T, QB, NQB, wbuf, evac, msl=None, bank_list=(0, 1), bkey='fmb'):
            wk = ('wch', id(wbuf))
            P.dma('sp', lambda: nc.sync.dma_start(out=wbuf[:, :, :], in_=wf_b[l, ch]), r=[('wf', l, ch)], w=[wk])
            for qi in range(NQB):
                bank = rr(list(bank_list), bkey)
                M = 128 if msl is None else 64
                for kc in range(KC):
                    lw = wbuf[:, kc, :] if msl is None else wbuf[:, kc, msl]
                    mm(ps[0:M, bank, 0:QB], lw, hT[:, kc, qi * QB:(qi + 1) * QB], kc == 0, kc == KC - 1,
                       r=[wk] + [('hT', jj) for jj in range(qi * QB // 128, max(qi * QB // 128 + 1, (qi + 1) * QB // 128))],
                       w=[('ps', bank)], acc=True)
                evac(qi, bank)

        def load_past_T(l, b, src, KT_dst_fn, ncol, JP, xst):
            for j in range(JP):
                xs_ = rr(xst, 'xst2'); xk = ('xst2', id(xs_))
                if isinstance(src, tuple):
                    P.dma('sp', lambda xs_=xs_, j=j: nc.sync.dma_start(out=xs_[:, 0:64], in_=src[0][l, b, j * 128:(j + 1) * 128, :]), w=[xk])
                    P.dma('sp', lambda xs_=xs_, j=j: nc.sync.dma_start(out=xs_[:, 64:128], in_=src[1][l, b, j * 128:(j + 1) * 128, :]), w=[xk + ('b',)])
                    rk = [xk, xk + ('b',)]
                    npair = 1
                else:
                    P.dma('sp', lambda xs_=xs_, j=j: nc.sync.dma_start(out=xs_[:, 0:256], in_=src[l, b, j * 128:(j + 1) * 128, :]), w=[xk])
                    rk = [xk]
                    npair = 2
                bank = rr([0, 1], 'ptb')
                for pr in range(npair):
                    P.op('pe', lambda xs_=xs_, pr=pr, bank=bank: nc.tensor.transpose(
                        out=ps[:, bank, pr * 128:(pr + 1) * 128], in_=xs_[:, pr * 128:(pr + 1) * 128], identity=ident_f),
                        r=rk + ['mats'], w=[('ps', bank)], acc=True)
                for pr in range(npair):
                    copy_op(evac_eng(), KT_dst_fn(pr, j), ps[:, bank, pr * 128:(pr + 1) * 128], r=[('ps', bank)], w=['KT'])

        def branch(br, g, b, l, T, PL, Ltot, NT, NTn, JP, QB, NQB, nq, NQT, topk, O, rows_of):
            ar16.reset(); ar32.reset()
            Lp = NT * 128
            kd_ = {0: 0, 1: 2, 2: 1, 3: 3}[br]
            wch = [ar16.get(KC, 128) for _ in range(3)]
            ybuf = ar16.get(4, QB)
            xst = [ar32.get(256) for _ in range(2)]
            srow = ar32.get(512); rbc = ar32.get(512)
            if kd_ in (0, 1, 2):
                Pk = 0 if kd_ == 2 else PL
                NTk = NTn if kd_ == 2 else NT
                QT = ar16.get(2, T)
                KT = ar16.get(2, NTk * 128)
                V = ar16.get(NTk, 5, 65)
                Vflat = V.rearrange("p j h d -> p j (h d)")
                Pt = [ar16.get(QB) for _ in range(4)]
                P.op('pool', lambda: nc.gpsimd.memset(KT[:, :, :], 0.0), w=['KT'])
                Vall = [('V', j_) for j_ in range(NTk)]
                P.op('pool', lambda: nc.gpsimd.memset(V[:, :, :, :], 0.0), w=Vall)
                if kd_ == 1:
                    P.op('pool', lambda: nc.gpsimd.memset(V[:, :, 0:4, 64:65], 1.0), w=Vall)
                if kd_ == 0:
                    qc, kc_, vout, kcache, vcache = CH_SB, CH_SB + 2, 'sb_v', c_sbk, c_sbv
                    vkeys = [('o_sbv', j) for j in range(NTn)]
                elif kd_ == 1:
                    qc, kc_, vout, kcache, vcache = CH_FOX, CH_FOX + 2, 'fox_v', c_fk, c_fv
                    vkeys = [('o_fv', j) for j in range(NTn)]
                if kd_ in (0, 1):
                    for pr in range(2):
                        proj_fm(l, qc + pr, T, QB, NQB, rr(wch, 'wch'),
                                lambda qi, bank, pr=pr: copy_op(evac_eng(), QT[:, pr, qi * QB:(qi + 1) * QB], ps[:, bank, 0:QB],
                                                                r=[('ps', bank)], w=['QT'], scale=0.125))
                        proj_fm(l, kc_ + pr, T, QB, NQB, rr(wch, 'wch'),
                                lambda qi, bank, pr=pr: copy_op(evac_eng(), KT[:, pr, PL + qi * QB:PL + (qi + 1) * QB], ps[:, bank, 0:QB],
                                                                r=[('ps', bank)], w=['KT']))
                    if PL > 0:
                        load_past_T(l, b, kcache, lambda pr, j: KT[:, pr, j * 128:(j + 1) * 128], 256, JP, xst)
                        for j in range(JP):
                            P.dma('pool', lambda j=j: nc.gpsimd.dma_start(
                                out=V[:, j, 0:4, 0:64], in_=vcache[l, b, j * 128:(j + 1) * 128, :].rearrange("t (h d) -> t h d", h=4)), w=[('V', j)])
                    for j in range(NTn):
                        rows = rows_of(j)
                        P.dma('pool', lambda j=j, rows=rows: nc.gpsimd.dma_start(
                            out=V[0:rows, JP + j, 0:4, 0:64], in_=O[vout][l, b, j * 128:j * 128 + rows, :].rearrange("t (h d) -> t h d", h=4)),
                            r=[vkeys[j]], w=[('V', JP + j)])
                else:
                    rtab = [ar32.get(QB) for _ in range(4)]
                    tq1 = ar32.get(QB); tq2 = ar32.get(QB)
                    for which, base, dst in ((0, CH_RET, QT), (1, CH_RET + 4, KT)):
                        for pr in range(2):
                            wa = rr(wch, 'wch'); wka = ('wch', id(wa))
                            wb_ = rr(wch, 'wch'); wkb = ('wch', id(wb_))
                            P.dma('sp', lambda wa=wa, ch=base + pr: nc.sync.dma_start(out=wa[:, :, :], in_=wf_b[l, ch]), r=[('wf', l, base + pr)], w=[wka])
                            P.dma('sp', lambda wb_=wb_, ch=base + 2 + pr: nc.sync.dma_start(out=wb_[:, :, :], in_=wf_b[l, ch]), r=[('wf', l, base + 2 + pr)], w=[wkb])
                            for qi in range(NQB):
                                blk = slice(qi * QB, (qi + 1) * QB)
                                hk = [('hT', jj) for jj in range(qi * QB // 128, max(qi * QB // 128 + 1, (qi + 1) * QB // 128))]
                                bA = rr([0, 1], 'rA'); bB = rr([2, 3], 'rB')
                                for kc in range(KC):
                                    mm(ps[:, bA, 0:QB], wa[:, kc, :], hT[:, kc, blk], kc == 0, kc == KC - 1, r=[wka] + hk, w=[('ps', bA)], acc=True)
                                for kc in range(KC):
                                    mm(ps[:, bB, 0:QB], wb_[:, kc, :], hT[:, kc, blk], kc == 0, kc == KC - 1, r=[wkb] + hk, w=[('ps', bB)], acc=True)
                                tc_ = rr(rtab, 'rtab'); tck = ('rtab', id(tc_))
                                ts_ = rr(rtab, 'rtab'); tsk = ('rtab', id(ts_))
                                P.dma('sp', lambda tc_=tc_, pr=pr, blk=blk, ti=which * 2: nc.sync.dma_start(out=tc_[:, :], in_=rotfm[g][ti, :, pr, blk]), w=[tck])
                                P.dma('sp', lambda ts_=ts_, pr=pr, blk=blk, ti=which * 2 + 1: nc.sync.dma_start(out=ts_[:, :], in_=rotfm[g][ti, :, pr, blk]), w=[tsk])
                                P.op('dve', lambda tc_=tc_, bA=bA: nc.vector.tensor_tensor(out=tq1[:, :], in0=ps[:, bA, 0:QB], in1=tc_[:, :], op=ALU.mult),
                                     r=[('ps', bA), tck], w=['tq1'])
                                P.op('dve', lambda ts_=ts_, bB=bB: nc.vector.tensor_tensor(out=tq2[:, :], in0=ps[:, bB, 0:QB], in1=ts_[:, :], op=ALU.mult),
                                     r=[('ps', bB), tsk], w=['tq2'])
                                P.op('pool', lambda dst=dst, pr=pr, blk=blk: nc.gpsimd.tensor_tensor(out=dst[:, pr, blk], in0=tq1[:, :], in1=tq2[:, :], op=ALU.add),
                                     r=['tq1', 'tq2'], w=['QT' if which == 0 else 'KT'])
                    for j in range(NTn):
                        rows = rows_of(j)
                        P.dma('pool', lambda j=j, rows=rows: nc.gpsimd.dma_start(
                            out=V[0:rows, j, 0:4, 0:64], in_=retv_s[j * 128:j * 128 + rows, :].rearrange("t (h d) -> t h d", h=4)),
                            r=[('retv', j)], w=[('V', j)])
                    wg = [ar16.get(KC, 128) for _ in range(2)]
                    for pr in range(2):
                        P.dma('sp', lambda pr=pr: nc.sync.dma_start(out=wg[pr][:, :, :], in_=wf_b[l, CH_RET + 8 + pr]), r=[('wf', l, CH_RET + 8 + pr)], w=[('wg', pr)])
                    fsets = [[(ar32.get(QB), 'fA%d' % i_) for i_ in range(4)], [(rtab[i_], ('rtab', id(rtab[i_]))) for i_ in range(4)]]
                    fb16 = [(ar16.get(QB), 'fb16_%d' % i_) for i_ in range(2)]
                    if PL > 0:
                        assert NQB == 1
                        s0f = ar32.get(2, 64)
                        s0b = ar16.get(2, 64)
                        P.dma('sp', lambda: nc.sync.dma_start(out=s0f[:, :, :], in_=st_ret[l, b].rearrange("(hp hh) d e -> (hh d) hp e", hh=2)), w=['s0f'])
                        for h in range(4):
                            hp, hh = h // 2, h % 2
                            P.op('dve', lambda hp=hp, hh=hh, h=h: nc.vector.tensor_scalar(
                                out=s0b[hh * 64:(hh + 1) * 64, hp, :], in0=s0f[hh * 64:(hh + 1) * 64, hp, :], scalar1=float(GAMMA[h]), scalar2=None,
                                op0=ALU.mult), r=['s0f'], w=['s0b'])
                if kd_ == 0:
                    e32 = [ar32.get(QB) for _ in range(4)]
                    spb = [ar16.get(QB) for _ in range(6)]
                    exb = [ar16.get(QB) for _ in range(3)]
                if kd_ == 1:
                    biasF = ar32.get(NQB, NT * 4)
                    fsets = [[(ar32.get(QB), 'fF%d_%d' % (s_, i_)) for i_ in range(3)] for s_ in range(2)]
                    cst = ar32.get(NT * 4)
                    P.op('pe', lambda: nc.tensor.matmul(ps[:, 2, 0:NT * 4], lhsT=tri, rhs=LF[:, 0:NT, :].rearrange("p j h -> p (j h)"), start=True, stop=True),
                         r=['LF', 'mats'], w=[('ps', 2)])
                    copy_op('dve', cst[:, :], ps[:, 2, 0:NT * 4], r=[('ps', 2)], w=['cst'])
                    P.op('pe', lambda: nc.tensor.matmul(ps[:, 3, 0:NT * 4], lhsT=mats[:, 5, :], rhs=cst[:, :], start=True, stop=True),
                         r=['cst', 'mats'], w=[('ps', 3)])
                    tot = ar32.get(NT * 4); pre = ar32.get(NT * 4)
                    copy_op('dve', tot[:, :], ps[:, 3, 0:NT * 4], r=[('ps', 3)], w=['tot'])
                    totv = tot.rearrange("p (j h) -> p j h", h=4)
                    prev = pre.rearrange("p (j h) -> p j h", h=4)
                    for h in range(4):
                        P.op('dve', lambda h=h: nc.vector.tensor_tensor_scan(out=prev[:, :, h], data0=mats[:, 4, 0:NT], data1=totv[:, :, h], initial=0.0,
                                                                             op0=ALU.mult, op1=ALU.add), r=['tot', 'mats'], w=['pre'])
                    P.op('dve', lambda: nc.vector.tensor_tensor(out=pre[:, :], in0=pre[:, :], in1=tot[:, :], op=ALU.subtract), r=['pre', 'tot'], w=['pre'])
                    P.op('dve', lambda: nc.vector.tensor_tensor(out=CS[:, 0:NT, :].rearrange("p j h -> p (j h)"), in0=cst[:, :], in1=pre[:, :], op=ALU.add),
                         r=['cst', 'pre'], w=['CS'])
                    for qi in range(NQB):
                        tmid = PL + qi * QB + QB // 2
                        jm, rm = tmid // 128, tmid % 128
                        selm = {0: 6, 16: 7, 127: 5}[rm]
                        P.op('pe', lambda jm=jm, selm=selm: nc.tensor.matmul(ps[:, 2, 0:4], lhsT=mats[:, selm, :], rhs=CS[:, jm, :], start=True, stop=True),
                             r=['CS', 'mats'], w=[('ps', 2)])
                        copy_op('dve', srow[:, 0:4], ps[:, 2, 0:4], r=[('ps', 2)], w=['srow'])
                        P.op('dve', lambda qi=qi: nc.vector.tensor_tensor(
                            out=biasF[:, qi, :].rearrange("p (j h) -> p j h", h=4), in0=srow[:, 0:4].unsqueeze(1).to_broadcast([128, NT, 4]),
                            in1=CS[:, 0:NT, :], op=ALU.subtract), r=['srow', 'CS'], w=['biasF'])

                ybufs = [ybuf, ar16.get(4, QB)]
                groups = []
                for qi in range(NQB):
                    q0 = qi * QB
                    qg = Pk + q0
                    hsets = [(0, 1), (2, 3)]
                    for hs in hsets:
                        lists = []
                        for h in hs:
                            tl = []
                            for j in range(NTk):
                                d = 128 * j - qg
                                if kd_ == 0:
                                    if d >= QB - 1:
                                        continue
                                else:
                                    if d > QB - 1:
                                        continue
                                tl.append((j, d))
                            if kd_ == 0:
                                tl = tl[::-1]
                            lists.append([dict(qi=qi, q0=q0, h=h, j=j, d=d, n=n, last=(n == len(tl) - 1)) for n, (j, d) in enumerate(tl)])
                        for tup in zip(*lists):
                            groups.extend(tup)
                tasks = groups
                gstate = {}

                def stA(t):
                    h = t['h']; hp, hh = h // 2, h % 2
                    prt = slice(hh * 64, hh * 64 + 64)
                    zb = rr([0, 1], 'zb') if kd_ == 0 else rr([0, 1, 2, 3], 'zb')
                    t['zb'] = zb
                    q0 = t['q0']; j = t['j']
                    mm(ps[:, zb, 0:QB], KT[prt, hp, j * 128:(j + 1) * 128], QT[prt, hp, q0:q0 + QB], True, True,
                       r=['KT', 'QT'], w=[('ps', zb)])

                def stB(t):
                    zb = t['zb']; j = t['j']; d = t['d']; h = t['h']; qi = t['qi']
                    diag = d > -128
                    if kd_ == 1:
                        pt = rr(Pt, 'Pt'); pk = ('Pt', id(pt))
                        t['pt'] = (pt, pk)
                        P.op('act', lambda: nc.scalar.activation(
                            out=pt[:, :], in_=ps[:, zb, 0:QB], func=AF.Exp, bias=biasF[:, qi, j * 4 + h:j * 4 + h + 1]),
                            r=[('ps', zb), 'biasF'], w=[pk])
                        if diag:
                            P.op('pool', lambda: nc.gpsimd.tensor_tensor(out=pt[:, :], in0=pt[:, :], in1=masks[:, 0, 384 - d:384 - d + QB], op=ALU.mult),
                                 r=[pk, 'masks'], w=[pk])
                    elif kd_ == 2:
                        pt = rr(Pt, 'Pt'); pk = ('Pt', id(pt))
                        t['pt'] = (pt, pk)
                        cij = GAMMA[h] ** (t['q0'] - 128 * j - 127)
                        copy_op('dve', pt[:, :], ps[:, zb, 0:QB], r=[('ps', zb)], w=[pk], scale=cij)
                        if diag:
                            P.op('pool', lambda: nc.gpsimd.tensor_tensor(out=pt[:, :], in0=pt[:, :], in1=masks[:, 0, 384 - d:384 - d + QB], op=ALU.mult),
                                 r=[pk, 'masks'], w=[pk])
                    else:
                        e_ = rr(e32, 'e32'); ek = ('e32', id(e_))
                        P.op('act', lambda: nc.scalar.activation(out=e_[:, :], in_=ps[:, zb, 0:QB], func=AF.Exp), r=[('ps', zb)], w=[ek])
                        if diag:
                            P.op('pool', lambda: nc.gpsimd.tensor_tensor(out=e_[:, :], in0=e_[:, :], in1=masks[:, 1, 384 - d:384 - d + QB], op=ALU.mult),
                                 r=[ek, 'masks'], w=[ek])
                        sp_ = rr(spb, 'spb'); spk = ('spb', id(sp_))
                        P.op('act', lambda: nc.scalar.activation(out=sp_[:, :], in_=e_[:, :], func=AF.Ln, bias=1.0), r=[ek], w=[spk])
                        t['e'] = (e_, ek); t['sp'] = (sp_, spk)

                def stC1(t):
                    h = t['h']
                    xb = 2 + (h % 2)
                    key = (t['qi'], h)
                    prev_sp = gstate.get(('sp', key))
                    sp_, spk = t['sp']
                    e_, ek = t['e']
                    if prev_sp is not None:
                        mm(ps[:, xb, 0:QB], negL, prev_sp[0][:, :], False, False, r=[prev_sp[1], 'matb'], w=[('ps', xb)], acc=True)
                    mm(ps[:, xb, 0:QB], negU, sp_[:, :], t['n'] == 0, True, r=[spk, 'matb'], w=[('ps', xb)], acc=True)
                    gstate[('sp', key)] = (sp_, spk)
                    ex_ = rr(exb, 'exb'); exk = ('exb', id(ex_))
                    P.op('act', lambda: nc.scalar.activation(out=ex_[:, :], in_=ps[:, xb, 0:QB], func=AF.Exp), r=[('ps', xb)], w=[exk])
                    pt = rr(Pt, 'Pt'); pk = ('Pt', id(pt))
                    t['pt'] = (pt, pk)
                    P.op('pool', lambda: nc.gpsimd.tensor_tensor(out=pt[:, :], in0=e_[:, :], in1=ex_[:, :], op=ALU.mult),
                         r=[ek, exk], w=[pk])

                def stC(t):
                    h = t['h']; hp, hh = h // 2, h % 2
                    prt = slice(hh * 64, hh * 64 + 64)
                    qi = t['qi']; q0 = t['q0']; j = t['j']
                    key = (qi, h)
                    ob = 4 + (h % 2)
                    M = 65 if kd_ == 1 else 64
                    started = t['n'] > 0
                    if kd_ == 2 and PL > 0 and t['n'] == 0:
                        mm(ps[0:64, ob, 0:QB], s0b[prt, hp, :], QT[prt, hp, q0:q0 + QB], True, False, r=['s0b', 'QT'], w=[('ps', ob)], acc=True)
                        started = True
                    pt, pk = t['pt']
                    if kd_ == 2 and PL > 0:
                        mm(ps[0:M, ob, 0:QB], V[:, j, h, 0:M], pt[:, :], not started, t['last'], r=[('V', j), pk], w=[('ps', ob)], acc=True)
                    else:
                        mm(ps[:, ob, 0:QB], Vflat[:, j, h * 65:h * 65 + 128], pt[:, :], not started, t['last'], r=[('V', j), pk], w=[('ps', ob)], acc=True)
                    if not t['last']:
                        return
                    yb_ = ybufs[qi % 2]
                    yk = ('ybuf', qi % 2)
                    stages = []

                    def fin_done():
                        gstate[('done', qi)] = gstate.get(('done', qi), 0) + 1
                        if gstate[('done', qi)] == 4:
                            P.dma('sp', lambda: nc.sync.dma_start(out=ysc[br, :, :, q0:q0 + QB], in_=yb_[0:64, :, :]), r=[yk], w=[('ysc', br, qi)])
                            if debug:
                                P.dma('sp', lambda: nc.sync.dma_start(out=ydbg[0 if g == 'p' else 1, l, br, :, :, q0:q0 + QB], in_=yb_[0:64, :, :]),
                                      r=[yk], w=[('ydbg', br, qi, l, g)])
                    if kd_ == 0:
                        copy_op('dve', yb_[0:64, h, :], ps[0:64, ob, 0:QB], r=[('ps', ob)], w=[yk])
                        fin_done()
                    elif kd_ == 1:
                        fsi = gstate.get('fsel', 0) % 2
                        for st_l in list(pending):
                            if st_l[0] == fsi:
                                while len(st_l) > 1:
                                    st_l.pop(1)()
                                pending.remove(st_l)
                        fs = fsets[fsi]; gstate['fsel'] = gstate.get('fsel', 0) + 1
                        (sr, srk), (lnr, lnk), (rb, rbk) = fs

                        def f1():
                            copy_op('act', sr[64:65, 0:QB], ps[64:65, ob, 0:QB], r=[('ps', ob)], w=[srk])
                            P.op('pe', lambda: nc.tensor.matmul(ps[0:64, 6, 0:QB], lhsT=mats[64:65, 4, 0:64], rhs=sr[64:65, 0:QB], start=True, stop=True),
                                 r=[srk, 'mats'], w=[('ps', 6)])
                            P.op('dve', lambda: nc.vector.reciprocal(out=rb[0:64, 0:QB], in_=ps[0:64, 6, 0:QB]), r=[('ps', 6)], w=[rbk])

                        def f2():
                            P.op('dve', lambda: nc.vector.tensor_tensor(out=yb_[0:64, h, :], in0=ps[0:64, ob, 0:QB], in1=rb[0:64, 0:QB], op=ALU.mult),
                                 r=[('ps', ob), rbk], w=[yk])
                            fin_done()
                        f1()
                        stages = [fsi, f2]
                    else:
                        fsi = gstate.get('fsel', 0) % 2
                        for st_l in list(pending):
                            if st_l[0] == fsi:
                                while len(st_l) > 1:
                                    st_l.pop(1)()
                                pending.remove(st_l)
                        fs = fsets[fsi]
                        ob16, ob16k = fb16[fsi]
                        gstate['fsel'] = gstate.get('fsel', 0) + 1
                        (osb, osk), (xc, xck), (sd, sdk), (sg, sgk) = fs
                        ones_b = matb[0:64, 3, 0:64]

                        def f1():
                            copy_op('act', osb[0:64, :], ps[0:64, ob, 0:QB], r=[('ps', ob)], w=[osk])
                            copy_op('act', ob16[0:64, :], ps[0:64, ob, 0:QB], r=[('ps', ob)], w=[ob16k])
                            P.op('pe', lambda: nc.tensor.matmul(ps[0:64, 6, 0:QB], lhsT=ones_b, rhs=ob16[0:64, :], start=True, stop=True),
                                 r=[ob16k, 'matb'], w=[('ps', 6)])
                            P.op('dve', lambda: nc.vector.scalar_tensor_tensor(out=xc[0:64, :], in0=ps[0:64, 6, 0:QB], scalar=-1.0 / 64, in1=osb[0:64, :],
                                                                               op0=ALU.mult, op1=ALU.add), r=[('ps', 6), osk], w=[xck])

                        def f2():
                            P.op('dve', lambda: nc.vector.tensor_tensor(out=ob16[0:64, :], in0=xc[0:64, :], in1=xc[0:64, :], op=ALU.mult), r=[xck], w=[ob16k])
                            P.op('pe', lambda: nc.tensor.matmul(ps[0:64, 6, 0:QB], lhsT=ones_b, rhs=ob16[0:64, :], start=True, stop=True),
                                 r=[ob16k, 'matb'], w=[('ps', 6)])
                            P.op('act', lambda: nc.scalar.activation(out=sd[0:64, :], in_=ps[0:64, 6, 0:QB], func=AF.Ln, scale=1.0 / 64, bias=eps_t[0:64, 0:1]),
                                 r=[('ps', 6), 'eps'], w=[sdk])

                        def f3():
                            P.op('act', lambda: nc.scalar.activation(out=sd[0:64, :], in_=sd[0:64, :], func=AF.Exp, scale=-0.5), r=[sdk], w=[sdk])
                            P.op('dve', lambda: nc.vector.tensor_tensor(out=xc[0:64, :], in0=xc[0:64, :], in1=sd[0:64, :], op=ALU.mult), r=[xck, sdk], w=[xck])
                            for kc in range(KC):
                                mm(ps[0:64, 6, 0:QB], wg[hp][:, kc, hh * 64:(hh + 1) * 64], hT[:, kc, q0:q0 + QB], kc == 0, kc == KC - 1,
                                   r=[('wg', hp)] + [('hT', jj) for jj in range(NTn)], w=[('ps', 6)], acc=True)
                            P.op('act', lambda: nc.scalar.activation(out=sg[0:64, :], in_=ps[0:64, 6, 0:QB], func=AF.Exp, scale=-1.0), r=[('ps', 6)], w=[sgk])
                            copy_op('dve', sd[0:64, :], ps[0:64, 6, 0:QB], r=[('ps', 6), xck], w=[sdk])
                            P.op('act', lambda: nc.scalar.activation(out=sg[0:64, :], in_=sg[0:64, :], func=AF.Ln, bias=1.0), r=[sgk], w=[sgk])

                        def f4():
                            P.op('act', lambda: nc.scalar.activation(out=sg[0:64, :], in_=sg[0:64, :], func=AF.Exp, scale=-1.0), r=[sgk], w=[sgk])
                            P.op('dve', lambda: nc.vector.tensor_tensor(out=sd[0:64, :], in0=sd[0:64, :], in1=sg[0:64, :], op=ALU.mult),
                                 r=[sdk, sgk], w=[sdk])
                            P.op('dve', lambda: nc.vector.tensor_tensor(out=yb_[0:64, h, :], in0=xc[0:64, :], in1=sd[0:64, :], op=ALU.mult),
                                 r=[xck, sdk], w=[yk])
                            fin_done()
                        f1()
                        stages = [fsi, f2, f3, f4]
                    if stages:
                        pending.append(stages)

                pending = []
                NTK = len(tasks)
                if kd_ == 0:
                    steps = [[t_] for t_ in tasks]
                else:
                    steps = [tasks[i_:i_ + 2] for i_ in range(0, NTK, 2)]
                NS = len(steps)
                for s_ in range(NS + 2):
                    if s_ < NS:
                        for t_ in steps[s_]:
                            stA(t_)
                        for t_ in steps[s_]:
                            stB(t_)
                    if kd_ == 0 and 0 <= s_ - 1 < NS:
                        for t_ in steps[s_ - 1]:
                            stC1(t_)
                    sk_ = 2 if kd_ == 0 else 1
                    if 0 <= s_ - sk_ < NS:
                        for t_ in steps[s_ - sk_]:
                            stC(t_)
                    for st_l in list(pending):
                        st_l.pop(1)()
                        if len(st_l) == 1:
                            pending.remove(st_l)
                while pending:
                    for st_l in list(pending):
                        st_l.pop(1)()
                        if len(st_l) == 1:
                            pending.remove(st_l)
                return

            NIT = 20
            QT = ar16.get(NQT, 4, nq)
            KT = ar16.get(1, Lp)
            Vd = ar16.get(NT, 65)
            Pt = [ar16.get(4 * nq) for _ in range(4)]
            negsel = [ar16.get(Lp) for _ in range(2)]
            I4 = ar16.get(4 * nq)
            score = ar32.get(Lp)
            tmp = ar32.get(512)
            bs = ar32.get(8)
            wtab = ar32.get(NIT + 1)
            wtab2 = ar32.get(NIT + 1)
            P.op('pool', lambda: nc.gpsimd.memset(KT[:, :, :], 0.0), w=['KT'])
            P.op('pool', lambda: nc.gpsimd.memset(Vd[:, :, :], 0.0), w=['V'])
            P.op('pool', lambda: nc.gpsimd.memset(Vd[:, :, 64:65], 1.0), w=['V'])
            for h in range(4):
                P.op('dve', lambda h=h: nc.vector.tensor_copy(out=I4[0:nq, h * nq:(h + 1) * nq], in_=ident_b[0:nq, 0:nq]), r=['matb'], w=['I4'])
            tpb = QB // nq
            for h in range(4):
                proj_fm(l, CH_DSA + h, T, QB, NQB, rr(wch, 'wch'),
                        lambda qi, bank, h=h: copy_op(evac_eng(), QT[:, qi * tpb:(qi + 1) * tpb, h, :],
                                                      ps[:, bank, 0:QB].rearrange("p (t q) -> p t q", t=tpb), r=[('ps', bank)], w=['QT'], scale=0.125))
            proj_fm(l, CH_DSA + 4, T, QB, NQB, rr(wch, 'wch'),
                    lambda qi, bank: copy_op(evac_eng(), KT[:, 0, PL + qi * QB:PL + (qi + 1) * QB], ps[:, bank, 0:QB], r=[('ps', bank)], w=['KT']))
            if PL > 0:
                load_past_T(l, b, (c_dk, c_dki), lambda pr, j: KT[:, 0, j * 128:(j + 1) * 128], 128, JP, xst)
                P.dma('pool', lambda: nc.gpsimd.dma_start(out=Vd[:, 0:JP, 0:64], in_=c_dv[l, b].rearrange("(j p) d -> p j d", p=128)), r=['V'], w=['V'])
            vkeys = [('o_dv', j) for j in range(NTn)]
            if T >= 128:
                P.dma('pool', lambda: nc.gpsimd.dma_start(out=Vd[:, JP:JP + NTn, 0:64], in_=O['dsa_v'][l, b].rearrange("(j p) d -> p j d", p=128)),
                      r=vkeys + ['V'], w=['V'])
            else:
                P.dma('pool', lambda: nc.gpsimd.dma_start(out=Vd[0:T, JP, 0:64], in_=O['dsa_v'][l, b]), r=vkeys + ['V'], w=['V'])
            ybufs = [ybuf, ar16.get(4, QB)]

            def geom(qt):
                tg0 = PL + qt * nq
                nk = min(Ltot, ((tg0 + nq - 1) // 64 + 1) * 64)
                nkt = (nk + 127) // 128
                return nk, nkt, nkt * 128

            tmps = [tmp, ar32.get(512)]

            def prep(qt):
                nk, nkt, nkp = geom(qt)
                ns = negsel[qt % 2]; nsk = ('negsel', qt % 2)
                for kb in range((nkp + 511) // 512):
                    kw = min(512, nkp - kb * 512)
                    for h in range(4):
                        mm(ps[0:nq, h, 0:kw], QT[64:128, qt, h, :], KT[64:128, 0, kb * 512:kb * 512 + kw], True, True, r=['QT', 'KT'], w=[('ps', h)])
                    ksl = slice(kb * 512, kb * 512 + kw)
                    for h in range(4):
                        t_ = rr(tmps, 'tmps'); tk = ('tmps', id(t_))
                        P.op('act', lambda t_=t_, h=h, kw=kw: nc.scalar.activation(out=t_[0:nq, 0:kw], in_=ps[0:nq, h, 0:kw], func=AF.Relu, scale=IW[0:nq, qt, h:h + 1]),
                             r=[('ps', h), 'IW'], w=[tk])
                        if h == 0:
                            P.op('dve', lambda t_=t_, ksl=ksl, kw=kw: nc.vector.tensor_scalar(out=score[0:nq, ksl], in0=t_[0:nq, 0:kw], scalar1=IWs[0:nq, qt, 0:1], scalar2=None,
                                                                                              op0=ALU.mult), r=[tk, 'IWs'], w=['score'])
                        else:
                            P.op('dve', lambda t_=t_, ksl=ksl, kw=kw, h=h: nc.vector.scalar_tensor_tensor(out=score[0:nq, ksl], in0=t_[0:nq, 0:kw], scalar=IWs[0:nq, qt, h:h + 1],
                                                                                                         in1=score[0:nq, ksl], op0=ALU.mult, op1=ALU.add),
                                 r=[tk, 'IWs', 'score'], w=['score'])
                    yield
                if nkp > nk:
                    P.op('pool', lambda: nc.gpsimd.memset(score[0:nq, nk:nkp], NEG), r=['score'], w=['score'])
                if nq == 128:
                    P.op('pool', lambda: nc.gpsimd.memset(score[0:64, nk - 64:nk], NEG), r=['score'], w=['score'])
                    ncommon = nk - 64
                else:
                    ncommon = nk
                if nk <= topk:
                    P.op('dve', lambda: nc.vector.tensor_scalar(out=ns[0:nq, 0:nkp], in0=score[0:nq, 0:nkp], scalar1=-1.0e29, scalar2=-30000.0,
                                                                op0=ALU.is_lt, op1=ALU.mult), r=['score'], w=[nsk])
                    return
                assert ncommon >= topk
                n1 = nkp if nkp < 768 else ((nkp * 9 // 16) // 64) * 64
                n2 = nkp - n1
                nmin = min(ncommon, max(topk, 256))
                P.op('dve', lambda: nc.vector.tensor_reduce(out=bs[0:nq, 0:1], in_=score[0:nq, 0:nmin], axis=mybir.AxisListType.X, op=ALU.min),
                     r=['score'], w=['b_lo'])
                P.op('dve', lambda: nc.vector.tensor_reduce(out=bs[0:nq, 1:2], in_=score[0:nq, 0:nkp], axis=mybir.AxisListType.X, op=ALU.max),
                     r=['score'], w=['b_hi'])
                P.op('dve', lambda: nc.vector.tensor_tensor(out=bs[0:nq, 1:2], in0=bs[0:nq, 1:2], in1=bs[0:nq, 0:1], op=ALU.subtract), r=['b_hi', 'b_lo'], w=['b_hi'])
                P.op('dve', lambda: nc.vector.tensor_scalar(out=wtab[0:nq, 0:NIT + 1], in0=mats[0:nq, 8, 0:NIT + 1], scalar1=bs[0:nq, 1:2], scalar2=None, op0=ALU.mult),
                     r=['b_hi', 'mats'], w=['wtab'])
                P.op('dve', lambda: nc.vector.tensor_scalar(out=wtab2[0:nq, 0:NIT + 1], in0=mats[0:nq, 8, 0:NIT + 1], scalar1=bs[0:nq, 1:2], scalar2=2.0, op0=ALU.mult, op1=ALU.mult),
                     r=['b_hi', 'mats'], w=['wtab2'])
                P.op('dve', lambda: nc.vector.tensor_tensor(out=bs[0:nq, 2:3], in0=bs[0:nq, 0:1], in1=wtab[0:nq, 0:1], op=ALU.add), r=['b_lo', 'wtab'], w=['b_mid'])
                yield
                thr2 = 2.0 * (float(topk) - 0.5) - n2
                for it in range(NIT):
                    if n2 > 0:
                        P.op('act', lambda: nc.scalar.activation(out=ns[0:nq, n1:nkp], in_=score[0:nq, n1:nkp], func=AF.Sign, bias=bs[0:nq, 2:3], scale=-1.0,
                                                                 accum_out=bs[0:nq, 5:6]), r=['score', 'b_mid'], w=['b_acc', (nsk, 'b')])
                    P.op('dve', lambda: nc.vector.tensor_scalar(out=ns[0:nq, 0:n1], in0=score[0:nq, 0:n1], scalar1=bs[0:nq, 2:3], scalar2=0.0,
                                                                op0=ALU.is_ge, op1=ALU.add, accum_out=bs[0:nq, 3:4]), r=['score', 'b_mid'], w=['b_cnt', (nsk, 'a')])
                    if n2 > 0:
                        P.op('dve', lambda: nc.vector.scalar_tensor_tensor(out=bs[0:nq, 3:4], in0=bs[0:nq, 3:4], scalar=2.0, in1=bs[0:nq, 5:6],
                                                                           op0=ALU.mult, op1=ALU.subtract), r=['b_acc', 'b_cnt'], w=['b_cnt'])
                        thr_ = thr2
                    else:
                        thr_ = float(topk) - 0.5
                    P.op('dve', lambda it=it, thr_=thr_: nc.vector.tensor_scalar(out=bs[0:nq, 4:5], in0=bs[0:nq, 3:4], scalar1=thr_, scalar2=wtab2[0:nq, it + 1:it + 2],
                                                                                op0=ALU.is_ge, op1=ALU.mult), r=['b_cnt', 'wtab2'], w=['b_gw'])
                    P.op('dve', lambda it=it: nc.vector.scalar_tensor_tensor(out=bs[0:nq, 2:3], in0=bs[0:nq, 2:3], scalar=wtab[0:nq, it + 1:it + 2], in1=bs[0:nq, 4:5],
                                                                             op0=ALU.subtract, op1=ALU.add), r=['b_mid', 'b_gw', 'wtab'], w=['b_mid'])
                    yield
                P.op('dve', lambda: nc.vector.tensor_tensor(out=bs[0:nq, 0:1], in0=bs[0:nq, 2:3], in1=wtab[0:nq, NIT:NIT + 1], op=ALU.subtract), r=['b_mid', 'wtab'], w=['b_lo'])
                P.op('dve', lambda: nc.vector.tensor_scalar(out=ns[0:nq, 0:nkp], in0=score[0:nq, 0:nkp], scalar1=bs[0:nq, 0:1], scalar2=-30000.0,
                                                            op0=ALU.is_lt, op1=ALU.mult), r=['score', 'b_lo', (nsk, 'a'), (nsk, 'b')], w=[nsk, (nsk, 'a'), (nsk, 'b')])

            def attend(qt):
                nk, nkt, nkp = geom(qt)
                ns = negsel[qt % 2]; nsk = ('negsel', qt % 2)
                W4 = 4 * nq
                ob = 6
                qflat = QT[0:64, qt, :, :].rearrange("p h q -> p (h q)")
                tl = [dict(j=j) for j in range(nkt)]

                def sA(t):
                    j = t['j']
                    zb = rr([4, 5], 'zbd')
                    t['zb'] = zb
                    mm(ps[:, zb, 0:W4], KT[0:64, 0, j * 128:(j + 1) * 128], qflat, True, False, r=['KT', 'QT'], w=[('ps', zb)], acc=True)
                    mm(ps[:, zb, 0:W4], ns[0:nq, j * 128:(j + 1) * 128], I4[0:nq, 0:W4], False, True, r=[nsk, 'I4'], w=[('ps', zb)], acc=True)
                    pt = rr(Pt, 'Ptd'); pk = ('Ptd', id(pt))
                    t['pt'] = (pt, pk)
                    P.op('act', lambda: nc.scalar.activation(out=pt[:, 0:W4], in_=ps[:, zb, 0:W4], func=AF.Exp), r=[('ps', zb)], w=[pk])

                def sC(t):
                    j = t['j']
                    pt, pk = t['pt']
                    mm(ps[0:65, ob, 0:W4], Vd[:, j, 0:65], pt[:, 0:W4], j == 0, j == nkt - 1, r=['V', pk], w=[('ps', ob)], acc=True)
                for s_ in range(nkt + 1):
                    if s_ < nkt:
                        sA(tl[s_])
                    if s_ >= 1:
                        sC(tl[s_ - 1])
                    yield
                copy_op('act', srow[64:65, 0:W4], ps[64:65, ob, 0:W4], r=[('ps', ob)], w=['srow'])
                P.op('pe', lambda: nc.tensor.matmul(ps[0:64, 7, 0:W4], lhsT=mats[64:65, 4, 0:64], rhs=srow[64:65, 0:W4], start=True, stop=True),
                     r=['srow', 'mats'], w=[('ps', 7)])
                P.op('dve', lambda: nc.vector.reciprocal(out=rbc[0:64, 0:W4], in_=ps[0:64, 7, 0:W4]), r=[('ps', 7)], w=['rbc'])
                qi = qt // tpb
                yb_ = ybufs[qi % 2]; yk = ('ybuf', qi % 2)
                yb = yb_[0:64, :, (qt % tpb) * nq:(qt % tpb + 1) * nq]
                P.op('dve', lambda: nc.vector.tensor_tensor(
                    out=yb, in0=ps[0:64, ob, 0:W4].rearrange("p (h q) -> p h q", h=4), in1=rbc[0:64, 0:W4].rearrange("p (h q) -> p h q", h=4), op=ALU.mult),
                    r=[('ps', ob), 'rbc'], w=[yk])
                if qt % tpb == tpb - 1:
                    q0 = qi * QB
                    P.dma('sp', lambda: nc.sync.dma_start(out=ysc[3, :, :, q0:q0 + QB], in_=yb_[0:64, :, :]), r=[yk], w=[('ysc', 3, qi)])
                    if debug:
                        P.dma('sp', lambda: nc.sync.dma_start(out=ydbg[0 if g == 'p' else 1, l, 3, :, :, q0:q0 + QB], in_=yb_[0:64, :, :]), r=[yk], w=[('ydbg', 3, qi, l, g)])

            def drive(*gens):
                gens = [g_ for g_ in gens if g_ is not None]
                while gens:
                    for g_ in list(gens):
                        try:
                            next(g_)
                        except StopIteration:
                            gens.remove(g_)

            drive(prep(0))
            for qt in range(NQT):
                drive(prep(qt + 1) if qt + 1 < NQT else None, attend(qt))

        def ln_fm(rbuf, xcb, W, l, which, out32, out16, key, lnsd):
            for kc in range(KC):
                mm(ps[:, 4, 0:W], ones_f, rbuf[:, kc, 0:W], kc == 0, kc == KC - 1, r=[key + 'r', 'mats'], w=[('ps', 4)], acc=True)
            P.op('dve', lambda: nc.vector.scalar_tensor_tensor(out=xcb[:, :, 0:W], in0=ps[:, 4, 0:W].unsqueeze(1).to_broadcast([128, KC, W]), scalar=-1.0 / DM,
                                                               in1=rbuf[:, :, 0:W], op0=ALU.mult, op1=ALU.add), r=[('ps', 4), key + 'r'], w=[key + 'xc'])
            P.op('act', lambda: nc.scalar.activation(out=rbuf[:, :, 0:W], in_=xcb[:, :, 0:W], func=AF.Square), r=[key + 'xc'], w=[key + 'r'])
            for kc in range(KC):
                mm(ps[:, 4, 0:W], ones_f, rbuf[:, kc, 0:W], kc == 0, kc == KC - 1, r=[key + 'r', 'mats'], w=[('ps', 4)], acc=True)
            P.op('act', lambda: nc.scalar.activation(out=lnsd[:, 0:W], in_=ps[:, 4, 0:W], func=AF.Sqrt, scale=1.0 / DM, bias=eps_t[:, 0:1]),
                 r=[('ps', 4), 'eps'], w=['lnsd'])
            P.op('dve', lambda: nc.vector.reciprocal(out=lnsd[:, 0:W], in_=lnsd[:, 0:W]), r=['lnsd'], w=['lnsd'])
            P.op('dve', lambda: nc.vector.tensor_tensor(out=xcb[:, :, 0:W], in0=xcb[:, :, 0:W], in1=lnsd[:, 0:W].unsqueeze(1).to_broadcast([128, KC, W]), op=ALU.mult),
                 r=[key + 'xc', 'lnsd'], w=[key + 'xc'])
            for kc in range(KC):
                e = rr(['pool', 'dve'], 'lnaff')
                engo = nc.gpsimd if e == 'pool' else nc.vector
                P.op(e, lambda kc=kc, engo=engo: engo.tensor_scalar(out=out32[:, kc, 0:W], in0=xcb[:, kc, 0:W], scalar1=lnp[:, l, which, kc:kc + 1],
                                                                     scalar2=lnp[:, l, which + 1, kc:kc + 1], op0=ALU.mult, op1=ALU.add),
                     r=[key + 'xc', 'lnp'], w=[key + 'o32'])
            if out16 is not None:
                copy_op('act', out16, out32[:, :, 0:W], r=[key + 'o32'], w=[key + 'o16'])

        lnsd = None
        eps_t = None

        def ln_fm2(rb, rkey, sq16, sm, W, l, which, out32, okey):
            ones_b = matb[:, 3, :]
            P.op('act', lambda: nc.scalar.activation(out=sq16[:, :, 0:W], in_=rb[:, :, 0:W], func=AF.Square), r=[rkey], w=['sq16'])
            for kc in range(KC):
                mm(ps[:, 6, 0:W], ones_f, rb[:, kc, 0:W], kc == 0, kc == KC - 1, r=[rkey, 'mats'], w=[('ps', 6)], acc=True)
            for kc in range(KC):
                mm(ps[:, 7, 0:W], ones_b, sq16[:, kc, 0:W], kc == 0, kc == KC - 1, r=['sq16', 'matb'], w=[('ps', 7)], acc=True)
            copy_op('act', sm[:, 0, 0:W], ps[:, 6, 0:W], r=[('ps', 6)], w=['sm0'], scale=1.0 / DM)
            P.op('dve', lambda: nc.vector.tensor_tensor(out=sm[:, 1, 0:W], in0=sm[:, 0, 0:W], in1=sm[:, 0, 0:W], op=ALU.mult), r=['sm0'], w=['sm1'])
            P.op('dve', lambda: nc.vector.scalar_tensor_tensor(out=sm[:, 1, 0:W], in0=ps[:, 7, 0:W], scalar=1.0 / DM, in1=sm[:, 1, 0:W],
                                                               op0=ALU.mult, op1=ALU.subtract), r=[('ps', 7), 'sm1'], w=['sm1'])
            P.op('act', lambda: nc.scalar.activation(out=sm[:, 2, 0:W], in_=sm[:, 1, 0:W], func=AF.Ln, bias=eps_t[:, 0:1]), r=['sm1', 'eps'], w=['sm2'])
            P.op('act', lambda: nc.scalar.activation(out=sm[:, 2, 0:W], in_=sm[:, 2, 0:W], func=AF.Exp, scale=-0.5), r=['sm2'], w=['sm2'])
            P.op('dve', lambda: nc.vector.tensor_tensor(out=rb[:, :, 0:W], in0=rb[:, :, 0:W], in1=sm[:, 0, 0:W].unsqueeze(1).to_broadcast([128, KC, W]), op=ALU.subtract),
                 r=[rkey, 'sm0'], w=[rkey])
            P.op('dve', lambda: nc.vector.tensor_tensor(out=rb[:, :, 0:W], in0=rb[:, :, 0:W], in1=sm[:, 2, 0:W].unsqueeze(1).to_broadcast([128, KC, W]), op=ALU.mult),
                 r=[rkey, 'sm2'], w=[rkey])
            for kc in range(KC):
                e = rr(['pool', 'dve'], 'lnaff')
                engo = nc.gpsimd if e == 'pool' else nc.vector
                P.op(e, lambda kc=kc, engo=engo: engo.tensor_scalar(out=out32[:, kc, 0:W], in0=rb[:, kc, 0:W], scalar1=lnp[:, l, which, kc:kc + 1],
                                                                     scalar2=lnp[:, l, which + 1, kc:kc + 1], op0=ALU.mult, op1=ALU.add),
                     r=[rkey, 'lnp'], w=[(okey, 'aff', kc)])

        def phase3(g, b, l, T, NTn, O):
            W3 = min(512, T)
            NB3 = T // W3
            W = min(256, T)
            NB = T // W
            hkeys = lambda t0, w: [('hT', jj) for jj in range(t0 // 128, max(t0 // 128 + 1, (t0 + w) // 128))]
            ar16.reset(); ar32.reset()
            Ys = [ar16.get(16, W3) for _ in range(2)]
            WGs = [[ar16.get(KC, 128) for _ in range(4)] for _ in range(2)]
            WBs = [ar16.get(16, 128) for _ in range(2)]
            mouts = [ar16.get(W3) for _ in range(3)]
            sgb = [ar32.get(W3) for _ in range(2)]
            tmpb = [ar32.get(W3) for _ in range(2)]
            maccs = [ar32.get(W3) for _ in range(2)]
            for cc in range(KC):
                wb_ = WBs[cc % 2]; wbk = ('WB', cc % 2)
                P.dma('sp', lambda wb_=wb_, cc=cc: nc.sync.dma_start(out=wb_[0:64, :, :], in_=wb_b[l, cc]), r=[('wb', l, cc)], w=[wbk])
                wgs = WGs[cc % 2]
                for br in range(4):
                    P.dma('sp', lambda wg_=wgs[br], ch=CH_GATE + br * 8 + cc: nc.sync.dma_start(out=wg_[:, :, :], in_=wf_b[l, ch]),
                          r=[('wf', l, CH_GATE + br * 8 + cc)], w=[('WG', cc % 2, br)])
                for bi in range(NB3):
                    t0 = bi * W3
                    blk = slice(t0, t0 + W3)
                    hk = hkeys(t0, W3)
                    Y = rr(Ys, 'Ys'); yk_ = ('Y', id(Y))
                    for br in range(4):
                        P.dma('sp', lambda Y=Y, br=br, blk=blk: nc.sync.dma_start(out=Y[0:64, br * 4:(br + 1) * 4, :], in_=ysc[br, :, :, blk]),
                              r=[('ysc', br, t0 // min(512, T))], w=[yk_ + (br,)])
                    macc = rr(maccs, 'maccs'); mk = ('macc', id(macc))
                    for br in range(4):
                        bB = rr([0, 1], 'bB'); bG = rr([2, 3, 4, 5], 'bG')
                        for h in range(4):
                            mm(ps[:, bB, 0:W3], wb_[0:64, br * 4 + h, :], Y[0:64, br * 4 + h, :], h == 0, h == 3, r=[wbk, yk_ + (br,)], w=[('ps', bB)], acc=True)
                        for kc in range(KC):
                            mm(ps[:, bG, 0:W3], wgs[br][:, kc, :], hT[:, kc, blk], kc == 0, kc == KC - 1, r=[('WG', cc % 2, br)] + hk, w=[('ps', bG)], acc=True)
                        sg_ = rr(sgb, 'sgb'); sgk = ('sgb', id(sg_))
                        P.op('act', lambda sg_=sg_, bG=bG: nc.scalar.activation(out=sg_[:, :], in_=ps[:, bG, 0:W3], func=AF.Sigmoid), r=[('ps', bG)], w=[sgk])
                        if br == 0:
                            P.op('dve', lambda sg_=sg_, bB=bB, macc=macc: nc.vector.tensor_tensor(out=macc[:, :], in0=ps[:, bB, 0:W3], in1=sg_[:, :], op=ALU.mult),
                                 r=[('ps', bB), sgk], w=[mk])
                        else:
                            tb_ = rr(tmpb, 'tmpb'); tk = ('tmpb', id(tb_))
                            P.op('dve', lambda sg_=sg_, bB=bB, tb_=tb_: nc.vector.tensor_tensor(out=tb_[:, :], in0=ps[:, bB, 0:W3], in1=sg_[:, :], op=ALU.mult),
                                 r=[('ps', bB), sgk], w=[tk])
                            if br < 3:
                                P.op('pool', lambda macc=macc, tb_=tb_: nc.gpsimd.tensor_tensor(out=macc[:, :], in0=macc[:, :], in1=tb_[:, :], op=ALU.add), r=[tk, mk], w=[mk])
                            else:
                                mo = rr(mouts, 'mouts'); mok = ('mout', id(mo))
                                P.op('pool', lambda macc=macc, tb_=tb_, mo=mo: nc.gpsimd.tensor_tensor(out=mo[:, :], in0=macc[:, :], in1=tb_[:, :], op=ALU.add),
                                     r=[tk, mk], w=[mok])
                                P.dma('pool', lambda mo=mo, cc=cc, blk=blk: nc.gpsimd.dma_start(out=msc[cc * 128:(cc + 1) * 128, blk], in_=mo[:, :]), r=[mok], w=[('msc', cc, bi)])
            P.barrier()
            ar16.reset(); ar32.reset()
            WOa = ar16.get(KC * KC, 128)
            for oc in range(KC):
                P.dma('sp', lambda oc=oc: nc.sync.dma_start(out=WOa[:, oc * KC:(oc + 1) * KC, :], in_=wf_b[l, CH_OUT + oc]), r=[('wf', l, CH_OUT + oc)], w=[('WO', oc)])
            mblk = [ar16.get(KC, W) for _ in range(2)]
            sq16 = ar16.get(KC, W)
            rbufs = [ar32.get(KC, W) for _ in range(3)]
            lnsm = ar32.get(4, W)
            def s1b(bi):
                t0 = bi * W
                blk = slice(t0, t0 + W)
                hk = hkeys(t0, W)
                mb = rr(mblk, 'mblk'); mbk = ('mblk', id(mb))
                P.dma('sp', lambda: nc.sync.dma_start(out=mb[:, :, :], in_=msc[:, blk].rearrange("(c p) t -> p c t", p=128)), w=[mbk])
                rbuf = rbufs[bi % 3]; rkey = 'L1r%d' % (bi % 3)
                for oc in range(KC):
                    bank = rr([0, 1, 2, 3, 4, 5], 'bO6')
                    for kc in range(KC):
                        mm(ps[:, bank, 0:W], WOa[:, oc * KC + kc, :], mb[:, kc, :], kc == 0, kc == KC - 1, r=[('WO', oc), mbk], w=[('ps', bank)], acc=True)
                    P.op('dve', lambda oc=oc, bank=bank: nc.vector.scalar_tensor_tensor(out=rbuf[:, oc, :], in0=hT[:, oc, blk], scalar=float(ALPHA), in1=ps[:, bank, 0:W],
                                                                                        op0=ALU.mult, op1=ALU.add), r=[('ps', bank)] + hk, w=[rkey])

            def s2b(bi):
                t0 = bi * W
                blk = slice(t0, t0 + W)
                hk = hkeys(t0, W)
                rbuf = rbufs[bi % 3]; rkey = 'L1r%d' % (bi % 3)
                ln_fm2(rbuf, rkey, sq16, lnsm, W, l, 0, rbuf, rkey)
                copy_op('act', hT[:, :, blk], rbuf[:, :, :], r=[rkey] + [(rkey, 'aff', kc_) for kc_ in range(KC)], w=hk)
                P.dma('pool', lambda: nc.gpsimd.dma_start(out=h1sc[:, blk].rearrange("(c p) t -> p c t", p=128), in_=rbuf[:, :, :]), r=[rkey] + [(rkey, 'aff', kc_) for kc_ in range(KC)], w=[('h1sc', bi)])
            s1b(0)
            for bi in range(NB):
                if bi + 1 < NB:
                    s1b(bi + 1)
                s2b(bi)
            P.barrier()
            ar16.reset(); ar32.reset()
            WA = [ar16.get(KC, 128) for _ in range(2)]
            WU = [ar16.get(KC, 128) for _ in range(2)]
            gout = [ar16.get(W3) for _ in range(3)]
            sgb = [ar32.get(W3) for _ in range(3)]
            for fc in range(NFC):
                wa_ = rr(WA, 'WA'); wak = ('WA', id(wa_))
                wu_ = rr(WU, 'WU'); wuk = ('WU', id(wu_))
                P.dma('sp', lambda wa_=wa_, fc=fc: nc.sync.dma_start(out=wa_[:, :, :], in_=wf_b[l, CH_FA + fc]), r=[('wf', l, CH_FA + fc)], w=[wak])
                P.dma('sp', lambda wu_=wu_, fc=fc: nc.sync.dma_start(out=wu_[:, :, :], in_=wf_b[l, CH_FU + fc]), r=[('wf', l, CH_FU + fc)], w=[wuk])
                for bi in range(NB3):
                    t0 = bi * W3
                    blk = slice(t0, t0 + W3)
                    hk = hkeys(t0, W3)
                    bA = rr([0, 1, 2], 'bA'); bU = rr([3, 4, 5], 'bU')
                    for kc in range(KC):
                        mm(ps[:, bA, 0:W3], wa_[:, kc, :], hT[:, kc, blk], kc == 0, kc == KC - 1, r=[wak] + hk, w=[('ps', bA)], acc=True)
                    for kc in range(KC):
                        mm(ps[:, bU, 0:W3], wu_[:, kc, :], hT[:, kc, blk], kc == 0, kc == KC - 1, r=[wuk] + hk, w=[('ps', bU)], acc=True)
                    sg_ = rr(sgb, 'sgb3'); sgk = ('sgb3', id(sg_))
                    P.op('act', lambda sg_=sg_, bA=bA: nc.scalar.activation(out=sg_[:, :], in_=ps[:, bA, 0:W3], func=AF.Silu), r=[('ps', bA)], w=[sgk])
                    go = rr(gout, 'gout'); gok = ('gout', id(go))
                    P.op('dve', lambda sg_=sg_, bU=bU, go=go: nc.vector.tensor_tensor(out=go[:, :], in0=ps[:, bU, 0:W3], in1=sg_[:, :], op=ALU.mult),
                         r=[('ps', bU), sgk], w=[gok])
                    P.dma('pool', lambda go=go, fc=fc, blk=blk: nc.gpsimd.dma_start(out=gsc[fc * 128:(fc + 1) * 128, blk], in_=go[:, :]), r=[gok], w=[('gsc', fc, bi)])
            P.barrier()
            ar16.reset(); ar32.reset()
            W2a = ar16.get(KC * NFC, 128)
            for oc in range(KC):
                P.dma('sp', lambda oc=oc: nc.sync.dma_start(out=W2a[:, oc * NFC:(oc + 1) * NFC, :], in_=w2_b[l, oc]), r=[('w2', l, oc)], w=[('W2', oc)])
            gblk = [ar16.get(NFC, W) for _ in range(2)]
            sq16 = ar16.get(KC, W)
            rbufs = [ar32.get(KC, W) for _ in range(3)]
            lnsm = ar32.get(4, W)
            ystage = ar16.get(2048)[:, 0:2048].bitcast(F32)
            def s1d(bi):
                t0 = bi * W
                blk = slice(t0, t0 + W)
                gb = rr(gblk, 'gblk'); gbk = ('gblk', id(gb))
                P.dma('sp', lambda: nc.sync.dma_start(out=gb[:, :, :], in_=gsc[:, blk].rearrange("(c p) t -> p c t", p=128)), w=[gbk])
                rbuf = rbufs[bi % 3]; rkey = 'L2r%d' % (bi % 3)
                P.dma('sp', lambda: nc.sync.dma_start(out=rbuf[:, :, :], in_=h1sc[:, blk].rearrange("(c p) t -> p c t", p=128)), w=[rkey])
                for oc in range(KC):
                    bank = rr([0, 1, 4, 5], 'bO4')
                    for fc in range(NFC):
                        mm(ps[:, bank, 0:W], W2a[:, oc * NFC + fc, :], gb[:, fc, :], fc == 0, fc == NFC - 1, r=[('W2', oc), gbk], w=[('ps', bank)], acc=True)
                    P.op('dve', lambda oc=oc, bank=bank: nc.vector.scalar_tensor_tensor(out=rbuf[:, oc, :], in0=rbuf[:, oc, :], scalar=float(ALPHA), in1=ps[:, bank, 0:W],
                                                                                        op0=ALU.mult, op1=ALU.add), r=[('ps', bank), rkey], w=[rkey])

            def s2d(bi):
                t0 = bi * W
                blk = slice(t0, t0 + W)
                hk = hkeys(t0, W)
                rbuf = rbufs[bi % 3]; rkey = 'L2r%d' % (bi % 3)
                ln_fm2(rbuf, rkey, sq16, lnsm, W, l, 2, rbuf, rkey)
                if l == 0:
                    copy_op('act', hT[:, :, blk], rbuf[:, :, :], r=[rkey] + [(rkey, 'aff', kc_) for kc_ in range(KC)], w=hk)
                else:
                    for tt in range((W + 127) // 128):
                        rows = min(128, W - tt * 128)
                        for half in range(2):
                            bank = rr([2, 3], 'tb3')
                            for q in range(4):
                                kc = half * 4 + q
                                P.op('pe', lambda kc=kc, tt=tt, rows=rows, bank=bank, q=q: nc.tensor.transpose(
                                    out=ps[0:rows, bank, q * 128:(q + 1) * 128], in_=rbuf[:, kc, tt * 128:tt * 128 + rows], identity=ident_f),
                                    r=[rkey, (rkey, 'aff', kc), 'mats'], w=[('ps', bank)], acc=True)
                            copy_op(evac_eng(), ystage[0:rows, half * 512:(half + 1) * 512], ps[0:rows, bank, 0:512], r=[('ps', bank)], w=['ystage'])
                        P.dma('pool', lambda tt=tt, rows=rows: nc.gpsimd.dma_start(out=O['y'][b, t0 + tt * 128:t0 + tt * 128 + rows, :], in_=ystage[0:rows, :]),
                              r=['ystage'], w=[('o_y', bi, tt)])
            s1d(0)
            for bi in range(NB):
                if bi + 1 < NB:
                    s1d(bi + 1)
                s2d(bi)

        eps_t = es.enter_context(nc.sbuf_tensor("eps_t", [128, 1], F32))
        P.op('pool', lambda: nc.gpsimd.memset(eps_t[:], EPS), w=['eps'])
        P.op('pool', lambda: nc.gpsimd.memset(IW[:], 0.0), w=['IW'])
        P.op('pool', lambda: nc.gpsimd.memset(IWs[:], 1.0), w=['IWs'])
        P.op('pool', lambda: nc.gpsimd.memset(CS[:], 0.0), w=['CS'])
        P.barrier()
        for g, B in (('p', BP), ('s', BS)):
            for b in range(B):
                process_seq(g, b)
        P.barrier()
        print("ops:", P.nops, {e: len(v) for e, v in P.ops.items()})
        P.emit(nc, sems, block)
    return nc


FULL_CFG = dict(NC=8, BP=2, BS=2, TP=4096, TS=32, PAST=2048)
_CACHE = {}


def run_cfg(inp, cfg, debug=False):
    NCc, BP, BS, TP, TS, PAST = cfg['NC'], cfg['BP'], cfg['BS'], cfg['TP'], cfg['TS'], cfg['PAST']
    f = lambda a: np.ascontiguousarray(np.asarray(a), dtype=np.float32)
    wd = prep_weights(f(inp['w_in']), f(inp['w_branch']), f(inp['w_out']), f(inp['w_ffn_in']), f(inp['w_ffn_out']),
                      f(inp['ln1_g']), f(inp['ln1_b']), f(inp['ln2_g']), f(inp['ln2_b']), f(inp['b_forget']))
    masks, mats = const_tables()
    rfp, rtp = rot_tables(0, TP)
    rfs, rts = rot_tables(PAST, TS)
    key = (tuple(sorted(cfg.items())), debug)
    if key not in _CACHE:
        _CACHE[key] = build_program(cfg, debug)
    nc = _CACHE[key]
    xp, xs = f(inp['x_prompt']), f(inp['x_sample'])
    in_maps = []
    for c in range(NCc):
        sp = slice(c * BP, (c + 1) * BP)
        ss = slice(c * BS, (c + 1) * BS)
        m = dict(xp=xp[sp], xs=xs[ss],
                 c_sbk=f(inp['cache_sb_k'])[:, ss].reshape(2, BS, PAST, 256), c_sbv=f(inp['cache_sb_v'])[:, ss].reshape(2, BS, PAST, 256),
                 c_fk=f(inp['cache_fox_k'])[:, ss].reshape(2, BS, PAST, 256), c_fv=f(inp['cache_fox_v'])[:, ss].reshape(2, BS, PAST, 256),
                 c_flf=f(inp['cache_fox_logf'])[:, ss], c_dk=f(inp['cache_dsa_k'])[:, ss], c_dv=f(inp['cache_dsa_v'])[:, ss],
                 c_dki=f(inp['cache_dsa_kidx'])[:, ss], st_ret=f(inp['state_ret'])[:, ss],
                 wtm=wd['wtm'], wf=wd['wf'], w2=wd['w2'], wb=wd['wb'], lnp=wd['lnp'], bfor=wd['bfor'],
                 masks=masks, mats=mats, rotfm_p=rfp, rotfm_s=rfs, rottm_p=rtp, rottm_s=rts)
        in_maps.append({k: np.ascontiguousarray(v) for k, v in m.items()})
    res = run_bass_kernel_spmd(nc, in_maps, core_ids=list(range(NCc)))
    R = res.results

    def cat(name, axis):
        return np.concatenate([np.asarray(R[c][name], dtype=np.float32) for c in range(NCc)], axis=axis)
    out = []
    out.append(cat('y_p', 0))
    out.append(cat('y_s', 0))
    for g, T in (('p', TP), ('s', TS)):
        Bt = (BP if g == 'p' else BS) * NCc
        out.append(cat('sb_k_' + g, 1).reshape(2, Bt, T, 4, 64))
        out.append(cat('sb_v_' + g, 1).reshape(2, Bt, T, 4, 64))
        out.append(cat('ret_' + g, 1))
        out.append(cat('fox_k_' + g, 1).reshape(2, Bt, T, 4, 64))
        out.append(cat('fox_v_' + g, 1).reshape(2, Bt, T, 4, 64))
        out.append(cat('fox_lf_' + g, 1))
        out.append(cat('dsa_k_' + g, 1))
        out.append(cat('dsa_v_' + g, 1))
        out.append(cat('dsa_ki_' + g, 1))
    if debug:
        return tuple(out), [np.asarray(R[c]['ydbg']) for c in range(NCc)]
    return tuple(out)


def kernel(**inputs):
    return run_cfg(inputs, FULL_CFG)
```
